# Optimizing a Trainium2 kernel written in Bass

```python
import math
import jax, jax.numpy as jnp
from jax import lax
import numpy as np

D_MODEL = 1024
BATCH = 2
SEQ = 8192
DEPTH = 2
DEC_BATCH = 32
DEC_SEQ = 8
PAST_LEN = 16384
PAGE_SIZE = 128

D_PLE = 256
RET_HEADS = 4
RET_DK = 128
RET_DV = 256
RET_CHUNK = 128
ATT_GROUPS = ((128, 1), (512, 4), (2048, 16))
N_GROUPS = 3
HPG = 4
ATT_HEAD_DIM = 128
ATT_HEADS = N_GROUPS * HPG
Q_BLOCK = 128
D_FF = 4 * D_MODEL
EPS = 1e-6

RET_QK_W = RET_HEADS * RET_DK
RET_V_W = RET_HEADS * RET_DV
ATT_W = ATT_HEADS * ATT_HEAD_DIM
ATT_OUT_W = HPG * ATT_HEAD_DIM
D_IN = 2 * RET_QK_W + 2 * RET_V_W + 3 * ATT_W + 2 * D_MODEL

kernel_name = 'hybrid_retention_dilated_attn_decode_step'


def _split_points():
    sizes = (RET_QK_W, RET_QK_W, RET_V_W, RET_V_W, ATT_W, ATT_W, ATT_W, D_MODEL, D_MODEL)
    pts, acc = [], 0
    for s in sizes[:-1]:
        acc += s
        pts.append(acc)
    return pts


def _ret_log_gamma():
    return jnp.log1p(-jnp.exp(jnp.linspace(math.log(1.0 / 32), math.log(1.0 / 512), RET_HEADS))).astype(jnp.float32)


def _alibi_slopes():
    return (2.0 ** (-8.0 * (jnp.arange(ATT_HEADS, dtype=jnp.float32) + 1.0) / ATT_HEADS)).astype(jnp.float32)


def rmsnorm(x, g):
    xf = x.astype(jnp.float32)
    y = xf * lax.rsqrt(jnp.mean(xf * xf, axis=-1, keepdims=True) + EPS) * g.astype(jnp.float32)
    return y.astype(x.dtype)


def head_layernorm(o):
    of = o.astype(jnp.float32)
    mu = jnp.mean(of, axis=-1, keepdims=True)
    var = jnp.mean((of - mu) ** 2, axis=-1, keepdims=True)
    return ((of - mu) * lax.rsqrt(var + EPS)).astype(o.dtype)


def retention(q, k, v, state0):
    B, T, H, DK = q.shape
    DV = v.shape[-1]
    dt = v.dtype
    C = math.gcd(T, RET_CHUNK)
    NC = T // C
    lg = _ret_log_gamma()
    pos = jnp.arange(C, dtype=jnp.float32)
    diff = pos[:, None] - pos[None, :]
    intra = jnp.where(diff[None] >= 0, jnp.exp(lg[:, None, None] * jnp.maximum(diff, 0.0)[None]), 0.0)
    xi = jnp.exp(lg[:, None] * (pos[None] + 1.0)).T
    zeta = jnp.exp(lg[:, None] * (C - 1.0 - pos)[None]).T
    chunk_decay = jnp.exp(lg * C).astype(dt)
    qc = q.reshape(B, NC, C, H, DK)
    kc = (k * (DK ** -0.5)).reshape(B, NC, C, H, DK)
    vc = v.reshape(B, NC, C, H, DV)
    scores = jnp.einsum('bnihd,bnjhd->bnhij', qc, kc) * intra.astype(dt)[None, None]
    o_intra = jnp.einsum('bnhij,bnjhv->bnihv', scores, vc)
    u = jnp.einsum('bnjhd,bnjhv->bnhdv', kc, vc * zeta.astype(dt)[None, None, :, :, None])

    def step(R, u_c):
        return chunk_decay[None, :, None, None] * R + u_c, R

    R_final, R_prev = lax.scan(step, state0.astype(dt), jnp.moveaxis(u, 1, 0))
    R_prev = jnp.moveaxis(R_prev, 0, 1)
    o_cross = jnp.einsum('bnihd,bnhdv->bnihv', qc * xi.astype(dt)[None, None, :, :, None], R_prev)
    return (o_intra + o_cross).reshape(B, T, H, DV), R_final


def dilated_group_attention(q, k, v, q_start, window, dil, slopes):
    B, Tq, H, Dh = q.shape
    n_taps = window // dil + 1
    QB = math.gcd(Tq, Q_BLOCK)
    nb = Tq // QB
    taps = jnp.arange(n_taps, dtype=jnp.int32) * dil
    alibi = -slopes[:, None] * taps.astype(jnp.float32)[None]
    qb = jnp.moveaxis(q.reshape(B, nb, QB, H, Dh), 1, 0)
    starts = q_start + jnp.arange(nb, dtype=jnp.int32) * QB
    scale = Dh ** -0.5

    def block(args):
        qblk, s0 = args
        qi = s0 + jnp.arange(QB, dtype=jnp.int32)
        idx = qi[:, None] - taps[None, :]
        valid = idx >= 0
        idx = jnp.maximum(idx, 0)
        kg = jnp.take(k, idx, axis=1)
        vg = jnp.take(v, idx, axis=1)
        s = jnp.einsum('bqhd,bqnhd->bhqn', qblk, kg).astype(jnp.float32) * scale + alibi[None, :, None, :]
        s = jnp.where(valid[None, None], s, -jnp.inf)
        m = jnp.max(s, axis=-1, keepdims=True)
        e = jnp.exp(s - m)
        den = jnp.sum(e, axis=-1, keepdims=True)
        o = jnp.einsum('bhqn,bqnhd->bqhd', (e / den).astype(v.dtype), vg)
        lse = (m + jnp.log(den))[..., 0]
        return o, lse

    o, lse = lax.map(block, (qb, starts))
    o = jnp.moveaxis(o, 0, 1).reshape(B, Tq, H, Dh)
    lse = jnp.transpose(lse, (1, 0, 3, 2)).reshape(B, Tq, H)
    return o, lse


def layer(x, p_l, ret_state0, win_bufs, lw):
    (norm_mix, w_in, w_ret_br, w_att_br, w_out, norm_ffn, w_up, w_down, w_ple, w_ple_gate) = lw
    B, T, _ = x.shape
    dt = x.dtype
    h = rmsnorm(x, norm_mix)
    z = h @ w_in
    rq, rk, rv, rg, aq, ak, av, ga, gb = jnp.split(z, _split_points(), axis=-1)
    o_ret, ret_state = retention(rq.reshape(B, T, RET_HEADS, RET_DK), rk.reshape(B, T, RET_HEADS, RET_DK),
                                 rv.reshape(B, T, RET_HEADS, RET_DV), ret_state0)
    o_ret = head_layernorm(o_ret).reshape(B, T, RET_V_W) * jax.nn.silu(rg)
    br_ret = o_ret @ w_ret_br
    aq = aq.reshape(B, T, N_GROUPS, HPG, ATT_HEAD_DIM)
    ak = ak.reshape(B, T, N_GROUPS, HPG, ATT_HEAD_DIM)
    av = av.reshape(B, T, N_GROUPS, HPG, ATT_HEAD_DIM)
    slopes = _alibi_slopes()
    outs, lses, new_bufs = [], [], []
    for g, (W, dil) in enumerate(ATT_GROUPS):
        kg, vg = ak[:, :, g], av[:, :, g]
        if win_bufs is None:
            k_all, v_all, q_start = kg, vg, 0
        else:
            kb, vb = win_bufs[g]
            k_all = jnp.concatenate([kb.astype(dt), kg], axis=1)
            v_all = jnp.concatenate([vb.astype(dt), vg], axis=1)
            q_start = kb.shape[1]
        o_g, lse_g = dilated_group_attention(aq[:, :, g], k_all, v_all, q_start, W, dil,
                                             slopes[g * HPG:(g + 1) * HPG])
        outs.append(o_g)
        lses.append(lse_g)
        L = min(W, k_all.shape[1])
        new_bufs.append((k_all[:, -L:], v_all[:, -L:]))
    wts = jax.nn.softmax(jnp.stack(lses, axis=0), axis=0)
    o_att = jnp.einsum('gbth,gbthd->bthd', wts, jnp.stack(outs, axis=0).astype(jnp.float32)).astype(dt)
    br_att = o_att.reshape(B, T, ATT_OUT_W) @ w_att_br
    x = x + (jax.nn.sigmoid(ga) * br_ret + jax.nn.sigmoid(gb) * br_att) @ w_out
    u = jax.nn.relu(rmsnorm(x, norm_ffn) @ w_up)
    x = x + (u * u) @ w_down
    x = x + (p_l @ w_ple) * jax.nn.sigmoid(x @ w_ple_gate)
    return x, ret_state, new_bufs


def setup_inputs(seed: int = 0) -> dict:
    key = jax.random.key(seed)
    ks = jax.random.split(key, 24)
    f32 = jnp.float32
    nrm = lambda k, s, sc=1.0: jax.random.normal(k, s, f32) * sc
    d = {}
    d['x_prompt'] = nrm(ks[0], (BATCH, SEQ, D_MODEL))
    d['x_sample'] = nrm(ks[1], (DEC_BATCH, DEC_SEQ, D_MODEL))
    for g, (W, dil) in enumerate(ATT_GROUPS):
        L = min(W, PAST_LEN)
        d['cache_win_k%d' % g] = nrm(ks[2 + 2 * g], (DEPTH, DEC_BATCH, L, HPG, ATT_HEAD_DIM))
        d['cache_win_v%d' % g] = nrm(ks[3 + 2 * g], (DEPTH, DEC_BATCH, L, HPG, ATT_HEAD_DIM))
    d['state_ret'] = nrm(ks[8], (DEPTH, DEC_BATCH, RET_HEADS, RET_DK, RET_DV))
    d['p_prompt'] = nrm(ks[9], (DEPTH, BATCH, SEQ, D_PLE))
    d['p_sample'] = nrm(ks[10], (DEPTH, DEC_BATCH, DEC_SEQ, D_PLE))
    d['norm_mix'] = 1.0 + nrm(ks[11], (DEPTH, D_MODEL), 0.02)
    d['w_in'] = nrm(ks[12], (DEPTH, D_MODEL, D_IN), D_MODEL ** -0.5)
    d['w_ret_br'] = nrm(ks[13], (DEPTH, RET_V_W, D_MODEL), RET_V_W ** -0.5)
    d['w_att_br'] = nrm(ks[14], (DEPTH, ATT_OUT_W, D_MODEL), ATT_OUT_W ** -0.5)
    d['w_out'] = nrm(ks[15], (DEPTH, D_MODEL, D_MODEL), D_MODEL ** -0.5)
    d['norm_ffn'] = 1.0 + nrm(ks[16], (DEPTH, D_MODEL), 0.02)
    d['w_up'] = nrm(ks[17], (DEPTH, D_MODEL, D_FF), D_MODEL ** -0.5)
    d['w_down'] = nrm(ks[18], (DEPTH, D_FF, D_MODEL), D_FF ** -0.5)
    d['w_ple'] = nrm(ks[19], (DEPTH, D_PLE, D_MODEL), D_PLE ** -0.5)
    d['w_ple_gate'] = nrm(ks[20], (DEPTH, D_MODEL, D_MODEL), D_MODEL ** -0.5)
    d['norm_final'] = 1.0 + nrm(ks[21], (D_MODEL,), 0.02)
    return d


def reference(x_prompt, x_sample, cache_win_k0, cache_win_v0, cache_win_k1, cache_win_v1, cache_win_k2, cache_win_v2,
              state_ret, p_prompt, p_sample, norm_mix, w_in, w_ret_br, w_att_br, w_out, norm_ffn, w_up, w_down,
              w_ple, w_ple_gate, norm_final):
    xp, xs = x_prompt, x_sample
    cache_k = (cache_win_k0, cache_win_k1, cache_win_k2)
    cache_v = (cache_win_v0, cache_win_v1, cache_win_v2)
    pk = [[] for _ in range(N_GROUPS)]
    pv = [[] for _ in range(N_GROUPS)]
    sk = [[] for _ in range(N_GROUPS)]
    sv = [[] for _ in range(N_GROUPS)]
    pr, sr = [], []
    for i in range(DEPTH):
        lw = (norm_mix[i], w_in[i], w_ret_br[i], w_att_br[i], w_out[i], norm_ffn[i], w_up[i], w_down[i],
              w_ple[i], w_ple_gate[i])
        zero_state = jnp.zeros((xp.shape[0], RET_HEADS, RET_DK, RET_DV), xp.dtype)
        xp, rp, bp = layer(xp, p_prompt[i], zero_state, None, lw)
        bufs = [(cache_k[g][i], cache_v[g][i]) for g in range(N_GROUPS)]
        xs, rs, bs = layer(xs, p_sample[i], state_ret[i], bufs, lw)
        pr.append(rp)
        sr.append(rs)
        for g in range(N_GROUPS):
            pk[g].append(bp[g][0]); pv[g].append(bp[g][1])
            sk[g].append(bs[g][0]); sv[g].append(bs[g][1])
    y_prompt = rmsnorm(xp, norm_final)
    y_sample = rmsnorm(xs, norm_final)
    prompt_win_k0, prompt_win_v0 = jnp.stack(pk[0]), jnp.stack(pv[0])
    prompt_win_k1, prompt_win_v1 = jnp.stack(pk[1]), jnp.stack(pv[1])
    prompt_win_k2, prompt_win_v2 = jnp.stack(pk[2]), jnp.stack(pv[2])
    sample_win_k0, sample_win_v0 = jnp.stack(sk[0]), jnp.stack(sv[0])
    sample_win_k1, sample_win_v1 = jnp.stack(sk[1]), jnp.stack(sv[1])
    sample_win_k2, sample_win_v2 = jnp.stack(sk[2]), jnp.stack(sv[2])
    prompt_ret = jnp.stack(pr)
    sample_ret = jnp.stack(sr)
    return (y_prompt, y_sample, prompt_win_k0, prompt_win_v0, prompt_win_k1, prompt_win_v1, prompt_win_k2, prompt_win_v2,
            prompt_ret, sample_win_k0, sample_win_v0, sample_win_k1, sample_win_v1, sample_win_k2, sample_win_v2,
            sample_ret)
```

```python
import math
import numpy as np
import concourse.bass as bass
import concourse.mybir as mybir
from concourse.bass_utils import run_bass_kernel_spmd

F32 = mybir.dt.float32
BF16 = mybir.dt.bfloat16
AF = mybir.ActivationFunctionType
ALU = mybir.AluOpType

D = 1024
DEPTH = 2
DPLE = 256
DFF = 4096
DIN = 9728
EPS = 1e-6
GROUPS = ((128, 1), (512, 4), (2048, 16))
SB_W = 2048
NS = 4
ST = 8
NDMA = 16
NSDMA = 4
import os
STOP_AFTER = int(os.environ.get("KSTOP", "0"))
KP = int(os.environ.get("KP", "0"))
KSKIP = int(os.environ.get("KSKIP", "0"))
DMASPREAD = int(os.environ.get("DMASPREAD", "0"))
PRUNE = int(os.environ.get("KPRUNE", "1"))
ENGS = ("pe", "act", "dve", "pool", "sp")

C_RQ, C_RK, C_RV, C_RG, C_AQ, C_AK, C_AV, C_GA, C_GB = 0, 512, 1024, 2048, 3072, 4608, 6144, 7680, 8704


def _lg():
    return np.log1p(-np.exp(np.linspace(math.log(1.0 / 32), math.log(1.0 / 512), 4)))


def _slopes():
    return 2.0 ** (-8.0 * (np.arange(12, dtype=np.float64) + 1.0) / 12)


def make_consts():
    lg = _lg()
    sl = _slopes()
    p = np.arange(128)
    c = {}
    c["c_ident"] = np.eye(128, dtype=np.float32)
    diff = p[None, :] - p[:, None]
    rm = np.zeros((128, 4, 128), np.float64)
    for h in range(4):
        rm[:, h, :] = np.where(diff >= 0, np.exp(lg[h] * np.maximum(diff, 0)), 0.0) * (128 ** -0.5)
    c["c_rm"] = rm.astype(np.float32)
    xi = np.zeros((128, 4, 128), np.float64)
    for h in range(4):
        xi[:, h, :] = np.exp(lg[h] * (p + 1.0))[None, :]
    c["c_xi"] = xi.astype(np.float32)
    z = np.zeros((128, 8), np.float64)
    for h in range(4):
        z[:, h] = np.exp(lg[h] * (127.0 - p)) * (128 ** -0.5)
        z[:, 4 + h] = np.exp(lg[h] * (7.0 - (p % 8))) * (128 ** -0.5)
    c["c_z"] = z.astype(np.float32)
    am = np.zeros((128, 12, 256), np.float64)
    i = np.arange(128)
    for g, (Wg, dil) in enumerate(GROUPS):
        for j in range(4):
            s = sl[g * 4 + j] * dil
            dprev = i[None, :] + 128 - p[:, None]
            ddiag = i[None, :] - p[:, None]
            am[:, g * 4 + j, 0:128] = np.where(dprev <= 128, np.exp(-s * dprev), 0.0)
            am[:, g * 4 + j, 128:256] = np.where(ddiag >= 0, np.exp(-s * np.maximum(ddiag, 0)), 0.0)
    c["c_am"] = am.astype(np.float32)
    sm = np.zeros((128, 24, 32), np.float64)
    t = 0
    for g, (Wg, dil) in enumerate(GROUPS):
        L = Wg
        ntile = L // 128
        for tt in range(ntile + 1):
            for j in range(4):
                s = sl[g * 4 + j]
                for qi in range(8):
                    if tt < ntile:
                        kpos = tt * 128 + p
                        ok = np.ones(128, bool)
                    else:
                        kpos = L + p
                        ok = p < 8
                    dist = (L + qi) - kpos
                    valid = ok & (dist >= 0) & (dist % dil == 0) & (dist // dil <= 128)
                    sm[:, t, j * 8 + qi] = np.where(valid, np.exp(-s * np.maximum(dist, 0)), 0.0)
            t += 1
    c["c_sm"] = sm.astype(np.float32)
    return c


class Ctx:
    def __init__(self, nc):
        self.nc = nc
        self.q = {e: [] for e in ENGS}
        self.semcount = {}
        self.seen = {e: {} for e in ENGS}
        self.lastw = {}
        self.readers = {}
        self.dma_rr = 0
        self.sdma_rr = 0
        self.semkeys = [("eng", e) for e in ("pe", "act", "dve", "pool")] + [("dma", i) for i in range(NDMA)] + [("sdma", i) for i in range(NSDMA)] + [("bg", 0)]
        for s in self.semkeys:
            self.semcount[s] = 0

    def op(self, eng, fn, reads=(), writes=(), dma=0, bg=False):
        deps = {}

        def add(tok):
            if tok is None:
                return
            s, v = tok
            if deps.get(s, 0) < v:
                deps[s] = v
        for k in reads:
            add(self.lastw.get(k))
        for k in writes:
            add(self.lastw.get(k))
            for t in self.readers.get(k, ()):
                add(t)
        if bg:
            s = ("bg", 0)
            self.semcount[s] += 16 * dma
            tok = (s, self.semcount[s])
        elif dma:
            if eng == "pool":
                d = self.sdma_rr
                self.sdma_rr = (d + 1) % NSDMA
                s = ("sdma", d)
            else:
                d = self.dma_rr
                self.dma_rr = (d + 1) % NDMA
                s = ("dma", d)
            if self.semcount[s] > 0:
                add((s, self.semcount[s]))
            self.semcount[s] += 16 * dma
            tok = (s, self.semcount[s])
        else:
            s = ("eng", eng)
            self.semcount[s] += 1
            tok = (s, self.semcount[s])
        waits = []
        for sk, v in deps.items():
            if sk == ("eng", "pe") and eng == "pe" and not dma:
                continue
            if self.seen[eng].get(sk, 0) >= v:
                continue
            self.seen[eng][sk] = v
            waits.append((sk, v))
        self.q[eng].append((waits, fn, tok[0], dma))
        for k in reads:
            self.readers.setdefault(k, []).append(tok)
        for k in writes:
            self.lastw[k] = tok
            self.readers[k] = []
        return tok

    def barrier(self, final=False):
        for e in ENGS:
            waits = []
            for sk in self.semkeys:
                if sk == ("bg", 0) and not final:
                    continue
                v = self.semcount[sk]
                if v > 0 and self.seen[e].get(sk, 0) < v:
                    self.seen[e][sk] = v
                    waits.append((sk, v))
            self.q[e].append((waits, None, None, 0))
        self.lastw = {}
        self.readers = {}

    def replay(self, eng, e, sems):
        for waits, fn, s, dma in self.q[eng]:
            for (sk, v) in waits:
                e.wait_ge(sems[sk], v)
            if fn is None:
                continue
            r = fn(e)
            if dma:
                assert len(r) == dma, (len(r), dma)
                for ins in r:
                    ins.then_inc(sems[s], 16)
            else:
                ins = r[-1] if isinstance(r, (list, tuple)) else r
                ins.then_inc(sems[s], 1)


def build(W):
    NSB = W // SB_W
    NT = W // 128
    TT = NT + 1
    NTOK = TT * 128
    LG = [min(g[0], W) for g in GROUPS]
    lg = _lg()
    dec128 = [float(np.exp(lg[h] * 128)) for h in range(4)]
    dec8 = [float(np.exp(lg[h] * 8)) for h in range(4)]

    nc = bass.Bass("TRN2", target_bir_lowering=False)

    def din(name, shape, dt=F32):
        return nc.dram_tensor(name, list(shape), dt, kind="ExternalInput").ap()

    def dout(name, shape):
        return nc.dram_tensor(name, list(shape), F32, kind="ExternalOutput").ap()

    def dscr(name, shape, dt=BF16):
        return nc.dram_tensor(name, list(shape), dt, kind="Internal").ap()

    xw = din("xw", [W, D]); xs = din("xs", [NS * ST, D])
    pw = din("pw", [DEPTH, W, DPLE]); ps_in = din("ps", [DEPTH, NS * ST, DPLE])
    ck = [din("ck%d" % g, [DEPTH, NS, GROUPS[g][0], 512]) for g in range(3)]
    cv = [din("cv%d" % g, [DEPTH, NS, GROUPS[g][0], 512]) for g in range(3)]
    st_in = din("st", [DEPTH, NS, 4, 128, 256])
    w_in = din("w_in", [DEPTH, D, DIN]); w_rbr = din("w_ret_br", [DEPTH, 1024, D]); w_abr = din("w_att_br", [DEPTH, 512, D])
    w_out = din("w_out", [DEPTH, D, D]); w_up = din("w_up", [DEPTH, D, DFF]); w_dn = din("w_down", [DEPTH, DFF, D])
    w_ple = din("w_ple", [DEPTH, DPLE, D]); w_pg = din("w_ple_gate", [DEPTH, D, D])
    nmix = din("nmix", [DEPTH, 128, D]); nffn = din("nffn", [DEPTH, 128, D]); nfin = din("nfin", [128, D])
    c_ident = din("c_ident", [128, 128]); c_rm = din("c_rm", [128, 4, 128]); c_xi = din("c_xi", [128, 4, 128])
    c_z = din("c_z", [128, 8]); c_am = din("c_am", [128, 12, 256]); c_sm = din("c_sm", [128, 24, 32])
    vones_in = din("vones", [128, NSB, 128])

    y = dout("y", [SB_W, D]); ys = dout("ys", [NS * ST, D])
    pk = [dout("pk%d" % g, [DEPTH, LG[g], 512]) for g in range(3)]
    pv = [dout("pv%d" % g, [DEPTH, LG[g], 512]) for g in range(3)]
    pret = dout("pret", [DEPTH, 4, 128, 256])
    sk = [dout("sk%d" % g, [DEPTH, NS, GROUPS[g][0], 512]) for g in range(3)]
    sv = [dout("sv%d" % g, [DEPTH, NS, GROUPS[g][0], 512]) for g in range(3)]
    sret = dout("sret", [DEPTH, NS, 4, 128, 256])

    wb_in = dscr("wb_in", [DEPTH, D, DIN]); wb_rbr = dscr("wb_rbr", [DEPTH, 1024, D]); wb_abr = dscr("wb_abr", [DEPTH, 512, D])
    wb_out = dscr("wb_out", [DEPTH, D, D]); wb_up = dscr("wb_up", [DEPTH, D, DFF]); wb_dn = dscr("wb_dn", [DEPTH, DFF, D])
    wb_ple = dscr("wb_ple", [DEPTH, DPLE, D]); wb_pg = dscr("wb_pg", [DEPTH, D, D])
    xres = dscr("xres", [NTOK, D], F32)
    rqT = dscr("rqT", [512, NTOK]); rkT = dscr("rkT", [512, NTOK])
    rv_d = dscr("rv", [NTOK, 1024]); kz_d = dscr("kz", [NTOK, 512]); srg_d = dscr("srg", [NTOK, 1024])
    aqT = [dscr("aqT%d" % g, [512, NTOK]) for g in range(3)]
    akT = [dscr("akT%d" % g, [512, NTOK]) for g in range(3)]
    av_d = [dscr("av%d" % g, [NTOK, 512]) for g in range(3)]
    sgaT = dscr("sgaT", [1024, NTOK]); sgbT = dscr("sgbT", [1024, NTOK])
    oretT = dscr("oretT", [1024, NTOK]); oattT = dscr("oattT", [512, NTOK])
    uT = dscr("uT", [DFF, NTOK])

    def fm(t):
        return t.rearrange("(k p) n -> p k n", p=128)

    cx = Ctx(nc)
    ARENA = 47 * 1024

    with (
        nc.sbuf_tensor("arena", [128, ARENA], F32) as arena,
        nc.psum_tensor("pa0", [128, 512], F32) as pa0, nc.psum_tensor("pa1", [128, 512], F32) as pa1,
        nc.psum_tensor("pa2", [128, 512], F32) as pa2, nc.psum_tensor("pa3", [128, 512], F32) as pa3,
        nc.psum_tensor("pa4", [128, 512], F32) as pa4, nc.psum_tensor("pa5", [128, 512], F32) as pa5,
        nc.psum_tensor("pt0", [128, 1024], BF16) as pt0, nc.psum_tensor("pt1", [128, 1024], BF16) as pt1,
    ):
        arena_ap = arena[:, :]
        PA = [p_[:, :] for p_ in (pa0, pa1, pa2, pa3, pa4, pa5)]
        PT = [p_[:, :] for p_ in (pt0, pt1)]
        state = {"off": 0, "base": 0}

        def carve(n_elems, dt=F32, shape=None):
            n32 = (n_elems + 1) // 2 if dt == BF16 else n_elems
            n32 = (n32 + 7) // 8 * 8
            off = state["off"]
            assert off + n32 <= ARENA, ("SBUF arena overflow", off, n32)
            state["off"] = off + n32
            v = arena_ap[:, off:off + n32]
            if dt == BF16:
                v = v.bitcast(BF16)[:, 0:n_elems]
            else:
                v = v[:, 0:n_elems]
            if shape is not None:
                names = " ".join("d%d" % i for i in range(len(shape)))
                kw = {"d%d" % i: s for i, s in enumerate(shape)}
                v = v.rearrange("p (%s) -> p %s" % (names, names), **kw)
            return v

        def phase_begin():
            state["off"] = state["base"]

        class _Stop(Exception):
            pass

        def phase_end():
            cx.barrier()
            state["nph"] = state.get("nph", 0) + 1
            if STOP_AFTER and state["nph"] >= STOP_AFTER and state.get("main"):
                raise _Stop()

        ident = carve(128, BF16)
        ones = carve(128, BF16)
        RM = carve(512, F32, (4, 128))
        XI = carve(512, F32, (4, 128))
        ZC = carve(8, F32)
        AM = carve(12 * 256, F32, (12, 256))
        SM = carve(24 * 32, F32, (24, 32))
        hT = carve(8 * SB_W, BF16, (8, SB_W))
        Rf = carve(4 * 256, F32, (4, 256))
        Rb = carve(4 * 256, BF16, (4, 256))
        epsb = carve(1, F32)
        zt16 = carve(8 * 128, BF16, (8, 128))
        vones = carve(NSB * 128, BF16, (NSB, 128))
        state["base"] = state["off"]

        rr = {"ps": 0, "ev": 0}

        def dma(fn, reads=(), writes=(), n=1, eng="sp", bg=False):
            if eng == "sp" and DMASPREAD:
                rr["dq"] = (rr.get("dq", 0) + 1) % 2
                eng = ("sp", "act")[rr["dq"]]
            return cx.op(eng, fn, reads=reads, writes=writes, dma=n, bg=bg)

        def dma1(out, in_, reads=(), writes=(), eng="sp", bg=False):
            return dma(lambda e: [e.dma_start(out=out, in_=in_)], reads, writes, 1, eng, bg)

        def mm(ps_ap, pairs, reads, ps_key):
            def fn(e):
                r = None
                n = len(pairs)
                for i, (a, b) in enumerate(pairs):
                    r = e.matmul(ps_ap, a, b, start=(i == 0), stop=(i == n - 1))
                return r
            return cx.op("pe", fn, reads=reads, writes=[ps_key])

        def transposes(pt_ap_list, in_list, reads, ps_key):
            def fn(e):
                r = None
                for o, i_ in zip(pt_ap_list, in_list):
                    r = e.transpose(o, i_, ident[0:i_.shape[0], 0:i_.shape[0]])
                return r
            return cx.op("pe", fn, reads=list(reads) + ["ident"], writes=[ps_key])

        def act(out, in_, func, reads, writes, bias=None, scale=None):
            kw = {}
            if bias is not None:
                kw["bias"] = bias
            if scale is not None:
                kw["scale"] = scale
            return cx.op("act", lambda e: e.activation(out=out, in_=in_, func=func, **kw), reads=reads, writes=writes)

        def tt(eng, out, in0, in1, op, reads, writes):
            return cx.op(eng, lambda e: e.tensor_tensor(out=out, in0=in0, in1=in1, op=op), reads=reads, writes=writes)

        def evac_copy(out, in_, reads, writes, force=None):
            rr["ev"] ^= 1
            if force == "dve":
                rr["ev"] = 0
            if rr["ev"]:
                return act(out, in_, AF.Copy, reads, writes)
            return cx.op("dve", lambda e: e.tensor_copy(out=out, in_=in_), reads=reads, writes=writes)


        def dve_ttr(out, in0, in1, accum, reads, writes):
            act(out, in0, AF.Square, reads, [writes[0]])
            return cx.op("dve", lambda e: e.tensor_reduce(out=accum, in_=out, axis=mybir.AxisListType.X, op=ALU.add),
                         reads=[writes[0]], writes=list(writes[1:]))

        def recip(out, in_, reads, writes):
            return cx.op("dve", lambda e: e.reciprocal(out=out, in_=in_), reads=reads, writes=writes)

        def stt(out, in0, scalar, in1, op0, op1, reads, writes):
            return cx.op("dve", lambda e: e.scalar_tensor_tensor(out=out, in0=in0, scalar=scalar, in1=in1, op0=op0, op1=op1),
                         reads=reads, writes=writes)

        def ts(eng, out, in0, s1, s2, op0, op1, reads, writes):
            if s2 is None:
                return cx.op(eng, lambda e: e.tensor_scalar(out=out, in0=in0, scalar1=s1, scalar2=None, op0=op0), reads=reads, writes=writes)
            return cx.op(eng, lambda e: e.tensor_scalar(out=out, in0=in0, scalar1=s1, scalar2=s2, op0=op0, op1=op1), reads=reads, writes=writes)

        def copy(eng, out, in_, reads, writes):
            return cx.op(eng, lambda e: e.tensor_copy(out=out, in_=in_), reads=reads, writes=writes)

        def memset(eng, out, val, writes):
            return cx.op(eng, lambda e: e.memset(out, val), writes=writes)

        def bnstats(out, in_, reads, writes):
            return cx.op("dve", lambda e: e.bn_stats(out=out, in_=in_), reads=reads, writes=writes)

        def bnaggr(out, in_, reads, writes):
            return cx.op("dve", lambda e: e.bn_aggr(out=out, in_=in_), reads=reads, writes=writes)

        def dman(pairs, reads=(), writes=(), eng="sp"):
            pairs = list(pairs)
            return dma(lambda e: [e.dma_start(out=o, in_=i) for (o, i) in pairs], reads, writes, len(pairs), eng)

        phase_begin()
        tmpc = carve(128, F32)
        dma1(tmpc, c_ident[:, :], writes=["tmpc"])
        copy("dve", ident, tmpc, ["tmpc"], ["ident"])
        memset("dve", ones, 1.0, ["ones"])
        memset("dve", epsb, EPS, ["epsb"])
        memset("dve", zt16, 0.0, ["zt16"])
        vtmp = carve(NSB * 128, F32, (NSB, 128))
        dma1(vtmp, vones_in[:, :, :], writes=["vtmp"])
        copy("dve", vones, vtmp, ["vtmp"], ["vones"])
        dma1(RM, c_rm[:, :, :], writes=["RM"])
        dma1(XI, c_xi[:, :, :], writes=["XI"])
        dma1(ZC, c_z[:, :], writes=["ZC"])
        dma1(AM, c_am[:, :, :], writes=["AM"])
        dma1(SM, c_sm[:, :, :], writes=["SM"])
        for (src, dst, rows) in ((w_in, wb_in, D), (w_rbr, wb_rbr, 1024), (w_abr, wb_abr, 512), (w_out, wb_out, D),
                                 (w_up, wb_up, D), (w_dn, wb_dn, DFF), (w_ple, wb_ple, DPLE), (w_pg, wb_pg, D)):
            for l in range(DEPTH):
                for r0 in range(0, rows, 128):
                    dma1(dst[l, r0:r0 + 128, :], src[l, r0:r0 + 128, :], eng="pool")
        for r0 in range(0, W, 1024):
            dma1(xres[r0:r0 + 1024, :], xw[r0:r0 + 1024, :])
        ztile = carve(D, F32)
        memset("dve", ztile, 0.0, ["ztile"])
        dma1(xres[W:W + 128, :], ztile, reads=["ztile"], writes=["xres_s"])
        dma1(xres[W:W + NS * ST, :], xs[:, :], writes=["xres_s"])
        for l in range(DEPTH):
            for g in range(3):
                L = GROUPS[g][0]
                for (src, dst) in ((ck[g], sk[g]), (cv[g], sv[g])):
                    for q in range(NS):
                        dma1(dst[l, q, 0:L - ST, :], src[l, q, ST:L, :], bg=True)
        phase_end()

        def norm_tile(xt, xkey, gain, col0, bufs, i):
            junk, ss, hb = bufs
            s = i % 2
            dve_ttr(junk[s], xt, xt, ss[s][:, 0:1], [xkey], [("junk", s), ("ss", s)])
            act(ss[s][:, 1:2], ss[s][:, 0:1], AF.Sqrt, [("ss", s), "epsb"], [("ss", s)], bias=epsb[:, 0:1], scale=1.0 / D)
            recip(ss[s][:, 2:3], ss[s][:, 1:2], [("ss", s)], [("ss", s)])
            stt(hb[s], xt, ss[s][:, 2:3], gain, ALU.mult, ALU.mult, [xkey, ("ss", s), "gain"], [("hb", s)])
            ptv = PT[s].rearrange("p (k n) -> p k n", k=8)
            transposes([ptv[:, k, :] for k in range(8)], [hb[s][:, k * 128:(k + 1) * 128] for k in range(8)],
                       [("hb", s)], ("pt", s))
            act(hT[:, :, col0:col0 + 128], ptv, AF.Copy, [("pt", s)], ["hT"])

        def norm_bufs():
            junk = [carve(D, F32) for _ in range(2)]
            ss = [carve(4, F32) for _ in range(2)]
            hb = [carve(D, BF16) for _ in range(2)]
            return junk, ss, hb

        state["main"] = True
        try:
            SBS = [(sb, sb * SB_W, SB_W, False) for sb in range(NSB)] + [(NSB, W, 128, True)]

            for l in range(DEPTH):
                memset("dve", Rf, 0.0, ["Rf"])
                memset("dve", Rb, 0.0, ["Rb"])
                for (sb, c0, Wd, is_s) in SBS:
                    ntile = Wd // 128
                    nblk = max(1, Wd // 512)
                    bw = min(512, Wd)
                    last_prompt = (not is_s) and sb == NSB - 1
                    pruned = PRUNE and (l == DEPTH - 1) and (not is_s) and sb < NSB - 1
                    need_halo_kv = pruned and sb == NSB - 2

                    phase_begin()
                    gain = carve(D, F32)
                    dma1(gain, nmix[l, :, :], writes=["gain"])
                    xts = [carve(D, F32) for _ in range(2)]
                    nb = norm_bufs()
                    for t in range(ntile):
                        s = t % 2
                        dma1(xts[s], xres[c0 + t * 128:c0 + (t + 1) * 128, :], writes=[("xt", s)])
                        norm_tile(xts[s], ("xt", s), gain, t * 128, nb, t)
                    phase_end()

                    phase_begin()
                    wps = [carve(8 * 512, BF16, (8, 512)) for _ in range(2)]
                    stg = [carve(512, BF16) for _ in range(3)]
                    stgf = [carve(512, F32) for _ in range(2)]
                    stgB = [carve(SB_W, BF16) for _ in range(2)]
                    cnt = {"w": 0, "s": 0, "f": 0, "B": 0}

                    def load_w(col0):
                        if KP and cnt["w"] >= KP:
                            raise _Stop()
                        s = cnt["w"] % 2
                        cnt["w"] += 1
                        dma1(wps[s], fm(wb_in[l])[:, :, col0:col0 + 512], writes=[("wp", s)])
                        return wps[s], ("wp", s)

                    def psum_next():
                        rr["ps"] = (rr["ps"] + 1) % 2
                        return PA[rr["ps"]], ("pa", rr["ps"])

                    def fm_piece(wp, wkey, dst, func, dil):
                        dd = 1 if is_s else dil
                        upb = bw // dd
                        for cc in range(4):
                            sB = cnt["B"] % 2
                            cnt["B"] += 1
                            stv = stgB[sB][:, 0:Wd].rearrange("p (r u) -> p r u", r=dd)
                            for b in range(nblk):
                                pap, pkey = psum_next()
                                mm(pap[:, 0:bw], [(wp[:, k, cc * 128:(cc + 1) * 128], hT[:, k, b * bw:(b + 1) * bw]) for k in range(8)],
                                   [wkey, "hT"], pkey)
                                src_ = pap[:, 0:bw].rearrange("p (u r) -> p r u", r=dd)
                                dstv = stv[:, :, b * upb:(b + 1) * upb]
                                if func is None:
                                    evac_copy(dstv, src_, [pkey], [("stgB", sB, b)])
                                else:
                                    act(dstv, src_, func, [pkey], [("stgB", sB, b)])
                            dma1(dst[cc * 128:(cc + 1) * 128, c0:c0 + Wd], stgB[sB][:, 0:Wd], reads=[("stgB", sB, b) for b in range(nblk)])

                    def tm_cols(t, dil):
                        if is_s or dil == 1:
                            return slice(t * 128, (t + 1) * 128)
                        per = 16 // dil
                        r, c = t // per, t % per
                        start = r + dil * 128 * c
                        return slice(start, start + dil * 127 + 1, dil)

                    def tm_piece(wp, wkey, dst, dcol0, func, dil, zscale=False, outs=None):
                        for t in range(ntile):
                            pap, pkey = psum_next()
                            cs = tm_cols(t, dil)
                            mm(pap, [(hT[:, k, cs], wp[:, k, :]) for k in range(8)], [wkey, "hT"], pkey)
                            if dst is not None and not (KSKIP & 1 and cnt["w"] == 9):
                                s = cnt["s"] % 3
                                cnt["s"] += 1
                                if zscale:
                                    zo = 4 if is_s else 0
                                    for h in range(4):
                                        ts("dve", stg[s][:, h * 128:(h + 1) * 128], pap[:, h * 128:(h + 1) * 128], ZC[:, zo + h:zo + h + 1], None,
                                           ALU.mult, None, [pkey, "ZC"], [("stg", s)])
                                elif func is None:
                                    evac_copy(stg[s], pap, [pkey], [("stg", s)], force=("dve" if outs is not None else None))
                                else:
                                    act(stg[s], pap, func, [pkey], [("stg", s)])
                                dma1(dst[c0 + t * 128:c0 + (t + 1) * 128, dcol0:dcol0 + 512], stg[s], reads=[("stg", s)])
                            if outs is not None and not (KSKIP & 2 and cnt["w"] == 9):
                                outs(t, pap, pkey)

                    def out_window(g, dst_p, dst_s):
                        L = LG[g]

                        def f(t, pap, pkey):
                            if is_s:
                                s = cnt["f"] % 2
                                cnt["f"] += 1
                                evac_copy(stgf[s], pap, [pkey], [("stgf", s)], force="dve")
                                Ls = GROUPS[g][0]
                                dman([(dst_s[l, q, Ls - ST:Ls, :], stgf[s][q * ST:(q + 1) * ST, :]) for q in range(NS)], reads=[("stgf", s)])
                            elif last_prompt:
                                tok0 = c0 + t * 128
                                if tok0 >= W - L:
                                    s = cnt["f"] % 2
                                    cnt["f"] += 1
                                    evac_copy(stgf[s], pap, [pkey], [("stgf", s)], force="dve")
                                    r0 = tok0 - (W - L)
                                    dma1(dst_p[l, r0:r0 + 128, :], stgf[s], reads=[("stgf", s)])
                        return f

                    need_out = is_s or last_prompt
                    if not pruned:
                        wp, wk = load_w(C_RQ); fm_piece(wp, wk, rqT, None, 1)
                    wp, wk = load_w(C_RK)
                    if not pruned:
                        fm_piece(wp, wk, rkT, None, 1)
                    tm_piece(wp, wk, kz_d, 0, None, 1, zscale=True)
                    for hlf in range(2):
                        wp, wk = load_w(C_RV + hlf * 512); tm_piece(wp, wk, rv_d, hlf * 512, None, 1)
                    for hlf in range(2):
                        if pruned:
                            break
                        wp, wk = load_w(C_RG + hlf * 512); tm_piece(wp, wk, srg_d, hlf * 512, AF.Silu, 1)
                    for g in range(3):
                        if pruned and not need_halo_kv:
                            break
                        dil = GROUPS[g][1]
                        if not pruned:
                            wp, wk = load_w(C_AQ + g * 512); fm_piece(wp, wk, aqT[g], None, dil)
                        wp, wk = load_w(C_AK + g * 512); fm_piece(wp, wk, akT[g], None, dil)
                        if need_out:
                            tm_piece(wp, wk, None, 0, None, 1, outs=out_window(g, pk[g], sk[g]))
                        wp, wk = load_w(C_AV + g * 512)
                        if dil == 1 or is_s:
                            tm_piece(wp, wk, av_d[g], 0, None, 1, outs=out_window(g, pv[g], sv[g]) if need_out else None)
                        else:
                            tm_piece(wp, wk, av_d[g], 0, None, dil)
                            if need_out:
                                tm_piece(wp, wk, None, 0, None, 1, outs=out_window(g, pv[g], sv[g]))
                    for hlf in range(2):
                        if pruned:
                            break
                        wp, wk = load_w(C_GA + hlf * 512); fm_piece(wp, wk, sgaT[hlf * 512:(hlf + 1) * 512, :], AF.Sigmoid, 1)
                    for hlf in range(2):
                        if pruned:
                            break
                        wp, wk = load_w(C_GB + hlf * 512); fm_piece(wp, wk, sgbT[hlf * 512:(hlf + 1) * 512, :], AF.Sigmoid, 1)
                    phase_end()

                    phase_begin()
                    NB2 = 2
                    qTc = [carve(512, BF16, (4, 128)) for _ in range(NB2)]
                    kTc = [carve(512, BF16, (4, 128)) for _ in range(NB2)]
                    vc = [carve(1024, BF16) for _ in range(NB2)]
                    kzc = [carve(512, BF16) for _ in range(NB2)]
                    srgc = [carve(1024, BF16) for _ in range(NB2)]
                    Sm = [carve(128, BF16) for _ in range(2)]
                    qx = [carve(128, BF16) for _ in range(2)]
                    lnst = [carve(16, F32) for _ in range(2)]
                    onb = [carve(256, F32) for _ in range(2)]
                    og = [carve(1024, BF16) for _ in range(2)]
                    oTs = [carve(8 * 128, BF16, (8, 128)) for _ in range(2)]
                    if is_s:
                        Rfs = carve(4 * 256, F32, (4, 256))
                        Rbs = carve(4 * 256, BF16, (4, 256))
                    nch = NS if is_s else ntile
                    cw = ST if is_s else 128
                    if is_s:
                        dma1(fm(oretT)[:, :, c0:c0 + 128], zt16, reads=["zt16"], writes=["oretT_s"])
                        dma1(fm(oattT)[:, :, c0:c0 + 128], zt16[:, 0:4, :], reads=["zt16"], writes=["oattT_s"])
                    it = 0
                    for n in range(nch):
                        s = n % NB2
                        tc0 = c0 + n * cw
                        dma1(vc[s][0:cw, :], rv_d[tc0:tc0 + cw, :], writes=[("vc", s)])
                        dma1(kzc[s][0:cw, :], kz_d[tc0:tc0 + cw, :], writes=[("kzc", s)])
                        if pruned:
                            for h in range(4):
                                i2 = it % 2
                                it += 1
                                mm(PA[4 + i2][:, 0:256], [(kzc[s][0:cw, h * 128:(h + 1) * 128], vc[s][0:cw, h * 256:(h + 1) * 256])],
                                   [("kzc", s), ("vc", s)], ("pa", 4 + i2))
                                stt(Rf[:, h, :], Rf[:, h, :], dec128[h], PA[4 + i2][:, 0:256], ALU.mult, ALU.add, [("pa", 4 + i2), "Rf"], ["Rf"])
                                if n == nch - 1:
                                    act(Rb[:, h, :], Rf[:, h, :], AF.Copy, ["Rf"], ["Rb"])
                            continue
                        dma1(qTc[s][:, :, 0:cw], fm(rqT)[:, :, tc0:tc0 + cw], writes=[("qTc", s)])
                        dma1(kTc[s][:, :, 0:cw], fm(rkT)[:, :, tc0:tc0 + cw], writes=[("kTc", s)])
                        dma1(srgc[s][0:cw, :], srg_d[tc0:tc0 + cw, :], writes=[("srgc", s)])
                        if is_s:
                            dma1(Rfs, st_in[l, n].rearrange("h p v -> p h v"), writes=["Rfs"])
                            copy("pool", Rbs, Rfs, ["Rfs"], ["Rbs"])
                            RF, RB, rfk, rbk, dec = Rfs, Rbs, "Rfs", "Rbs", dec8
                        else:
                            RF, RB, rfk, rbk, dec = Rf, Rb, "Rf", "Rb", dec128
                        for h in range(4):
                            i2 = it % 2
                            it += 1
                            mm(PA[i2][0:cw, 0:cw], [(kTc[s][:, h, 0:cw], qTc[s][:, h, 0:cw])], [("kTc", s), ("qTc", s)], ("pa", i2))
                            tt("dve", Sm[i2][0:cw, 0:cw], PA[i2][0:cw, 0:cw], RM[0:cw, h, 0:cw], ALU.mult, [("pa", i2), "RM"], [("Sm", i2)])
                            tt("pool", qx[i2][:, 0:cw], qTc[s][:, h, 0:cw], XI[:, h, 0:cw], ALU.mult, [("qTc", s), "XI"], [("qx", i2)])
                            mm(PA[2 + i2][0:cw, 0:256], [(Sm[i2][0:cw, 0:cw], vc[s][0:cw, h * 256:(h + 1) * 256]),
                                                     (qx[i2][:, 0:cw], RB[:, h, :])],
                               [("Sm", i2), ("vc", s), ("qx", i2), rbk], ("pa", 2 + i2))
                            mm(PA[4 + i2][:, 0:256], [(kzc[s][0:cw, h * 128:(h + 1) * 128], vc[s][0:cw, h * 256:(h + 1) * 256])],
                               [("kzc", s), ("vc", s)], ("pa", 4 + i2))
                            dh = dec[h]
                            stt(RF[:, h, :], RF[:, h, :], dh, PA[4 + i2][:, 0:256], ALU.mult, ALU.add, [("pa", 4 + i2), rfk], [rfk])
                            act(RB[:, h, :], RF[:, h, :], AF.Copy, [rfk], [rbk])
                            ls = lnst[i2]
                            bnstats(ls[0:cw, 0:6], PA[2 + i2][0:cw, 0:256], [("pa", 2 + i2)], [("ln", i2)])
                            bnaggr(ls[0:cw, 6:8], ls[0:cw, 0:6], [("ln", i2)], [("ln", i2)])
                            act(ls[0:cw, 8:9], ls[0:cw, 7:8], AF.Sqrt, [("ln", i2), "epsb"], [("ln", i2)], bias=epsb[0:cw, 0:1], scale=1.0)
                            recip(ls[0:cw, 9:10], ls[0:cw, 8:9], [("ln", i2)], [("ln", i2)])
                            ts("dve", ls[0:cw, 10:11], ls[0:cw, 6:7], ls[0:cw, 9:10], -1.0, ALU.mult, ALU.mult, [("ln", i2)], [("ln", i2)])
                            act(onb[i2][0:cw, :], PA[2 + i2][0:cw, 0:256], AF.Identity, [("pa", 2 + i2), ("ln", i2)], [("onb", i2)],
                                bias=ls[0:cw, 10:11], scale=ls[0:cw, 9:10])
                            os_ = n % 2
                            tt("pool", og[os_][0:cw, h * 256:(h + 1) * 256], onb[i2][0:cw, :], srgc[s][0:cw, h * 256:(h + 1) * 256], ALU.mult,
                               [("onb", i2), ("srgc", s)], [("og", os_)])
                        os_ = n % 2
                        ptv = PT[os_].rearrange("p (k n) -> p k n", k=8)
                        transposes([ptv[:, k, 0:cw] for k in range(8)], [og[os_][0:cw, k * 128:(k + 1) * 128] for k in range(8)],
                                   [("og", os_)], ("pt", os_))
                        act(oTs[os_][:, :, 0:cw], ptv[:, :, 0:cw], AF.Copy, [("pt", os_)], [("oTs", os_)])
                        dma1(fm(oretT)[:, :, tc0:tc0 + cw], oTs[os_][:, :, 0:cw], reads=[("oTs", os_)], writes=(["oretT_s"] if is_s else []))
                        if is_s:
                            dma1(sret[l, n].rearrange("h p v -> p h v"), Rfs, reads=["Rfs"])
                    if last_prompt:
                        dma1(pret[l].rearrange("h p v -> p h v"), Rf, reads=["Rf"])
                    phase_end()

                    phase_begin()
                    sc = 128 ** -0.5
                    if pruned:
                        pass
                    elif not is_s:
                        accU = carve(4 * SB_W, F32, (4, SB_W))
                        accD = carve(4 * SB_W, F32, (4, SB_W))
                        NB3 = 2
                        KTs = [carve(128 + 512, BF16) for _ in range(NB3)]
                        QTs = [carve(512, BF16) for _ in range(NB3)]
                        Vs = [carve(5 * 128, BF16, (5, 128)) for _ in range(NB3)]
                        Eb = [carve(256, F32) for _ in range(2)]
                        Pb = [carve(256, BF16) for _ in range(2)]
                        it = 0
                        ld = 0
                        for g in range(3):
                            dil = GROUPS[g][1]
                            U = SB_W // dil
                            for j in range(4):
                                for r in range(dil):
                                    for ub in range(0, U, 512):
                                        ubw = min(512, U - ub)
                                        nq = ubw // 128
                                        s = ld % NB3
                                        ld += 1
                                        colb = c0 + r * U + ub
                                        has_halo = not (sb == 0 and ub == 0)
                                        if ub > 0:
                                            hcol = colb - 128
                                        else:
                                            hcol = (c0 - SB_W) + r * U + (U - 128)
                                        rows = slice(j * 128, (j + 1) * 128)
                                        if has_halo:
                                            dma1(KTs[s][:, 0:128], akT[g][rows, hcol:hcol + 128], writes=[("KTs", s)])
                                            dma1(Vs[s][:, 0, :], av_d[g][hcol:hcol + 128, rows], writes=[("Vs", s)])
                                        dma1(KTs[s][:, 128:128 + ubw], akT[g][rows, colb:colb + ubw], writes=[("KTs", s)])
                                        dma1(QTs[s][:, 0:ubw], aqT[g][rows, colb:colb + ubw], writes=[("QTs", s)])
                                        dma1(Vs[s][:, 1:1 + nq, :], av_d[g][colb:colb + ubw, rows].rearrange("(c p) d -> p c d", p=128),
                                             writes=[("Vs", s)])
                                        for cq in range(nq):
                                            i2 = it % 2
                                            it += 1
                                            halo = has_halo or cq > 0
                                            qap = QTs[s][:, cq * 128:(cq + 1) * 128]
                                            lo = 0 if halo else 128
                                            if halo:
                                                mm(PA[i2][:, 0:128], [(KTs[s][:, cq * 128:(cq + 1) * 128], qap)], [("KTs", s), ("QTs", s)], ("pa", i2))
                                            mm(PA[i2][:, 128:256], [(KTs[s][:, (cq + 1) * 128:(cq + 2) * 128], qap)], [("KTs", s), ("QTs", s)], ("pa", i2))
                                            act(Eb[i2][:, lo:256], PA[i2][:, lo:256], AF.Exp, [("pa", i2)], [("Eb", i2)], scale=sc)
                                            tt("pool", Pb[i2][:, lo:256], Eb[i2][:, lo:256], AM[:, g * 4 + j, lo:256], ALU.mult, [("Eb", i2), "AM"], [("Pb", i2)])
                                            pairs_u = []
                                            pairs_d = []
                                            if halo:
                                                pairs_u.append((Vs[s][:, cq, :], Pb[i2][:, 0:128]))
                                                pairs_d.append((vones[:, (sb - 1 if (cq == 0 and ub == 0) else sb), :], Pb[i2][:, 0:128]))
                                            pairs_u.append((Vs[s][:, cq + 1, :], Pb[i2][:, 128:256]))
                                            pairs_d.append((vones[:, sb, :], Pb[i2][:, 128:256]))
                                            mm(PA[2 + i2][:, 0:128], pairs_u, [("Vs", s), ("Pb", i2)], ("pa", 2 + i2))
                                            mm(PA[4 + i2][:, 0:128], pairs_d, ["vones", ("Pb", i2)], ("pa", 4 + i2))
                                            u0 = ub + cq * 128
                                            start = r + dil * u0
                                            csl = slice(start, start + dil * 127 + 1, dil)
                                            if g == 0:
                                                copy("dve", accU[:, j, csl], PA[2 + i2][:, 0:128], [("pa", 2 + i2)], [("accU", j)])
                                                act(accD[:, j, csl], PA[4 + i2][:, 0:128], AF.Copy, [("pa", 4 + i2)], [("accD", j)])
                                            else:
                                                tt("dve", accU[:, j, csl], PA[2 + i2][:, 0:128], accU[:, j, csl], ALU.add, [("pa", 2 + i2), ("accU", j)], [("accU", j)])
                                                tt("dve", accD[:, j, csl], PA[4 + i2][:, 0:128], accD[:, j, csl], ALU.add, [("pa", 4 + i2), ("accD", j)], [("accD", j)])
                        ofin = [carve(512, BF16) for _ in range(2)]
                        k2 = 0
                        for j in range(4):
                            for b in range(SB_W // 512):
                                bs = slice(b * 512, (b + 1) * 512)
                                ts("dve", accD[:, j, bs], accD[:, j, bs], 1e-30, None, ALU.add, None, [("accD", j)], [("accD", j)])
                                recip(accD[:, j, bs], accD[:, j, bs], [("accD", j)], [("accD", j)])
                                s2 = k2 % 2
                                k2 += 1
                                tt("pool", ofin[s2], accU[:, j, bs], accD[:, j, bs], ALU.mult, [("accU", j), ("accD", j)], [("ofin", s2)])
                                dma1(oattT[j * 128:(j + 1) * 128, c0 + b * 512:c0 + (b + 1) * 512], ofin[s2], reads=[("ofin", s2)])
                    else:
                        Kc = [carve(512, BF16) for _ in range(2)]
                        KT = [carve(512, BF16, (4, 128)) for _ in range(2)]
                        Vc = carve(21 * 512, BF16, (21, 512))
                        Knew = carve(3 * 4 * ST, BF16, (3, 4, ST))
                        Vnew = carve(3 * 512, BF16, (3, 512))
                        Qn = carve(3 * 4 * ST, BF16, (3, 4, ST))
                        Pall = carve(24 * 32, BF16, (24, 32))
                        Es = [carve(32, F32) for _ in range(2)]
                        osb = carve(64, F32)
                        ofs = carve(32, BF16)
                        for q in range(NS):
                            tcol = c0 + q * ST
                            for g in range(3):
                                dma1(Knew[:, g, :, :], fm(akT[g])[:, :, tcol:tcol + ST], writes=["Knew"])
                                dma1(Qn[:, g, :, :], fm(aqT[g])[:, :, tcol:tcol + ST], writes=["Qn"])
                                dma1(Vnew[0:ST, g, :], av_d[g][tcol:tcol + ST, :], writes=["Vnew"])
                            vt = 0
                            for g in range(3):
                                L = GROUPS[g][0]
                                dma1(Vc[:, vt:vt + L // 128, :], cv[g][l, q].rearrange("(t p) d -> p t d", p=128), writes=["Vc"], eng="pool")
                                vt += L // 128
                            ti = 0
                            kt_i = 0
                            for g in range(3):
                                L = GROUPS[g][0]
                                for t in range(L // 128 + 1):
                                    e2 = ti % 2
                                    if t < L // 128:
                                        s = kt_i % 2
                                        kt_i += 1
                                        dma1(Kc[s], ck[g][l, q, t * 128:(t + 1) * 128, :], writes=[("Kc", s)], eng="pool")
                                        ptv = PT[s].rearrange("p (k n) -> p k n", k=8)
                                        transposes([ptv[:, jj, :] for jj in range(4)], [Kc[s][:, jj * 128:(jj + 1) * 128] for jj in range(4)],
                                                   [("Kc", s)], ("pt", s))
                                        act(KT[s], ptv[:, 0:4, :], AF.Copy, [("pt", s)], [("KT", s)])
                                        kp = 128
                                        for jj in range(4):
                                            mm(PA[2][:, jj * ST:(jj + 1) * ST], [(KT[s][:, jj, :], Qn[:, g, jj, :])], [("KT", s), "Qn"], ("pa", 2))
                                    else:
                                        kp = ST
                                        for jj in range(4):
                                            mm(PA[2][0:ST, jj * ST:(jj + 1) * ST], [(Knew[:, g, jj, :], Qn[:, g, jj, :])], ["Knew", "Qn"], ("pa", 2))
                                    act(Es[e2][0:kp, :], PA[2][0:kp, 0:32], AF.Exp, [("pa", 2)], [("Es", e2)], scale=sc)
                                    tt("dve", Pall[0:kp, ti, :], Es[e2][0:kp, :], SM[0:kp, ti, :], ALU.mult, [("Es", e2), "SM"], ["Pall"])
                                    ti += 1
                            tiles = []
                            ti = 0
                            vt = 0
                            for g in range(3):
                                L = GROUPS[g][0]
                                for t in range(L // 128):
                                    tiles.append((ti, 128, ("c", vt)))
                                    ti += 1
                                    vt += 1
                                tiles.append((ti, ST, ("n", g)))
                                ti += 1
                            for jj in range(4):
                                pairs = []
                                for (ti_, kp, (kind, idx)) in tiles:
                                    if kind == "c":
                                        lhs = Vc[:, idx, jj * 128:(jj + 1) * 128]
                                    else:
                                        lhs = Vnew[0:ST, idx, jj * 128:(jj + 1) * 128]
                                    pairs.append((lhs, Pall[0:kp, ti_, jj * ST:(jj + 1) * ST]))
                                mm(PA[3][:, jj * ST:(jj + 1) * ST], pairs, ["Vc", "Vnew", "Pall"], ("pa", 3))
                            pairs = [(ones[0:kp, :], Pall[0:kp, ti_, :]) for (ti_, kp, _) in tiles]
                            mm(PA[4][:, 0:32], pairs, ["ones", "Pall"], ("pa", 4))
                            recip(osb[:, 0:32], PA[4][:, 0:32], [("pa", 4)], ["osb"])
                            tt("dve", ofs, PA[3][:, 0:32], osb[:, 0:32], ALU.mult, [("pa", 3), "osb"], ["ofs"])
                            dma1(fm(oattT)[:, :, tcol:tcol + ST], ofs.rearrange("p (j i) -> p j i", j=4), reads=["ofs"])
                    phase_end()
                    if pruned:
                        continue

                    phase_begin()
                    gain = carve(D, F32)
                    dma1(gain, nffn[l, :, :], writes=["gain"])
                    wr = carve(8 * 1024, BF16, (8, 1024))
                    wa = carve(4 * 1024, BF16, (4, 1024))
                    wo = carve(8 * 1024, BF16, (8, 1024))
                    dma1(wr, fm(wb_rbr[l]), writes=["wr"])
                    dma1(wa, fm(wb_abr[l]), writes=["wa"])
                    dma1(wo, fm(wb_out[l]), writes=["wo"])
                    NB4 = 1
                    orT = [carve(8 * 512, BF16, (8, 512)) for _ in range(NB4)]
                    oaT = [carve(4 * 512, BF16, (4, 512)) for _ in range(NB4)]
                    gaT = [carve(8 * 512, BF16, (8, 512)) for _ in range(NB4)]
                    gbT = [carve(8 * 512, BF16, (8, 512)) for _ in range(NB4)]
                    mT = [carve(8 * 512, BF16, (8, 512)) for _ in range(2)]
                    t1 = [carve(512, F32) for _ in range(2)]
                    t2 = [carve(512, F32) for _ in range(2)]
                    xts = [carve(D, F32) for _ in range(2)]
                    nb = norm_bufs()
                    ci = 0
                    ti_g = 0
                    for b in range(nblk):
                        s = b % NB4
                        cb = c0 + b * bw
                        dma1(orT[s][:, :, 0:bw], fm(oretT)[:, :, cb:cb + bw], writes=[("orT", s)])
                        dma1(oaT[s][:, :, 0:bw], fm(oattT)[:, :, cb:cb + bw], writes=[("oaT", s)])
                        dma1(gaT[s][:, :, 0:bw], fm(sgaT)[:, :, cb:cb + bw], writes=[("gaT", s)])
                        dma1(gbT[s][:, :, 0:bw], fm(sgbT)[:, :, cb:cb + bw], writes=[("gbT", s)])
                        ms = b % 2
                        for cc in range(8):
                            c2 = ci % 2
                            ci += 1
                            mm(PA[0][:, 0:bw], [(wr[:, k, cc * 128:(cc + 1) * 128], orT[s][:, k, 0:bw]) for k in range(8)], ["wr", ("orT", s)], ("pa", 0))
                            mm(PA[1][:, 0:bw], [(wa[:, k, cc * 128:(cc + 1) * 128], oaT[s][:, k, 0:bw]) for k in range(4)], ["wa", ("oaT", s)], ("pa", 1))
                            tt("dve", t1[c2][:, 0:bw], PA[0][:, 0:bw], gaT[s][:, cc, 0:bw], ALU.mult, [("pa", 0), ("gaT", s)], [("t1", c2)])
                            tt("dve", t2[c2][:, 0:bw], PA[1][:, 0:bw], gbT[s][:, cc, 0:bw], ALU.mult, [("pa", 1), ("gbT", s)], [("t2", c2)])
                            tt("pool", mT[ms][:, cc, 0:bw], t1[c2][:, 0:bw], t2[c2][:, 0:bw], ALU.add, [("t1", c2), ("t2", c2)], [("mT", ms)])
                        for tl in range(bw // 128):
                            t = b * (bw // 128) + tl
                            xs_ = ti_g % 2
                            ti_g += 1
                            r0 = c0 + t * 128
                            dma1(xts[xs_], xres[r0:r0 + 128, :], writes=[("xt", xs_)])
                            for hf in range(2):
                                pb = 2 + hf
                                mm(PA[pb], [(mT[ms][:, k, tl * 128:(tl + 1) * 128], wo[:, k, hf * 512:(hf + 1) * 512]) for k in range(8)],
                                   [("mT", ms), "wo"], ("pa", pb))
                                tt("dve", xts[xs_][:, hf * 512:(hf + 1) * 512], PA[pb], xts[xs_][:, hf * 512:(hf + 1) * 512], ALU.add,
                                   [("pa", pb), ("xt", xs_)], [("xt", xs_)])
                            dma1(xres[r0:r0 + 128, :], xts[xs_], reads=[("xt", xs_)])
                            norm_tile(xts[xs_], ("xt", xs_), gain, t * 128, nb, ti_g)
                    phase_end()

                    phase_begin()
                    wps = [carve(8 * 512, BF16, (8, 512)) for _ in range(2)]
                    rl = [carve(512, F32) for _ in range(2)]
                    us = [carve(512, BF16) for _ in range(2)]
                    k3 = 0
                    for pc in range(DFF // 512):
                        s = pc % 2
                        dma1(wps[s], fm(wb_up[l])[:, :, pc * 512:(pc + 1) * 512], writes=[("wp", s)])
                        for cc in range(4):
                            for b in range(nblk):
                                rr["ps"] = (rr["ps"] + 1) % 2
                                pb = rr["ps"]
                                mm(PA[pb][:, 0:bw], [(wps[s][:, k, cc * 128:(cc + 1) * 128], hT[:, k, b * bw:(b + 1) * bw]) for k in range(8)],
                                   [("wp", s), "hT"], ("pa", pb))
                                s3 = k3 % 2
                                k3 += 1
                                act(rl[s3][:, 0:bw], PA[pb][:, 0:bw], AF.Relu, [("pa", pb)], [("rl", s3)])
                                tt("pool", us[s3][:, 0:bw], rl[s3][:, 0:bw], rl[s3][:, 0:bw], ALU.mult, [("rl", s3)], [("us", s3)])
                                fr = pc * 512 + cc * 128
                                dma1(uT[fr:fr + 128, c0 + b * bw:c0 + (b + 1) * bw], us[s3][:, 0:bw], reads=[("us", s3)])
                    phase_end()

                    phase_begin()
                    wd = carve(32 * 1024, BF16, (32, 1024))
                    for kq in range(4):
                        dma1(wd[:, kq * 8:(kq + 1) * 8, :], fm(wb_dn[l])[:, kq * 8:(kq + 1) * 8, :], writes=["wd"])
                    wg = carve(8 * 1024, BF16, (8, 1024))
                    dma1(wg, fm(wb_pg[l]), writes=["wg"])
                    wpl = carve(2 * 1024, BF16, (2, 1024))
                    dma1(wpl, fm(wb_ple[l]), writes=["wpl"])
                    final = (l == DEPTH - 1)
                    if final:
                        gain = carve(D, F32)
                        dma1(gain, nfin[:, :], writes=["gain"])
                    hT_flat = hT.rearrange("p k n -> p (k n)")
                    uTs = [hT_flat[:, i_ * 8192:(i_ + 1) * 8192].rearrange("p (k n) -> p k n", k=32) for i_ in range(2)]
                    xts = [carve(D, F32) for _ in range(2)]
                    pts = [carve(DPLE, F32) for _ in range(2)]
                    xb = [carve(D, BF16) for _ in range(1)]
                    pb16 = [carve(DPLE, BF16) for _ in range(2)]
                    xT = [carve(8 * 128, BF16, (8, 128)) for _ in range(1)]
                    pT = [carve(2 * 128, BF16, (2, 128)) for _ in range(2)]
                    sg = [carve(512, F32) for _ in range(1)]
                    pp = [carve(512, F32) for _ in range(1)]
                    yt = [carve(D, F32) for _ in range(1)]
                    ssf = [carve(4, F32) for _ in range(2)]
                    k4 = 0
                    for t in range(ntile):
                        s = t % 2
                        r0 = c0 + t * 128
                        us_ = (t // 2) % 2
                        uo = (t % 2) * 128
                        if t % 2 == 0:
                            tw = min(256, Wd - t * 128)
                            dma1(uTs[us_][:, :, 0:tw], fm(uT)[:, :, r0:r0 + tw], writes=[("uTs", us_)])
                        dma1(xts[s], xres[r0:r0 + 128, :], writes=[("xt", s)])
                        if is_s:
                            memset("pool", pts[s], 0.0, [("pts", s)])
                            dma1(pts[s][0:NS * ST, :], ps_in[l, :, :], writes=[("pts", s)])
                        else:
                            dma1(pts[s], pw[l, r0:r0 + 128, :], writes=[("pts", s)])
                        for hf in range(2):
                            pbk = hf
                            mm(PA[pbk], [(uTs[us_][:, k, uo:uo + 128], wd[:, k, hf * 512:(hf + 1) * 512]) for k in range(32)], [("uTs", us_), "wd"], ("pa", pbk))
                            tt("dve", xts[s][:, hf * 512:(hf + 1) * 512], PA[pbk], xts[s][:, hf * 512:(hf + 1) * 512], ALU.add,
                               [("pa", pbk), ("xt", s)], [("xt", s)])
                        copy("pool", xb[0], xts[s], [("xt", s)], [("xb", 0)])
                        copy("pool", pb16[s], pts[s], [("pts", s)], [("pb16", s)])
                        ptv = PT[0].rearrange("p (k n) -> p k n", k=8)
                        transposes([ptv[:, k, :] for k in range(8)], [xb[0][:, k * 128:(k + 1) * 128] for k in range(8)], [("xb", 0)], ("pt", 0))
                        act(xT[0], ptv, AF.Copy, [("pt", 0)], [("xT", 0)])
                        ptv1 = PT[1].rearrange("p (k n) -> p k n", k=8)
                        transposes([ptv1[:, k, :] for k in range(2)], [pb16[s][:, k * 128:(k + 1) * 128] for k in range(2)], [("pb16", s)], ("pt", 1))
                        copy("dve", pT[s], ptv1[:, 0:2, :], [("pt", 1)], [("pT", s)])
                        for hf in range(2):
                            s4 = 0
                            mm(PA[2], [(xT[0][:, k, :], wg[:, k, hf * 512:(hf + 1) * 512]) for k in range(8)], [("xT", 0), "wg"], ("pa", 2))
                            mm(PA[3], [(pT[s][:, k, :], wpl[:, k, hf * 512:(hf + 1) * 512]) for k in range(2)], [("pT", s), "wpl"], ("pa", 3))
                            act(sg[s4], PA[2], AF.Sigmoid, [("pa", 2)], [("sg", s4)])
                            tt("dve", pp[s4], PA[3], sg[s4], ALU.mult, [("pa", 3), ("sg", s4)], [("pp", s4)])
                            tt("pool", xts[s][:, hf * 512:(hf + 1) * 512], xts[s][:, hf * 512:(hf + 1) * 512], pp[s4], ALU.add,
                               [("xt", s), ("pp", s4)], [("xt", s)])
                        if not final:
                            dma1(xres[r0:r0 + 128, :], xts[s], reads=[("xt", s)])
                        else:
                            dve_ttr(yt[0], xts[s], xts[s], ssf[s][:, 0:1], [("xt", s)], [("yt", 0), ("ssf", s)])
                            act(ssf[s][:, 1:2], ssf[s][:, 0:1], AF.Sqrt, [("ssf", s), "epsb"], [("ssf", s)], bias=epsb[:, 0:1], scale=1.0 / D)
                            recip(ssf[s][:, 2:3], ssf[s][:, 1:2], [("ssf", s)], [("ssf", s)])
                            stt(yt[0], xts[s], ssf[s][:, 2:3], gain, ALU.mult, ALU.mult, [("xt", s), ("ssf", s), "gain"], [("yt", 0)])
                            if is_s:
                                dma1(ys[:, :], yt[0][0:NS * ST, :], reads=[("yt", 0)])
                            elif sb == NSB - 1:
                                dma1(y[r0 - (W - SB_W):r0 - (W - SB_W) + 128, :], yt[0], reads=[("yt", 0)])
                    phase_end()
        except _Stop:
            pass

        cx.barrier(final=True)

        semnames = {}
        import contextlib
        with contextlib.ExitStack() as es:
            sems = {}
            for i, sk_ in enumerate(cx.semkeys):
                sems[sk_] = es.enter_context(nc.semaphore("s%d" % i))
            with nc.Block() as block:
                @block.tensor
                def _(e):
                    cx.replay("pe", e, sems)

                @block.scalar
                def _(e):
                    cx.replay("act", e, sems)

                @block.vector
                def _(e):
                    cx.replay("dve", e, sems)

                @block.gpsimd
                def _(e):
                    cx.replay("pool", e, sems)

                @block.sync
                def _(e):
                    cx.replay("sp", e, sems)
    return nc


_CACHE = {}


def make_in_maps(W, inputs, n_cores=8):
    consts = make_consts()
    f = lambda a: np.ascontiguousarray(np.asarray(a, dtype=np.float32))
    B = inputs["x_prompt"].shape[0]
    cores_per_b = n_cores // B
    shared = {k: f(inputs[k]) for k in ("w_in", "w_ret_br", "w_att_br", "w_out", "w_up", "w_down", "w_ple", "w_ple_gate")}
    shared["nmix"] = f(np.broadcast_to(np.asarray(inputs["norm_mix"])[:, None, :], (DEPTH, 128, D)))
    shared["nffn"] = f(np.broadcast_to(np.asarray(inputs["norm_ffn"])[:, None, :], (DEPTH, 128, D)))
    shared["nfin"] = f(np.broadcast_to(np.asarray(inputs["norm_final"])[None, :], (128, D)))
    shared.update(consts)
    maps = []
    NSB = W // SB_W
    for c in range(n_cores):
        b = c // cores_per_b
        seg = ((c % cores_per_b) * NSB) // cores_per_b
        npad = (NSB - 1 - seg) * SB_W
        m = dict(shared)
        xwin = np.zeros((W, D), np.float32)
        xwin[npad:] = np.asarray(inputs["x_prompt"])[b, :W - npad]
        pwin = np.zeros((DEPTH, W, DPLE), np.float32)
        pwin[:, npad:] = np.asarray(inputs["p_prompt"])[:, b, :W - npad]
        m["xw"] = xwin
        m["pw"] = pwin
        vo = np.zeros((128, NSB, 128), np.float32)
        vo[:, NSB - 1 - seg:, :] = 1.0
        m["vones"] = vo
        sl = slice(c * NS, (c + 1) * NS)
        m["xs"] = f(np.asarray(inputs["x_sample"])[sl].reshape(NS * ST, D))
        m["ps"] = f(np.asarray(inputs["p_sample"])[:, sl].reshape(DEPTH, NS * ST, DPLE))
        caches_k = (inputs["cache_win_k0"], inputs["cache_win_k1"], inputs["cache_win_k2"])
        caches_v = (inputs["cache_win_v0"], inputs["cache_win_v1"], inputs["cache_win_v2"])
        for g in range(3):
            L = GROUPS[g][0]
            m["ck%d" % g] = f(np.asarray(caches_k[g])[:, sl].reshape(DEPTH, NS, L, 512))
            m["cv%d" % g] = f(np.asarray(caches_v[g])[:, sl].reshape(DEPTH, NS, L, 512))
        m["st"] = f(np.asarray(inputs["state_ret"])[:, sl])
        maps.append(m)
    return maps


def assemble(W, res, B, n_cores=8):
    cores_per_b = n_cores // B
    R = res.results
    LG = [min(g[0], W) for g in GROUPS]
    NSB = W // SB_W
    y_prompt = np.zeros((B, W, D), np.float32)
    for c in range(n_cores):
        b = c // cores_per_b
        seg = ((c % cores_per_b) * NSB) // cores_per_b
        y_prompt[b, seg * SB_W:(seg + 1) * SB_W] = R[c]["y"]
    y_sample = np.concatenate([R[c]["ys"].reshape(NS, ST, D) for c in range(n_cores)]).astype(np.float32)
    outs = [y_prompt, y_sample]
    for g in range(3):
        for nm in ("pk", "pv"):
            a = np.stack([R[b * cores_per_b + cores_per_b - 1][nm + str(g)] for b in range(B)], axis=1)
            outs.append(a.reshape(DEPTH, B, LG[g], 4, 128).astype(np.float32))
    outs.append(np.stack([R[b * cores_per_b + cores_per_b - 1]["pret"] for b in range(B)], axis=1).astype(np.float32))
    for g in range(3):
        L = GROUPS[g][0]
        for nm in ("sk", "sv"):
            a = np.concatenate([R[c][nm + str(g)] for c in range(n_cores)], axis=1)
            outs.append(a.reshape(DEPTH, n_cores * NS, L, 4, 128).astype(np.float32))
    outs.append(np.concatenate([R[c]["sret"] for c in range(n_cores)], axis=1).astype(np.float32))
    return tuple(outs)


def kernel(**inputs):
    W = int(np.asarray(inputs["x_prompt"]).shape[1])
    B = int(np.asarray(inputs["x_prompt"]).shape[0])
    if W not in _CACHE:
        _CACHE[W] = build(W)
    nc = _CACHE[W]
    in_maps = make_in_maps(W, inputs)
    res = run_bass_kernel_spmd(nc, in_maps, core_ids=list(range(8)))
    return assemble(W, res, B)
```

```python
import math
import numpy as np
import concourse.bass as bass
import concourse.mybir as mybir
from concourse.bass_utils import run_bass_kernel_spmd

F32 = mybir.dt.float32
BF16 = mybir.dt.bfloat16
AF = mybir.ActivationFunctionType
ALU = mybir.AluOpType

D = 1024
DEPTH = 2
DPLE = 256
DFF = 4096
DIN = 9728
EPS = 1e-6
GROUPS = ((128, 1), (512, 4), (2048, 16))
SB_W = 2048
NS = 4
ST = 8
NDMA = 16
NSDMA = 4
import os
STOP_AFTER = int(os.environ.get("KSTOP", "0"))
KP = int(os.environ.get("KP", "0"))
KSKIP = int(os.environ.get("KSKIP", "0"))
DMASPREAD = int(os.environ.get("DMASPREAD", "0"))
PRUNE = int(os.environ.get("KPRUNE", "1"))
ENGS = ("pe", "act", "dve", "pool", "sp")

C_RQ, C_RK, C_RV, C_RG, C_AQ, C_AK, C_AV, C_GA, C_GB = 0, 512, 1024, 2048, 3072, 4608, 6144, 7680, 8704


def _lg():
    return np.log1p(-np.exp(np.linspace(math.log(1.0 / 32), math.log(1.0 / 512), 4)))


def _slopes():
    return 2.0 ** (-8.0 * (np.arange(12, dtype=np.float64) + 1.0) / 12)


def make_consts():
    lg = _lg()
    sl = _slopes()
    p = np.arange(128)
    c = {}
    c["c_ident"] = np.eye(128, dtype=np.float32)
    diff = p[None, :] - p[:, None]
    rm = np.zeros((128, 4, 128), np.float64)
    for h in range(4):
        rm[:, h, :] = np.where(diff >= 0, np.exp(lg[h] * np.maximum(diff, 0)), 0.0) * (128 ** -0.5)
    c["c_rm"] = rm.astype(np.float32)
    xi = np.zeros((128, 4, 128), np.float64)
    for h in range(4):
        xi[:, h, :] = np.exp(lg[h] * (p + 1.0))[None, :]
    c["c_xi"] = xi.astype(np.float32)
    z = np.zeros((128, 8), np.float64)
    for h in range(4):
        z[:, h] = np.exp(lg[h] * (127.0 - p)) * (128 ** -0.5)
        z[:, 4 + h] = np.exp(lg[h] * (7.0 - (p % 8))) * (128 ** -0.5)
    c["c_z"] = z.astype(np.float32)
    am = np.zeros((128, 12, 256), np.float64)
    i = np.arange(128)
    for g, (Wg, dil) in enumerate(GROUPS):
        for j in range(4):
            s = sl[g * 4 + j] * dil
            dprev = i[None, :] + 128 - p[:, None]
            ddiag = i[None, :] - p[:, None]
            am[:, g * 4 + j, 0:128] = np.where(dprev <= 128, np.exp(-s * dprev), 0.0)
            am[:, g * 4 + j, 128:256] = np.where(ddiag >= 0, np.exp(-s * np.maximum(ddiag, 0)), 0.0)
    c["c_am"] = am.astype(np.float32)
    sm = np.zeros((128, 24, 32), np.float64)
    t = 0
    for g, (Wg, dil) in enumerate(GROUPS):
        L = Wg
        ntile = L // 128
        for tt in range(ntile + 1):
            for j in range(4):
                s = sl[g * 4 + j]
                for qi in range(8):
                    if tt < ntile:
                        kpos = tt * 128 + p
                        ok = np.ones(128, bool)
                    else:
                        kpos = L + p
                        ok = p < 8
                    dist = (L + qi) - kpos
                    valid = ok & (dist >= 0) & (dist % dil == 0) & (dist // dil <= 128)
                    sm[:, t, j * 8 + qi] = np.where(valid, np.exp(-s * np.maximum(dist, 0)), 0.0)
            t += 1
    c["c_sm"] = sm.astype(np.float32)
    return c


class Ctx:
    def __init__(self, nc):
        self.nc = nc
        self.q = {e: [] for e in ENGS}
        self.semcount = {}
        self.seen = {e: {} for e in ENGS}
        self.lastw = {}
        self.readers = {}
        self.dma_rr = 0
        self.sdma_rr = 0
        self.semkeys = [("eng", e) for e in ("pe", "act", "dve", "pool")] + [("dma", i) for i in range(NDMA)] + [("sdma", i) for i in range(NSDMA)] + [("sbg", i) for i in range(NSDMA)] + [("bg", 0)]
        for s in self.semkeys:
            self.semcount[s] = 0

    def op(self, eng, fn, reads=(), writes=(), dma=0, bg=False):
        deps = {}

        def add(tok):
            if tok is None:
                return
            s, v = tok
            if deps.get(s, 0) < v:
                deps[s] = v
        for k in reads:
            add(self.lastw.get(k))
        for k in writes:
            add(self.lastw.get(k))
            for t in self.readers.get(k, ()):
                add(t)
        if bg == "s":
            d = self.sdma_rr
            self.sdma_rr = (d + 1) % NSDMA
            s = ("sbg", d)
            if self.semcount[s] > 0:
                add((s, self.semcount[s]))
            self.semcount[s] += 16 * dma
            tok = (s, self.semcount[s])
        elif bg:
            s = ("bg", 0)
            self.semcount[s] += 16 * dma
            tok = (s, self.semcount[s])
        elif dma:
            if eng == "pool":
                d = self.sdma_rr
                self.sdma_rr = (d + 1) % NSDMA
                s = ("sdma", d)
            else:
                d = self.dma_rr
                self.dma_rr = (d + 1) % NDMA
                s = ("dma", d)
            if self.semcount[s] > 0:
                add((s, self.semcount[s]))
            self.semcount[s] += 16 * dma
            tok = (s, self.semcount[s])
        else:
            s = ("eng", eng)
            self.semcount[s] += 1
            tok = (s, self.semcount[s])
        waits = []
        for sk, v in deps.items():
            if sk == ("eng", "pe") and eng == "pe" and not dma:
                continue
            if self.seen[eng].get(sk, 0) >= v:
                continue
            self.seen[eng][sk] = v
            waits.append((sk, v))
        self.q[eng].append((waits, fn, tok[0], dma))
        for k in reads:
            self.readers.setdefault(k, []).append(tok)
        for k in writes:
            self.lastw[k] = tok
            self.readers[k] = []
        return tok

    def barrier(self, final=False, sbg=False):
        for e in ENGS:
            waits = []
            for sk in self.semkeys:
                if sk == ("bg", 0) and not final:
                    continue
                if sk[0] == "sbg" and not (final or sbg):
                    continue
                v = self.semcount[sk]
                if v > 0 and self.seen[e].get(sk, 0) < v:
                    self.seen[e][sk] = v
                    waits.append((sk, v))
            self.q[e].append((waits, None, None, 0))
        self.lastw = {}
        self.readers = {}

    def replay(self, eng, e, sems):
        for waits, fn, s, dma in self.q[eng]:
            for (sk, v) in waits:
                e.wait_ge(sems[sk], v)
            if fn is None:
                continue
            r = fn(e)
            if dma:
                assert len(r) == dma, (len(r), dma)
                for ins in r:
                    ins.then_inc(sems[s], 16)
            else:
                ins = r[-1] if isinstance(r, (list, tuple)) else r
                ins.then_inc(sems[s], 1)


def build(W):
    NSB = W // SB_W
    NT = W // 128
    TT = NT + 1
    NTOK = TT * 128
    LG = [min(g[0], W) for g in GROUPS]
    lg = _lg()
    dec128 = [float(np.exp(lg[h] * 128)) for h in range(4)]
    dec8 = [float(np.exp(lg[h] * 8)) for h in range(4)]

    nc = bass.Bass("TRN2", target_bir_lowering=False)

    def din(name, shape, dt=F32):
        return nc.dram_tensor(name, list(shape), dt, kind="ExternalInput").ap()

    def dout(name, shape):
        return nc.dram_tensor(name, list(shape), F32, kind="ExternalOutput").ap()

    def dscr(name, shape, dt=BF16):
        return nc.dram_tensor(name, list(shape), dt, kind="Internal").ap()

    xw = din("xw", [W, D]); xs = din("xs", [NS * ST, D])
    pw = din("pw", [DEPTH, W, DPLE]); ps_in = din("ps", [DEPTH, NS * ST, DPLE])
    ck = [din("ck%d" % g, [DEPTH, NS, GROUPS[g][0], 512]) for g in range(3)]
    cv = [din("cv%d" % g, [DEPTH, NS, GROUPS[g][0], 512]) for g in range(3)]
    st_in = din("st", [DEPTH, NS, 4, 128, 256])
    w_in = din("w_in", [DEPTH, D, DIN]); w_rbr = din("w_ret_br", [DEPTH, 1024, D]); w_abr = din("w_att_br", [DEPTH, 512, D])
    w_out = din("w_out", [DEPTH, D, D]); w_up = din("w_up", [DEPTH, D, DFF]); w_dn = din("w_down", [DEPTH, DFF, D])
    w_ple = din("w_ple", [DEPTH, DPLE, D]); w_pg = din("w_ple_gate", [DEPTH, D, D])
    nmix = din("nmix", [DEPTH, 128, D]); nffn = din("nffn", [DEPTH, 128, D]); nfin = din("nfin", [128, D])
    c_ident = din("c_ident", [128, 128]); c_rm = din("c_rm", [128, 4, 128]); c_xi = din("c_xi", [128, 4, 128])
    c_z = din("c_z", [128, 8]); c_am = din("c_am", [128, 12, 256]); c_sm = din("c_sm", [128, 24, 32])
    vones_in = din("vones", [128, NSB, 128])

    y = dout("y", [SB_W, D]); ys = dout("ys", [NS * ST, D])
    pk = [dout("pk%d" % g, [DEPTH, LG[g], 512]) for g in range(3)]
    pv = [dout("pv%d" % g, [DEPTH, LG[g], 512]) for g in range(3)]
    pret = dout("pret", [DEPTH, 4, 128, 256])
    sk = [dout("sk%d" % g, [DEPTH, NS, GROUPS[g][0], 512]) for g in range(3)]
    sv = [dout("sv%d" % g, [DEPTH, NS, GROUPS[g][0], 512]) for g in range(3)]
    sret = dout("sret", [DEPTH, NS, 4, 128, 256])

    wb_in = dscr("wb_in", [DEPTH, D, DIN]); wb_rbr = dscr("wb_rbr", [DEPTH, 1024, D]); wb_abr = dscr("wb_abr", [DEPTH, 512, D])
    wb_out = dscr("wb_out", [DEPTH, D, D]); wb_up = dscr("wb_up", [DEPTH, D, DFF]); wb_dn = dscr("wb_dn", [DEPTH, DFF, D])
    wb_ple = dscr("wb_ple", [DEPTH, DPLE, D]); wb_pg = dscr("wb_pg", [DEPTH, D, D])
    xres = dscr("xres", [NTOK, D], F32)
    rqT = dscr("rqT", [512, NTOK]); rkT = dscr("rkT", [512, NTOK])
    rv_d = dscr("rv", [NTOK, 1024]); kz_d = dscr("kz", [NTOK, 512]); srg_d = dscr("srg", [NTOK, 1024])
    aqT = [dscr("aqT%d" % g, [512, NTOK]) for g in range(3)]
    akT = [dscr("akT%d" % g, [512, NTOK]) for g in range(3)]
    av_d = [dscr("av%d" % g, [NTOK, 512]) for g in range(3)]
    sgaT = dscr("sgaT", [1024, NTOK]); sgbT = dscr("sgbT", [1024, NTOK])
    oretT = dscr("oretT", [1024, NTOK]); oattT = dscr("oattT", [512, NTOK])
    uT = dscr("uT", [DFF, NTOK])

    def fm(t):
        return t.rearrange("(k p) n -> p k n", p=128)

    cx = Ctx(nc)
    ARENA = 47 * 1024

    with (
        nc.sbuf_tensor("arena", [128, ARENA], F32) as arena,
        nc.psum_tensor("pab0", [128, 1024], F32) as pab0, nc.psum_tensor("pab1", [128, 1024], F32) as pab1,
        nc.psum_tensor("pa4", [128, 512], F32) as pa4, nc.psum_tensor("pa5", [128, 512], F32) as pa5,
        nc.psum_tensor("pt0", [128, 1024], BF16) as pt0, nc.psum_tensor("pt1", [128, 1024], BF16) as pt1,
    ):
        arena_ap = arena[:, :]
        PAB = [pab0[:, :], pab1[:, :]]
        PA = [PAB[0][:, 0:512], PAB[0][:, 512:1024], PAB[1][:, 0:512], PAB[1][:, 512:1024], pa4[:, :], pa5[:, :]]
        PT = [p_[:, :] for p_ in (pt0, pt1)]
        state = {"off": 0, "base": 0}

        def carve(n_elems, dt=F32, shape=None):
            n32 = (n_elems + 1) // 2 if dt == BF16 else n_elems
            n32 = (n32 + 7) // 8 * 8
            off = state["off"]
            assert off + n32 <= ARENA, ("SBUF arena overflow", off, n32)
            state["off"] = off + n32
            v = arena_ap[:, off:off + n32]
            if dt == BF16:
                v = v.bitcast(BF16)[:, 0:n_elems]
            else:
                v = v[:, 0:n_elems]
            if shape is not None:
                names = " ".join("d%d" % i for i in range(len(shape)))
                kw = {"d%d" % i: s for i, s in enumerate(shape)}
                v = v.rearrange("p (%s) -> p %s" % (names, names), **kw)
            return v

        def phase_begin():
            state["off"] = state["base"]

        class _Stop(Exception):
            pass

        def phase_end():
            cx.barrier()
            state["nph"] = state.get("nph", 0) + 1
            if STOP_AFTER and state["nph"] >= STOP_AFTER and state.get("main"):
                raise _Stop()

        ident = carve(128, BF16)
        ones = carve(128, BF16)
        RM = carve(512, F32, (4, 128))
        XI = carve(512, F32, (4, 128))
        ZC = carve(8, F32)
        AM = carve(12 * 256, F32, (12, 256))
        SM = carve(24 * 32, F32, (24, 32))
        hT = carve(8 * SB_W, BF16, (8, SB_W))
        Rf = carve(4 * 256, F32, (4, 256))
        Rb = carve(4 * 256, BF16, (4, 256))
        epsb = carve(1, F32)
        zt16 = carve(8 * 128, BF16, (8, 128))
        vones = carve(NSB * 128, BF16, (NSB, 128))
        state["base"] = state["off"]

        rr = {"ps": 0, "ev": 0}

        def dma(fn, reads=(), writes=(), n=1, eng="sp", bg=False):
            if eng == "sp" and DMASPREAD:
                rr["dq"] = (rr.get("dq", 0) + 1) % 2
                eng = ("sp", "act")[rr["dq"]]
            return cx.op(eng, fn, reads=reads, writes=writes, dma=n, bg=bg)

        def dma1(out, in_, reads=(), writes=(), eng="sp", bg=False):
            return dma(lambda e: [e.dma_start(out=out, in_=in_)], reads, writes, 1, eng, bg)

        def mm(ps_ap, pairs, reads, ps_key):
            def fn(e):
                r = None
                n = len(pairs)
                for i, (a, b) in enumerate(pairs):
                    r = e.matmul(ps_ap, a, b, start=(i == 0), stop=(i == n - 1))
                return r
            return cx.op("pe", fn, reads=reads, writes=[ps_key])

        def transposes(pt_ap_list, in_list, reads, ps_key):
            def fn(e):
                r = None
                for o, i_ in zip(pt_ap_list, in_list):
                    r = e.transpose(o, i_, ident[0:i_.shape[0], 0:i_.shape[0]])
                return r
            return cx.op("pe", fn, reads=list(reads) + ["ident"], writes=[ps_key])

        def act(out, in_, func, reads, writes, bias=None, scale=None):
            kw = {}
            if bias is not None:
                kw["bias"] = bias
            if scale is not None:
                kw["scale"] = scale
            return cx.op("act", lambda e: e.activation(out=out, in_=in_, func=func, **kw), reads=reads, writes=writes)

        def tt(eng, out, in0, in1, op, reads, writes):
            return cx.op(eng, lambda e: e.tensor_tensor(out=out, in0=in0, in1=in1, op=op), reads=reads, writes=writes)

        def evac_copy(out, in_, reads, writes, force=None):
            rr["ev"] ^= 1
            if force == "dve":
                rr["ev"] = 0
            if rr["ev"]:
                return act(out, in_, AF.Copy, reads, writes)
            return cx.op("dve", lambda e: e.tensor_copy(out=out, in_=in_), reads=reads, writes=writes)


        def dve_ttr(out, in0, in1, accum, reads, writes):
            act(out, in0, AF.Square, reads, [writes[0]])
            return cx.op("dve", lambda e: e.tensor_reduce(out=accum, in_=out, axis=mybir.AxisListType.X, op=ALU.add),
                         reads=[writes[0]], writes=list(writes[1:]))

        def recip(out, in_, reads, writes):
            return cx.op("dve", lambda e: e.reciprocal(out=out, in_=in_), reads=reads, writes=writes)

        def stt(out, in0, scalar, in1, op0, op1, reads, writes):
            return cx.op("dve", lambda e: e.scalar_tensor_tensor(out=out, in0=in0, scalar=scalar, in1=in1, op0=op0, op1=op1),
                         reads=reads, writes=writes)

        def ts(eng, out, in0, s1, s2, op0, op1, reads, writes):
            if s2 is None:
                return cx.op(eng, lambda e: e.tensor_scalar(out=out, in0=in0, scalar1=s1, scalar2=None, op0=op0), reads=reads, writes=writes)
            return cx.op(eng, lambda e: e.tensor_scalar(out=out, in0=in0, scalar1=s1, scalar2=s2, op0=op0, op1=op1), reads=reads, writes=writes)

        def copy(eng, out, in_, reads, writes):
            return cx.op(eng, lambda e: e.tensor_copy(out=out, in_=in_), reads=reads, writes=writes)

        def memset(eng, out, val, writes):
            return cx.op(eng, lambda e: e.memset(out, val), writes=writes)

        def bnstats(out, in_, reads, writes):
            return cx.op("dve", lambda e: e.bn_stats(out=out, in_=in_), reads=reads, writes=writes)

        def bnaggr(out, in_, reads, writes):
            return cx.op("dve", lambda e: e.bn_aggr(out=out, in_=in_), reads=reads, writes=writes)

        def dman(pairs, reads=(), writes=(), eng="sp"):
            pairs = list(pairs)
            return dma(lambda e: [e.dma_start(out=o, in_=i) for (o, i) in pairs], reads, writes, len(pairs), eng)

        phase_begin()
        tmpc = carve(128, F32)
        dma1(tmpc, c_ident[:, :], writes=["tmpc"])
        copy("dve", ident, tmpc, ["tmpc"], ["ident"])
        memset("dve", ones, 1.0, ["ones"])
        memset("dve", epsb, EPS, ["epsb"])
        memset("dve", zt16, 0.0, ["zt16"])
        vtmp = carve(NSB * 128, F32, (NSB, 128))
        dma1(vtmp, vones_in[:, :, :], writes=["vtmp"])
        copy("dve", vones, vtmp, ["vtmp"], ["vones"])
        dma1(RM, c_rm[:, :, :], writes=["RM"])
        dma1(XI, c_xi[:, :, :], writes=["XI"])
        dma1(ZC, c_z[:, :], writes=["ZC"])
        dma1(AM, c_am[:, :, :], writes=["AM"])
        dma1(SM, c_sm[:, :, :], writes=["SM"])
        late_casts = []
        for (src, dst, rows) in ((w_in, wb_in, D), (w_rbr, wb_rbr, 1024), (w_abr, wb_abr, 512), (w_out, wb_out, D),
                                 (w_up, wb_up, D), (w_dn, wb_dn, DFF), (w_ple, wb_ple, DPLE), (w_pg, wb_pg, D)):
            for l in range(DEPTH):
                for r0 in range(0, rows, 128):
                    first = (dst is wb_in and l == 0)
                    if first:
                        dma1(dst[l, r0:r0 + 128, :], src[l, r0:r0 + 128, :], eng="pool")
                    else:
                        late_casts.append((dst[l, r0:r0 + 128, :], src[l, r0:r0 + 128, :]))
        for r0 in range(0, W, 1024):
            dma1(xres[r0:r0 + 1024, :], xw[r0:r0 + 1024, :])
        ztile = carve(D, F32)
        memset("dve", ztile, 0.0, ["ztile"])
        dma1(xres[W:W + 128, :], ztile, reads=["ztile"], writes=["xres_s"])
        dma1(xres[W:W + NS * ST, :], xs[:, :], writes=["xres_s"])
        for l in range(DEPTH):
            for g in range(3):
                L = GROUPS[g][0]
                for (src, dst) in ((ck[g], sk[g]), (cv[g], sv[g])):
                    for q in range(NS):
                        dma1(dst[l, q, 0:L - ST, :], src[l, q, ST:L, :], bg=True)
        phase_end()
        for (o_, i_) in late_casts:
            dma1(o_, i_, eng="pool", bg="s")

        def norm_tile(xt, xkey, gain, col0, bufs, i):
            junk, ss, hb = bufs
            s = i % 2
            dve_ttr(junk[s], xt, xt, ss[s][:, 0:1], [xkey], [("junk", s), ("ss", s)])
            act(ss[s][:, 1:2], ss[s][:, 0:1], AF.Sqrt, [("ss", s), "epsb"], [("ss", s)], bias=epsb[:, 0:1], scale=1.0 / D)
            recip(ss[s][:, 2:3], ss[s][:, 1:2], [("ss", s)], [("ss", s)])
            stt(hb[s], xt, ss[s][:, 2:3], gain, ALU.mult, ALU.mult, [xkey, ("ss", s), "gain"], [("hb", s)])
            ptv = PT[s].rearrange("p (k n) -> p k n", k=8)
            transposes([ptv[:, k, :] for k in range(8)], [hb[s][:, k * 128:(k + 1) * 128] for k in range(8)],
                       [("hb", s)], ("pt", s))
            act(hT[:, :, col0:col0 + 128], ptv, AF.Copy, [("pt", s)], ["hT"])

        def norm_bufs():
            junk = [carve(D, F32) for _ in range(2)]
            ss = [carve(4, F32) for _ in range(2)]
            hb = [carve(D, BF16) for _ in range(2)]
            return junk, ss, hb

        state["main"] = True
        try:
            SBS = [(sb, sb * SB_W, SB_W, False) for sb in range(NSB)] + [(NSB, W, 128, True)]

            for l in range(DEPTH):
                memset("dve", Rf, 0.0, ["Rf"])
                memset("dve", Rb, 0.0, ["Rb"])
                for (sb, c0, Wd, is_s) in SBS:
                    ntile = Wd // 128
                    nblk = max(1, Wd // 512)
                    bw = min(512, Wd)
                    last_prompt = (not is_s) and sb == NSB - 1
                    pruned = PRUNE and (l == DEPTH - 1) and (not is_s) and sb < NSB - 1
                    need_halo_kv = pruned and sb == NSB - 2

                    phase_begin()
                    gain = carve(D, F32)
                    dma1(gain, nmix[l, :, :], writes=["gain"])
                    xts = [carve(D, F32) for _ in range(2)]
                    nb = norm_bufs()
                    for t in range(ntile):
                        s = t % 2
                        dma1(xts[s], xres[c0 + t * 128:c0 + (t + 1) * 128, :], writes=[("xt", s)])
                        norm_tile(xts[s], ("xt", s), gain, t * 128, nb, t)
                    phase_end()

                    phase_begin()
                    wps = [carve(8 * 512, BF16, (8, 512)) for _ in range(2)]
                    stg = [carve(512, BF16) for _ in range(3)]
                    stgf = [carve(512, F32) for _ in range(2)]
                    stgB = [carve(SB_W, BF16) for _ in range(2)]
                    cnt = {"w": 0, "s": 0, "f": 0, "B": 0}

                    def load_w(col0):
                        if KP and cnt["w"] >= KP:
                            raise _Stop()
                        s = cnt["w"] % 2
                        cnt["w"] += 1
                        dma1(wps[s], fm(wb_in[l])[:, :, col0:col0 + 512], writes=[("wp", s)])
                        return wps[s], ("wp", s)

                    def psum_next():
                        rr["ps"] = (rr["ps"] + 1) % 4
                        return PA[rr["ps"]], ("pa", rr["ps"])

                    def fm_piece(wp, wkey, dst, func, dil):
                        dd = 1 if is_s else dil
                        upb = bw // dd
                        for cc in range(4):
                            sB = cnt["B"] % 2
                            cnt["B"] += 1
                            stv = stgB[sB][:, 0:Wd].rearrange("p (r u) -> p r u", r=dd)
                            for b in range(nblk):
                                pap, pkey = psum_next()
                                mm(pap[:, 0:bw], [(wp[:, k, cc * 128:(cc + 1) * 128], hT[:, k, b * bw:(b + 1) * bw]) for k in range(8)],
                                   [wkey, "hT"], pkey)
                                src_ = pap[:, 0:bw].rearrange("p (u r) -> p r u", r=dd)
                                dstv = stv[:, :, b * upb:(b + 1) * upb]
                                if func is None:
                                    evac_copy(dstv, src_, [pkey], [("stgB", sB, b)])
                                else:
                                    act(dstv, src_, func, [pkey], [("stgB", sB, b)])
                            dma1(dst[cc * 128:(cc + 1) * 128, c0:c0 + Wd], stgB[sB][:, 0:Wd], reads=[("stgB", sB, b) for b in range(nblk)])

                    def tm_cols(t, dil):
                        if is_s or dil == 1:
                            return slice(t * 128, (t + 1) * 128)
                        per = 16 // dil
                        r, c = t // per, t % per
                        start = r + dil * 128 * c
                        return slice(start, start + dil * 127 + 1, dil)

                    def tm_piece(wp, wkey, dst, dcol0, func, dil, zscale=False, outs=None):
                        for t in range(ntile):
                            pap, pkey = psum_next()
                            cs = tm_cols(t, dil)
                            mm(pap, [(hT[:, k, cs], wp[:, k, :]) for k in range(8)], [wkey, "hT"], pkey)
                            if dst is not None and not (KSKIP & 1 and cnt["w"] == 9):
                                s = cnt["s"] % 3
                                cnt["s"] += 1
                                if zscale:
                                    zo = 4 if is_s else 0
                                    for h in range(4):
                                        ts("dve", stg[s][:, h * 128:(h + 1) * 128], pap[:, h * 128:(h + 1) * 128], ZC[:, zo + h:zo + h + 1], None,
                                           ALU.mult, None, [pkey, "ZC"], [("stg", s)])
                                elif func is None:
                                    evac_copy(stg[s], pap, [pkey], [("stg", s)], force=("dve" if outs is not None else None))
                                else:
                                    act(stg[s], pap, func, [pkey], [("stg", s)])
                                dma1(dst[c0 + t * 128:c0 + (t + 1) * 128, dcol0:dcol0 + 512], stg[s], reads=[("stg", s)])
                            if outs is not None and not (KSKIP & 2 and cnt["w"] == 9):
                                outs(t, pap, pkey)

                    def out_window(g, dst_p, dst_s):
                        L = LG[g]

                        def f(t, pap, pkey):
                            if is_s:
                                s = cnt["f"] % 2
                                cnt["f"] += 1
                                evac_copy(stgf[s], pap, [pkey], [("stgf", s)], force="dve")
                                Ls = GROUPS[g][0]
                                dman([(dst_s[l, q, Ls - ST:Ls, :], stgf[s][q * ST:(q + 1) * ST, :]) for q in range(NS)], reads=[("stgf", s)])
                            elif last_prompt:
                                tok0 = c0 + t * 128
                                if tok0 >= W - L:
                                    s = cnt["f"] % 2
                                    cnt["f"] += 1
                                    evac_copy(stgf[s], pap, [pkey], [("stgf", s)], force="dve")
                                    r0 = tok0 - (W - L)
                                    dma1(dst_p[l, r0:r0 + 128, :], stgf[s], reads=[("stgf", s)])
                        return f

                    need_out = is_s or last_prompt
                    if not pruned:
                        wp, wk = load_w(C_RQ); fm_piece(wp, wk, rqT, None, 1)
                    wp, wk = load_w(C_RK)
                    if not pruned:
                        fm_piece(wp, wk, rkT, None, 1)
                    tm_piece(wp, wk, kz_d, 0, None, 1, zscale=True)
                    for hlf in range(2):
                        wp, wk = load_w(C_RV + hlf * 512); tm_piece(wp, wk, rv_d, hlf * 512, None, 1)
                    for hlf in range(2):
                        if pruned:
                            break
                        wp, wk = load_w(C_RG + hlf * 512); tm_piece(wp, wk, srg_d, hlf * 512, AF.Silu, 1)
                    for g in range(3):
                        if pruned and not need_halo_kv:
                            break
                        dil = GROUPS[g][1]
                        if not pruned:
                            wp, wk = load_w(C_AQ + g * 512); fm_piece(wp, wk, aqT[g], None, dil)
                        wp, wk = load_w(C_AK + g * 512); fm_piece(wp, wk, akT[g], None, dil)
                        if need_out:
                            tm_piece(wp, wk, None, 0, None, 1, outs=out_window(g, pk[g], sk[g]))
                        wp, wk = load_w(C_AV + g * 512)
                        if dil == 1 or is_s:
                            tm_piece(wp, wk, av_d[g], 0, None, 1, outs=out_window(g, pv[g], sv[g]) if need_out else None)
                        else:
                            tm_piece(wp, wk, av_d[g], 0, None, dil)
                            if need_out:
                                tm_piece(wp, wk, None, 0, None, 1, outs=out_window(g, pv[g], sv[g]))
                    for hlf in range(2):
                        if pruned:
                            break
                        wp, wk = load_w(C_GA + hlf * 512); fm_piece(wp, wk, sgaT[hlf * 512:(hlf + 1) * 512, :], AF.Sigmoid, 1)
                    for hlf in range(2):
                        if pruned:
                            break
                        wp, wk = load_w(C_GB + hlf * 512); fm_piece(wp, wk, sgbT[hlf * 512:(hlf + 1) * 512, :], AF.Sigmoid, 1)
                    phase_end()

                    if l == 0 and sb == 0:
                        cx.barrier(sbg=True)
                    phase_begin()
                    NB2 = 2
                    qTc = [carve(512, BF16, (4, 128)) for _ in range(NB2)]
                    kTc = [carve(512, BF16, (4, 128)) for _ in range(NB2)]
                    vc = [carve(1024, BF16) for _ in range(NB2)]
                    kzc = [carve(512, BF16) for _ in range(NB2)]
                    srgc = [carve(1024, BF16) for _ in range(NB2)]
                    Sm = [carve(128, BF16) for _ in range(2)]
                    qx = [carve(128, BF16) for _ in range(2)]
                    lnst = [carve(16, F32) for _ in range(2)]
                    onb = [carve(256, F32) for _ in range(2)]
                    og = [carve(1024, BF16) for _ in range(2)]
                    oTs = [carve(8 * 128, BF16, (8, 128)) for _ in range(2)]
                    if is_s:
                        Rfs = carve(4 * 256, F32, (4, 256))
                        Rbs = carve(4 * 256, BF16, (4, 256))
                    nch = NS if is_s else ntile
                    cw = ST if is_s else 128
                    if is_s:
                        dma1(fm(oretT)[:, :, c0:c0 + 128], zt16, reads=["zt16"], writes=["oretT_s"])
                        dma1(fm(oattT)[:, :, c0:c0 + 128], zt16[:, 0:4, :], reads=["zt16"], writes=["oattT_s"])
                    it = 0
                    for n in range(nch):
                        s = n % NB2
                        tc0 = c0 + n * cw
                        dma1(vc[s][0:cw, :], rv_d[tc0:tc0 + cw, :], writes=[("vc", s)])
                        dma1(kzc[s][0:cw, :], kz_d[tc0:tc0 + cw, :], writes=[("kzc", s)])
                        if pruned:
                            for h in range(4):
                                i2 = it % 2
                                it += 1
                                mm(PA[4 + i2][:, 0:256], [(kzc[s][0:cw, h * 128:(h + 1) * 128], vc[s][0:cw, h * 256:(h + 1) * 256])],
                                   [("kzc", s), ("vc", s)], ("pa", 4 + i2))
                                stt(Rf[:, h, :], Rf[:, h, :], dec128[h], PA[4 + i2][:, 0:256], ALU.mult, ALU.add, [("pa", 4 + i2), "Rf"], ["Rf"])
                                if n == nch - 1:
                                    act(Rb[:, h, :], Rf[:, h, :], AF.Copy, ["Rf"], ["Rb"])
                            continue
                        dma1(qTc[s][:, :, 0:cw], fm(rqT)[:, :, tc0:tc0 + cw], writes=[("qTc", s)])
                        dma1(kTc[s][:, :, 0:cw], fm(rkT)[:, :, tc0:tc0 + cw], writes=[("kTc", s)])
                        dma1(srgc[s][0:cw, :], srg_d[tc0:tc0 + cw, :], writes=[("srgc", s)])
                        if is_s:
                            dma1(Rfs, st_in[l, n].rearrange("h p v -> p h v"), writes=["Rfs"])
                            copy("pool", Rbs, Rfs, ["Rfs"], ["Rbs"])
                            RF, RB, rfk, rbk, dec = Rfs, Rbs, "Rfs", "Rbs", dec8
                        else:
                            RF, RB, rfk, rbk, dec = Rf, Rb, "Rf", "Rb", dec128
                        for h in range(4):
                            i2 = it % 2
                            it += 1
                            mm(PA[i2][0:cw, 0:cw], [(kTc[s][:, h, 0:cw], qTc[s][:, h, 0:cw])], [("kTc", s), ("qTc", s)], ("pa", i2))
                            tt("dve", Sm[i2][0:cw, 0:cw], PA[i2][0:cw, 0:cw], RM[0:cw, h, 0:cw], ALU.mult, [("pa", i2), "RM"], [("Sm", i2)])
                            tt("pool", qx[i2][:, 0:cw], qTc[s][:, h, 0:cw], XI[:, h, 0:cw], ALU.mult, [("qTc", s), "XI"], [("qx", i2)])
                            mm(PA[2 + i2][0:cw, 0:256], [(Sm[i2][0:cw, 0:cw], vc[s][0:cw, h * 256:(h + 1) * 256]),
                                                     (qx[i2][:, 0:cw], RB[:, h, :])],
                               [("Sm", i2), ("vc", s), ("qx", i2), rbk], ("pa", 2 + i2))
                            mm(PA[4 + i2][:, 0:256], [(kzc[s][0:cw, h * 128:(h + 1) * 128], vc[s][0:cw, h * 256:(h + 1) * 256])],
                               [("kzc", s), ("vc", s)], ("pa", 4 + i2))
                            dh = dec[h]
                            stt(RF[:, h, :], RF[:, h, :], dh, PA[4 + i2][:, 0:256], ALU.mult, ALU.add, [("pa", 4 + i2), rfk], [rfk])
                            act(RB[:, h, :], RF[:, h, :], AF.Copy, [rfk], [rbk])
                            ls = lnst[i2]
                            bnstats(ls[0:cw, 0:6], PA[2 + i2][0:cw, 0:256], [("pa", 2 + i2)], [("ln", i2)])
                            bnaggr(ls[0:cw, 6:8], ls[0:cw, 0:6], [("ln", i2)], [("ln", i2)])
                            act(ls[0:cw, 8:9], ls[0:cw, 7:8], AF.Sqrt, [("ln", i2), "epsb"], [("ln", i2)], bias=epsb[0:cw, 0:1], scale=1.0)
                            recip(ls[0:cw, 9:10], ls[0:cw, 8:9], [("ln", i2)], [("ln", i2)])
                            ts("dve", ls[0:cw, 10:11], ls[0:cw, 6:7], ls[0:cw, 9:10], -1.0, ALU.mult, ALU.mult, [("ln", i2)], [("ln", i2)])
                            act(onb[i2][0:cw, :], PA[2 + i2][0:cw, 0:256], AF.Identity, [("pa", 2 + i2), ("ln", i2)], [("onb", i2)],
                                bias=ls[0:cw, 10:11], scale=ls[0:cw, 9:10])
                            os_ = n % 2
                            tt("pool", og[os_][0:cw, h * 256:(h + 1) * 256], onb[i2][0:cw, :], srgc[s][0:cw, h * 256:(h + 1) * 256], ALU.mult,
                               [("onb", i2), ("srgc", s)], [("og", os_)])
                        os_ = n % 2
                        ptv = PT[os_].rearrange("p (k n) -> p k n", k=8)
                        transposes([ptv[:, k, 0:cw] for k in range(8)], [og[os_][0:cw, k * 128:(k + 1) * 128] for k in range(8)],
                                   [("og", os_)], ("pt", os_))
                        act(oTs[os_][:, :, 0:cw], ptv[:, :, 0:cw], AF.Copy, [("pt", os_)], [("oTs", os_)])
                        dma1(fm(oretT)[:, :, tc0:tc0 + cw], oTs[os_][:, :, 0:cw], reads=[("oTs", os_)], writes=(["oretT_s"] if is_s else []))
                        if is_s:
                            dma1(sret[l, n].rearrange("h p v -> p h v"), Rfs, reads=["Rfs"])
                    if last_prompt:
                        dma1(pret[l].rearrange("h p v -> p h v"), Rf, reads=["Rf"])
                    phase_end()

                    phase_begin()
                    sc = 128 ** -0.5
                    if pruned:
                        pass
                    elif not is_s:
                        accU = carve(4 * SB_W, F32, (4, SB_W))
                        accD = carve(4 * SB_W, F32, (4, SB_W))
                        NB3 = 2
                        KTs = [carve(1024, BF16) for _ in range(NB3)]
                        QTs = [carve(512, BF16) for _ in range(NB3)]
                        Vs = [carve(8 * 128, BF16, (8, 128)) for _ in range(NB3)]
                        Eb = [carve(1024, F32) for _ in range(2)]
                        Pb = [carve(1024, BF16) for _ in range(2)]
                        it = 0
                        ld = 0
                        for g in range(3):
                            dil = GROUPS[g][1]
                            U = SB_W // dil
                            for j in range(4):
                                rows = slice(j * 128, (j + 1) * 128)
                                gj = g * 4 + j
                                if U >= 512:
                                    batches = [("c", r, ub) for r in range(dil) for ub in range(0, U, 512)]
                                else:
                                    batches = [("r", r0_, 0) for r0_ in range(0, dil, 4)]
                                for (kind, r, ub) in batches:
                                    s = ld % NB3
                                    ld += 1
                                    i2 = it % 2
                                    it += 1
                                    if kind == "c":
                                        colb = c0 + r * U + ub
                                        hcol = (colb - 128) if ub > 0 else ((c0 - SB_W) + r * U + (U - 128))
                                        halo_k = [not (sb == 0 and ub == 0 and k == 0) for k in range(4)]
                                        if halo_k[0]:
                                            dma1(KTs[s][:, 0:128], akT[g][rows, hcol:hcol + 128], writes=[("KTs", s)])
                                            dma1(Vs[s][:, 0, :], av_d[g][hcol:hcol + 128, rows], writes=[("Vs", s)])
                                        dma1(KTs[s][:, 128:640], akT[g][rows, colb:colb + 512], writes=[("KTs", s)])
                                        dma1(Vs[s][:, 1:5, :], av_d[g][colb:colb + 512, rows].rearrange("(c p) d -> p c d", p=128), writes=[("Vs", s)])
                                        prevK = lambda k: KTs[s][:, k * 128:(k + 1) * 128]
                                        diagK = lambda k: KTs[s][:, (k + 1) * 128:(k + 2) * 128]
                                        prevV = lambda k: Vs[s][:, k, :]
                                        diagV = lambda k: Vs[s][:, k + 1, :]
                                        pv_sb = [(sb - 1 if (k == 0 and ub == 0) else sb) for k in range(4)]
                                        start = r + dil * ub
                                        accsl = lambda a: a[:, j, start:start + dil * 511 + 1:dil]
                                    else:
                                        colb = c0 + r * U
                                        hcol = (c0 - SB_W) + r * U
                                        halo_k = [sb > 0] * 4
                                        if sb > 0:
                                            dma1(KTs[s][:, 0:512], akT[g][rows, hcol:hcol + 512], writes=[("KTs", s)])
                                            dma1(Vs[s][:, 0:4, :], av_d[g][hcol:hcol + 512, rows].rearrange("(c p) d -> p c d", p=128), writes=[("Vs", s)])
                                        dma1(KTs[s][:, 512:1024], akT[g][rows, colb:colb + 512], writes=[("KTs", s)])
                                        dma1(Vs[s][:, 4:8, :], av_d[g][colb:colb + 512, rows].rearrange("(c p) d -> p c d", p=128), writes=[("Vs", s)])
                                        prevK = lambda k: KTs[s][:, k * 128:(k + 1) * 128]
                                        diagK = lambda k: KTs[s][:, 512 + k * 128:512 + (k + 1) * 128]
                                        prevV = lambda k: Vs[s][:, k, :]
                                        diagV = lambda k: Vs[s][:, 4 + k, :]
                                        pv_sb = [sb - 1] * 4
                                        accsl = lambda a: a[:, j, :].rearrange("p (u d) -> p d u", d=dil)[:, r:r + 4, :]
                                    dma1(QTs[s], aqT[g][rows, colb:colb + 512], writes=[("QTs", s)])
                                    SP = PAB[i2]
                                    skey = ("pab", i2)
                                    for k in range(4):
                                        qap = QTs[s][:, k * 128:(k + 1) * 128]
                                        if halo_k[k]:
                                            mm(SP[:, k * 256:k * 256 + 128], [(prevK(k), qap)], [("KTs", s), ("QTs", s)], skey)
                                        mm(SP[:, k * 256 + 128:(k + 1) * 256], [(diagK(k), qap)], [("KTs", s), ("QTs", s)], skey)
                                    if all(halo_k):
                                        act(Eb[i2], SP, AF.Exp, [skey], [("Eb", i2)], scale=sc)
                                    else:
                                        for k in range(4):
                                            lo = k * 256 + (0 if halo_k[k] else 128)
                                            act(Eb[i2][:, lo:(k + 1) * 256], SP[:, lo:(k + 1) * 256], AF.Exp, [skey], [("Eb", i2)], scale=sc)
                                    for k in range(4):
                                        lo = k * 256 + (0 if halo_k[k] else 128)
                                        tt("pool", Pb[i2][:, lo:(k + 1) * 256], Eb[i2][:, lo:(k + 1) * 256], AM[:, gj, lo - k * 256:256], ALU.mult,
                                           [("Eb", i2), "AM"], [("Pb", i2)])
                                    for k in range(4):
                                        pairs_u = []
                                        pairs_d = []
                                        if halo_k[k]:
                                            pairs_u.append((prevV(k), Pb[i2][:, k * 256:k * 256 + 128]))
                                            pairs_d.append((vones[:, pv_sb[k], :], Pb[i2][:, k * 256:k * 256 + 128]))
                                        pairs_u.append((diagV(k), Pb[i2][:, k * 256 + 128:(k + 1) * 256]))
                                        pairs_d.append((vones[:, sb, :], Pb[i2][:, k * 256 + 128:(k + 1) * 256]))
                                        mm(PA[4][:, k * 128:(k + 1) * 128], pairs_u, [("Vs", s), ("Pb", i2)], ("pa", 4))
                                        mm(PA[5][:, k * 128:(k + 1) * 128], pairs_d, ["vones", ("Pb", i2)], ("pa", 5))
                                    if kind == "c":
                                        pu, pd = PA[4], PA[5]
                                    else:
                                        pu = PA[4].rearrange("p (k i) -> p k i", k=4)
                                        pd = PA[5].rearrange("p (k i) -> p k i", k=4)
                                    if g == 0:
                                        copy("dve", accsl(accU), pu, [("pa", 4)], [("accU", j)])
                                        act(accsl(accD), pd, AF.Copy, [("pa", 5)], [("accD", j)])
                                    else:
                                        tt("dve", accsl(accU), pu, accsl(accU), ALU.add, [("pa", 4), ("accU", j)], [("accU", j)])
                                        tt("dve", accsl(accD), pd, accsl(accD), ALU.add, [("pa", 5), ("accD", j)], [("accD", j)])
                        ofin = [carve(512, BF16) for _ in range(2)]
                        k2 = 0
                        for j in range(4):
                            for b in range(SB_W // 512):
                                bs = slice(b * 512, (b + 1) * 512)
                                ts("dve", accD[:, j, bs], accD[:, j, bs], 1e-30, None, ALU.add, None, [("accD", j)], [("accD", j)])
                                recip(accD[:, j, bs], accD[:, j, bs], [("accD", j)], [("accD", j)])
                                s2 = k2 % 2
                                k2 += 1
                                tt("pool", ofin[s2], accU[:, j, bs], accD[:, j, bs], ALU.mult, [("accU", j), ("accD", j)], [("ofin", s2)])
                                dma1(oattT[j * 128:(j + 1) * 128, c0 + b * 512:c0 + (b + 1) * 512], ofin[s2], reads=[("ofin", s2)])
                    else:
                        Kc = [carve(512, BF16) for _ in range(2)]
                        KT = [carve(512, BF16, (4, 128)) for _ in range(2)]
                        Vc = carve(21 * 512, BF16, (21, 512))
                        Knew = carve(3 * 4 * ST, BF16, (3, 4, ST))
                        Vnew = carve(3 * 512, BF16, (3, 512))
                        Qn = carve(3 * 4 * ST, BF16, (3, 4, ST))
                        Pall = carve(24 * 32, BF16, (24, 32))
                        Es = [carve(32, F32) for _ in range(2)]
                        osb = carve(64, F32)
                        ofs = carve(32, BF16)
                        for q in range(NS):
                            tcol = c0 + q * ST
                            for g in range(3):
                                dma1(Knew[:, g, :, :], fm(akT[g])[:, :, tcol:tcol + ST], writes=["Knew"])
                                dma1(Qn[:, g, :, :], fm(aqT[g])[:, :, tcol:tcol + ST], writes=["Qn"])
                                dma1(Vnew[0:ST, g, :], av_d[g][tcol:tcol + ST, :], writes=["Vnew"])
                            vt = 0
                            for g in range(3):
                                L = GROUPS[g][0]
                                dma1(Vc[:, vt:vt + L // 128, :], cv[g][l, q].rearrange("(t p) d -> p t d", p=128), writes=["Vc"], eng="pool")
                                vt += L // 128
                            ti = 0
                            kt_i = 0
                            for g in range(3):
                                L = GROUPS[g][0]
                                for t in range(L // 128 + 1):
                                    e2 = ti % 2
                                    if t < L // 128:
                                        s = kt_i % 2
                                        kt_i += 1
                                        dma1(Kc[s], ck[g][l, q, t * 128:(t + 1) * 128, :], writes=[("Kc", s)], eng="pool")
                                        ptv = PT[s].rearrange("p (k n) -> p k n", k=8)
                                        transposes([ptv[:, jj, :] for jj in range(4)], [Kc[s][:, jj * 128:(jj + 1) * 128] for jj in range(4)],
                                                   [("Kc", s)], ("pt", s))
                                        act(KT[s], ptv[:, 0:4, :], AF.Copy, [("pt", s)], [("KT", s)])
                                        kp = 128
                                        for jj in range(4):
                                            mm(PA[2][:, jj * ST:(jj + 1) * ST], [(KT[s][:, jj, :], Qn[:, g, jj, :])], [("KT", s), "Qn"], ("pa", 2))
                                    else:
                                        kp = ST
                                        for jj in range(4):
                                            mm(PA[2][0:ST, jj * ST:(jj + 1) * ST], [(Knew[:, g, jj, :], Qn[:, g, jj, :])], ["Knew", "Qn"], ("pa", 2))
                                    act(Es[e2][0:kp, :], PA[2][0:kp, 0:32], AF.Exp, [("pa", 2)], [("Es", e2)], scale=sc)
                                    tt("dve", Pall[0:kp, ti, :], Es[e2][0:kp, :], SM[0:kp, ti, :], ALU.mult, [("Es", e2), "SM"], ["Pall"])
                                    ti += 1
                            tiles = []
                            ti = 0
                            vt = 0
                            for g in range(3):
                                L = GROUPS[g][0]
                                for t in range(L // 128):
                                    tiles.append((ti, 128, ("c", vt)))
                                    ti += 1
                                    vt += 1
                                tiles.append((ti, ST, ("n", g)))
                                ti += 1
                            for jj in range(4):
                                pairs = []
                                for (ti_, kp, (kind, idx)) in tiles:
                                    if kind == "c":
                                        lhs = Vc[:, idx, jj * 128:(jj + 1) * 128]
                                    else:
                                        lhs = Vnew[0:ST, idx, jj * 128:(jj + 1) * 128]
                                    pairs.append((lhs, Pall[0:kp, ti_, jj * ST:(jj + 1) * ST]))
                                mm(PA[3][:, jj * ST:(jj + 1) * ST], pairs, ["Vc", "Vnew", "Pall"], ("pa", 3))
                            pairs = [(ones[0:kp, :], Pall[0:kp, ti_, :]) for (ti_, kp, _) in tiles]
                            mm(PA[4][:, 0:32], pairs, ["ones", "Pall"], ("pa", 4))
                            recip(osb[:, 0:32], PA[4][:, 0:32], [("pa", 4)], ["osb"])
                            tt("dve", ofs, PA[3][:, 0:32], osb[:, 0:32], ALU.mult, [("pa", 3), "osb"], ["ofs"])
                            dma1(fm(oattT)[:, :, tcol:tcol + ST], ofs.rearrange("p (j i) -> p j i", j=4), reads=["ofs"])
                    phase_end()
                    if pruned:
                        continue

                    phase_begin()
                    gain = carve(D, F32)
                    dma1(gain, nffn[l, :, :], writes=["gain"])
                    wr = carve(8 * 1024, BF16, (8, 1024))
                    wa = carve(4 * 1024, BF16, (4, 1024))
                    wo = carve(8 * 1024, BF16, (8, 1024))
                    dma1(wr, fm(wb_rbr[l]), writes=["wr"])
                    dma1(wa, fm(wb_abr[l]), writes=["wa"])
                    dma1(wo, fm(wb_out[l]), writes=["wo"])
                    NB4 = 1
                    orT = [carve(8 * 512, BF16, (8, 512)) for _ in range(NB4)]
                    oaT = [carve(4 * 512, BF16, (4, 512)) for _ in range(NB4)]
                    gaT = [carve(8 * 512, BF16, (8, 512)) for _ in range(NB4)]
                    gbT = [carve(8 * 512, BF16, (8, 512)) for _ in range(NB4)]
                    mT = [carve(8 * 512, BF16, (8, 512)) for _ in range(2)]
                    t1 = [carve(512, F32) for _ in range(2)]
                    t2 = [carve(512, F32) for _ in range(2)]
                    xts = [carve(D, F32) for _ in range(2)]
                    nb = norm_bufs()
                    ci = 0
                    ti_g = 0
                    for b in range(nblk):
                        s = b % NB4
                        cb = c0 + b * bw
                        dma1(orT[s][:, :, 0:bw], fm(oretT)[:, :, cb:cb + bw], writes=[("orT", s)])
                        dma1(oaT[s][:, :, 0:bw], fm(oattT)[:, :, cb:cb + bw], writes=[("oaT", s)])
                        dma1(gaT[s][:, :, 0:bw], fm(sgaT)[:, :, cb:cb + bw], writes=[("gaT", s)])
                        dma1(gbT[s][:, :, 0:bw], fm(sgbT)[:, :, cb:cb + bw], writes=[("gbT", s)])
                        ms = b % 2
                        for cc in range(8):
                            c2 = ci % 2
                            ci += 1
                            mm(PA[0][:, 0:bw], [(wr[:, k, cc * 128:(cc + 1) * 128], orT[s][:, k, 0:bw]) for k in range(8)], ["wr", ("orT", s)], ("pa", 0))
                            mm(PA[1][:, 0:bw], [(wa[:, k, cc * 128:(cc + 1) * 128], oaT[s][:, k, 0:bw]) for k in range(4)], ["wa", ("oaT", s)], ("pa", 1))
                            tt("dve", t1[c2][:, 0:bw], PA[0][:, 0:bw], gaT[s][:, cc, 0:bw], ALU.mult, [("pa", 0), ("gaT", s)], [("t1", c2)])
                            tt("dve", t2[c2][:, 0:bw], PA[1][:, 0:bw], gbT[s][:, cc, 0:bw], ALU.mult, [("pa", 1), ("gbT", s)], [("t2", c2)])
                            tt("pool", mT[ms][:, cc, 0:bw], t1[c2][:, 0:bw], t2[c2][:, 0:bw], ALU.add, [("t1", c2), ("t2", c2)], [("mT", ms)])
                        for tl in range(bw // 128):
                            t = b * (bw // 128) + tl
                            xs_ = ti_g % 2
                            ti_g += 1
                            r0 = c0 + t * 128
                            dma1(xts[xs_], xres[r0:r0 + 128, :], writes=[("xt", xs_)])
                            for hf in range(2):
                                pb = 2 + hf
                                mm(PA[pb], [(mT[ms][:, k, tl * 128:(tl + 1) * 128], wo[:, k, hf * 512:(hf + 1) * 512]) for k in range(8)],
                                   [("mT", ms), "wo"], ("pa", pb))
                                tt("dve", xts[xs_][:, hf * 512:(hf + 1) * 512], PA[pb], xts[xs_][:, hf * 512:(hf + 1) * 512], ALU.add,
                                   [("pa", pb), ("xt", xs_)], [("xt", xs_)])
                            dma1(xres[r0:r0 + 128, :], xts[xs_], reads=[("xt", xs_)])
                            norm_tile(xts[xs_], ("xt", xs_), gain, t * 128, nb, ti_g)
                    phase_end()

                    phase_begin()
                    wps = [carve(8 * 512, BF16, (8, 512)) for _ in range(2)]
                    rl = [carve(512, F32) for _ in range(2)]
                    us = [carve(512, BF16) for _ in range(2)]
                    k3 = 0
                    for pc in range(DFF // 512):
                        s = pc % 2
                        dma1(wps[s], fm(wb_up[l])[:, :, pc * 512:(pc + 1) * 512], writes=[("wp", s)])
                        for cc in range(4):
                            for b in range(nblk):
                                rr["ps"] = (rr["ps"] + 1) % 4
                                pb = rr["ps"]
                                mm(PA[pb][:, 0:bw], [(wps[s][:, k, cc * 128:(cc + 1) * 128], hT[:, k, b * bw:(b + 1) * bw]) for k in range(8)],
                                   [("wp", s), "hT"], ("pa", pb))
                                s3 = k3 % 2
                                k3 += 1
                                act(rl[s3][:, 0:bw], PA[pb][:, 0:bw], AF.Relu, [("pa", pb)], [("rl", s3)])
                                tt("pool", us[s3][:, 0:bw], rl[s3][:, 0:bw], rl[s3][:, 0:bw], ALU.mult, [("rl", s3)], [("us", s3)])
                                fr = pc * 512 + cc * 128
                                dma1(uT[fr:fr + 128, c0 + b * bw:c0 + (b + 1) * bw], us[s3][:, 0:bw], reads=[("us", s3)])
                    phase_end()

                    phase_begin()
                    wd = carve(32 * 1024, BF16, (32, 1024))
                    for kq in range(4):
                        dma1(wd[:, kq * 8:(kq + 1) * 8, :], fm(wb_dn[l])[:, kq * 8:(kq + 1) * 8, :], writes=["wd"])
                    wg = carve(8 * 1024, BF16, (8, 1024))
                    dma1(wg, fm(wb_pg[l]), writes=["wg"])
                    wpl = carve(2 * 1024, BF16, (2, 1024))
                    dma1(wpl, fm(wb_ple[l]), writes=["wpl"])
                    final = (l == DEPTH - 1)
                    if final:
                        gain = carve(D, F32)
                        dma1(gain, nfin[:, :], writes=["gain"])
                    hT_flat = hT.rearrange("p k n -> p (k n)")
                    uTs = [hT_flat[:, i_ * 8192:(i_ + 1) * 8192].rearrange("p (k n) -> p k n", k=32) for i_ in range(2)]
                    xts = [carve(D, F32) for _ in range(2)]
                    pts = [carve(DPLE, F32) for _ in range(2)]
                    xb = [carve(D, BF16) for _ in range(1)]
                    pb16 = [carve(DPLE, BF16) for _ in range(2)]
                    xT = [carve(8 * 128, BF16, (8, 128)) for _ in range(1)]
                    pT = [carve(2 * 128, BF16, (2, 128)) for _ in range(2)]
                    sg = [carve(512, F32) for _ in range(1)]
                    pp = [carve(512, F32) for _ in range(1)]
                    yt = [carve(D, F32) for _ in range(1)]
                    ssf = [carve(4, F32) for _ in range(2)]
                    def fd_a(t):
                        s = t % 2
                        r0 = c0 + t * 128
                        us_ = (t // 2) % 2
                        uo = (t % 2) * 128
                        if t % 2 == 0:
                            tw = min(256, Wd - t * 128)
                            dma1(uTs[us_][:, :, 0:tw], fm(uT)[:, :, r0:r0 + tw], writes=[("uTs", us_)])
                        dma1(xts[s], xres[r0:r0 + 128, :], writes=[("xt", s)])
                        if is_s:
                            memset("pool", pts[s], 0.0, [("pts", s)])
                            dma1(pts[s][0:NS * ST, :], ps_in[l, :, :], writes=[("pts", s)])
                        else:
                            dma1(pts[s], pw[l, r0:r0 + 128, :], writes=[("pts", s)])
                        for hf in range(2):
                            pbk = hf
                            mm(PA[pbk], [(uTs[us_][:, k, uo:uo + 128], wd[:, k, hf * 512:(hf + 1) * 512]) for k in range(32)], [("uTs", us_), "wd"], ("pa", pbk))
                            tt("dve", xts[s][:, hf * 512:(hf + 1) * 512], PA[pbk], xts[s][:, hf * 512:(hf + 1) * 512], ALU.add,
                               [("pa", pbk), ("xt", s)], [("xt", s)])
                    def fd_b(t):
                        s = t % 2
                        r0 = c0 + t * 128
                        copy("pool", xb[0], xts[s], [("xt", s)], [("xb", 0)])
                        copy("pool", pb16[s], pts[s], [("pts", s)], [("pb16", s)])
                        ptv = PT[0].rearrange("p (k n) -> p k n", k=8)
                        transposes([ptv[:, k, :] for k in range(8)], [xb[0][:, k * 128:(k + 1) * 128] for k in range(8)], [("xb", 0)], ("pt", 0))
                        act(xT[0], ptv, AF.Copy, [("pt", 0)], [("xT", 0)])
                        ptv1 = PT[1].rearrange("p (k n) -> p k n", k=8)
                        transposes([ptv1[:, k, :] for k in range(2)], [pb16[s][:, k * 128:(k + 1) * 128] for k in range(2)], [("pb16", s)], ("pt", 1))
                        copy("dve", pT[s], ptv1[:, 0:2, :], [("pt", 1)], [("pT", s)])
                        for hf in range(2):
                            s4 = 0
                            mm(PA[2], [(xT[0][:, k, :], wg[:, k, hf * 512:(hf + 1) * 512]) for k in range(8)], [("xT", 0), "wg"], ("pa", 2))
                            mm(PA[3], [(pT[s][:, k, :], wpl[:, k, hf * 512:(hf + 1) * 512]) for k in range(2)], [("pT", s), "wpl"], ("pa", 3))
                            act(sg[s4], PA[2], AF.Sigmoid, [("pa", 2)], [("sg", s4)])
                            tt("dve", pp[s4], PA[3], sg[s4], ALU.mult, [("pa", 3), ("sg", s4)], [("pp", s4)])
                            tt("pool", xts[s][:, hf * 512:(hf + 1) * 512], xts[s][:, hf * 512:(hf + 1) * 512], pp[s4], ALU.add,
                               [("xt", s), ("pp", s4)], [("xt", s)])
                        if not final:
                            dma1(xres[r0:r0 + 128, :], xts[s], reads=[("xt", s)])
                        else:
                            dve_ttr(yt[0], xts[s], xts[s], ssf[s][:, 0:1], [("xt", s)], [("yt", 0), ("ssf", s)])
                            act(ssf[s][:, 1:2], ssf[s][:, 0:1], AF.Sqrt, [("ssf", s), "epsb"], [("ssf", s)], bias=epsb[:, 0:1], scale=1.0 / D)
                            recip(ssf[s][:, 2:3], ssf[s][:, 1:2], [("ssf", s)], [("ssf", s)])
                            stt(yt[0], xts[s], ssf[s][:, 2:3], gain, ALU.mult, ALU.mult, [("xt", s), ("ssf", s), "gain"], [("yt", 0)])
                            if is_s:
                                dma1(ys[:, :], yt[0][0:NS * ST, :], reads=[("yt", 0)])
                            elif sb == NSB - 1:
                                dma1(y[r0 - (W - SB_W):r0 - (W - SB_W) + 128, :], yt[0], reads=[("yt", 0)])

                    fd_a(0)
                    for t in range(ntile):
                        if t + 1 < ntile:
                            fd_a(t + 1)
                        fd_b(t)
                    phase_end()
        except _Stop:
            pass

        cx.barrier(final=True)

        semnames = {}
        import contextlib
        with contextlib.ExitStack() as es:
            sems = {}
            for i, sk_ in enumerate(cx.semkeys):
                sems[sk_] = es.enter_context(nc.semaphore("s%d" % i))
            with nc.Block() as block:
                @block.tensor
                def _(e):
                    cx.replay("pe", e, sems)

                @block.scalar
                def _(e):
                    cx.replay("act", e, sems)

                @block.vector
                def _(e):
                    cx.replay("dve", e, sems)

                @block.gpsimd
                def _(e):
                    cx.replay("pool", e, sems)

                @block.sync
                def _(e):
                    cx.replay("sp", e, sems)
    return nc


_CACHE = {}


def make_in_maps(W, inputs, n_cores=8):
    consts = make_consts()
    f = lambda a: np.ascontiguousarray(np.asarray(a, dtype=np.float32))
    B = inputs["x_prompt"].shape[0]
    cores_per_b = n_cores // B
    shared = {k: f(inputs[k]) for k in ("w_in", "w_ret_br", "w_att_br", "w_out", "w_up", "w_down", "w_ple", "w_ple_gate")}
    shared["nmix"] = f(np.broadcast_to(np.asarray(inputs["norm_mix"])[:, None, :], (DEPTH, 128, D)))
    shared["nffn"] = f(np.broadcast_to(np.asarray(inputs["norm_ffn"])[:, None, :], (DEPTH, 128, D)))
    shared["nfin"] = f(np.broadcast_to(np.asarray(inputs["norm_final"])[None, :], (128, D)))
    shared.update(consts)
    maps = []
    NSB = W // SB_W
    for c in range(n_cores):
        b = c // cores_per_b
        seg = ((c % cores_per_b) * NSB) // cores_per_b
        npad = (NSB - 1 - seg) * SB_W
        m = dict(shared)
        xwin = np.zeros((W, D), np.float32)
        xwin[npad:] = np.asarray(inputs["x_prompt"])[b, :W - npad]
        pwin = np.zeros((DEPTH, W, DPLE), np.float32)
        pwin[:, npad:] = np.asarray(inputs["p_prompt"])[:, b, :W - npad]
        m["xw"] = xwin
        m["pw"] = pwin
        vo = np.zeros((128, NSB, 128), np.float32)
        vo[:, NSB - 1 - seg:, :] = 1.0
        m["vones"] = vo
        sl = slice(c * NS, (c + 1) * NS)
        m["xs"] = f(np.asarray(inputs["x_sample"])[sl].reshape(NS * ST, D))
        m["ps"] = f(np.asarray(inputs["p_sample"])[:, sl].reshape(DEPTH, NS * ST, DPLE))
        caches_k = (inputs["cache_win_k0"], inputs["cache_win_k1"], inputs["cache_win_k2"])
        caches_v = (inputs["cache_win_v0"], inputs["cache_win_v1"], inputs["cache_win_v2"])
        for g in range(3):
            L = GROUPS[g][0]
            m["ck%d" % g] = f(np.asarray(caches_k[g])[:, sl].reshape(DEPTH, NS, L, 512))
            m["cv%d" % g] = f(np.asarray(caches_v[g])[:, sl].reshape(DEPTH, NS, L, 512))
        m["st"] = f(np.asarray(inputs["state_ret"])[:, sl])
        maps.append(m)
    return maps


def assemble(W, res, B, n_cores=8):
    cores_per_b = n_cores // B
    R = res.results
    LG = [min(g[0], W) for g in GROUPS]
    NSB = W // SB_W
    y_prompt = np.zeros((B, W, D), np.float32)
    for c in range(n_cores):
        b = c // cores_per_b
        seg = ((c % cores_per_b) * NSB) // cores_per_b
        y_prompt[b, seg * SB_W:(seg + 1) * SB_W] = R[c]["y"]
    y_sample = np.concatenate([R[c]["ys"].reshape(NS, ST, D) for c in range(n_cores)]).astype(np.float32)
    outs = [y_prompt, y_sample]
    for g in range(3):
        for nm in ("pk", "pv"):
            a = np.stack([R[b * cores_per_b + cores_per_b - 1][nm + str(g)] for b in range(B)], axis=1)
            outs.append(a.reshape(DEPTH, B, LG[g], 4, 128).astype(np.float32))
    outs.append(np.stack([R[b * cores_per_b + cores_per_b - 1]["pret"] for b in range(B)], axis=1).astype(np.float32))
    for g in range(3):
        L = GROUPS[g][0]
        for nm in ("sk", "sv"):
            a = np.concatenate([R[c][nm + str(g)] for c in range(n_cores)], axis=1)
            outs.append(a.reshape(DEPTH, n_cores * NS, L, 4, 128).astype(np.float32))
    outs.append(np.concatenate([R[c]["sret"] for c in range(n_cores)], axis=1).astype(np.float32))
    return tuple(outs)


def kernel(**inputs):
    W = int(np.asarray(inputs["x_prompt"]).shape[1])
    B = int(np.asarray(inputs["x_prompt"]).shape[0])
    if W not in _CACHE:
        _CACHE[W] = build(W)
    nc = _CACHE[W]
    in_maps = make_in_maps(W, inputs)
    res = run_bass_kernel_spmd(nc, in_maps, core_ids=list(range(8)))
    return assemble(W, res, B)
```

```python
import math
import numpy as np
import concourse.bass as bass
import concourse.mybir as mybir
from concourse.bass_utils import run_bass_kernel_spmd

F32 = mybir.dt.float32
BF16 = mybir.dt.bfloat16
AF = mybir.ActivationFunctionType
ALU = mybir.AluOpType

D = 1024
DEPTH = 2
DPLE = 256
DFF = 4096
DIN = 9728
EPS = 1e-6
GROUPS = ((128, 1), (512, 4), (2048, 16))
SB_W = 2048
NS = 4
ST = 8
NDMA = 16
NSDMA = 4
import os
STOP_AFTER = int(os.environ.get("KSTOP", "0"))
KP = int(os.environ.get("KP", "0"))
KSKIP = int(os.environ.get("KSKIP", "0"))
DMASPREAD = int(os.environ.get("DMASPREAD", "0"))
PRUNE = int(os.environ.get("KPRUNE", "1"))
ENGS = ("pe", "act", "dve", "pool", "sp")

C_RQ, C_RK, C_RV, C_RG, C_AQ, C_AK, C_AV, C_GA, C_GB = 0, 512, 1024, 2048, 3072, 4608, 6144, 7680, 8704


def _lg():
    return np.log1p(-np.exp(np.linspace(math.log(1.0 / 32), math.log(1.0 / 512), 4)))


def _slopes():
    return 2.0 ** (-8.0 * (np.arange(12, dtype=np.float64) + 1.0) / 12)


def make_consts():
    lg = _lg()
    sl = _slopes()
    p = np.arange(128)
    c = {}
    c["c_ident"] = np.eye(128, dtype=np.float32)
    diff = p[None, :] - p[:, None]
    rm = np.zeros((128, 4, 128), np.float64)
    for h in range(4):
        rm[:, h, :] = np.where(diff >= 0, np.exp(lg[h] * np.maximum(diff, 0)), 0.0) * (128 ** -0.5)
    c["c_rm"] = rm.astype(np.float32)
    xi = np.zeros((128, 4, 128), np.float64)
    for h in range(4):
        xi[:, h, :] = np.exp(lg[h] * (p + 1.0))[None, :]
    c["c_xi"] = xi.astype(np.float32)
    z = np.zeros((128, 8), np.float64)
    for h in range(4):
        z[:, h] = np.exp(lg[h] * (127.0 - p)) * (128 ** -0.5)
        z[:, 4 + h] = np.exp(lg[h] * (7.0 - (p % 8))) * (128 ** -0.5)
    c["c_z"] = z.astype(np.float32)
    am = np.zeros((128, 12, 256), np.float64)
    i = np.arange(128)
    for g, (Wg, dil) in enumerate(GROUPS):
        for j in range(4):
            s = sl[g * 4 + j] * dil
            dprev = i[None, :] + 128 - p[:, None]
            ddiag = i[None, :] - p[:, None]
            am[:, g * 4 + j, 0:128] = np.where(dprev <= 128, np.exp(-s * dprev), 0.0)
            am[:, g * 4 + j, 128:256] = np.where(ddiag >= 0, np.exp(-s * np.maximum(ddiag, 0)), 0.0)
    c["c_am"] = am.astype(np.float32)
    sm = np.zeros((128, 24, 32), np.float64)
    t = 0
    for g, (Wg, dil) in enumerate(GROUPS):
        L = Wg
        ntile = L // 128
        for tt in range(ntile + 1):
            for j in range(4):
                s = sl[g * 4 + j]
                for qi in range(8):
                    if tt < ntile:
                        kpos = tt * 128 + p
                        ok = np.ones(128, bool)
                    else:
                        kpos = L + p
                        ok = p < 8
                    dist = (L + qi) - kpos
                    valid = ok & (dist >= 0) & (dist % dil == 0) & (dist // dil <= 128)
                    sm[:, t, j * 8 + qi] = np.where(valid, np.exp(-s * np.maximum(dist, 0)), 0.0)
            t += 1
    c["c_sm"] = sm.astype(np.float32)
    return c


class Ctx:
    def __init__(self, nc):
        self.nc = nc
        self.q = {e: [] for e in ENGS}
        self.semcount = {}
        self.seen = {e: {} for e in ENGS}
        self.lastw = {}
        self.readers = {}
        self.dma_rr = 0
        self.sdma_rr = 0
        self.semkeys = [("eng", e) for e in ("pe", "act", "dve", "pool")] + [("dma", i) for i in range(NDMA)] + [("sdma", i) for i in range(NSDMA)] + [("sbg", i) for i in range(NSDMA)] + [("bg", 0)]
        for s in self.semkeys:
            self.semcount[s] = 0

    def op(self, eng, fn, reads=(), writes=(), dma=0, bg=False):
        deps = {}

        def add(tok):
            if tok is None:
                return
            s, v = tok
            if deps.get(s, 0) < v:
                deps[s] = v
        for k in reads:
            add(self.lastw.get(k))
        for k in writes:
            add(self.lastw.get(k))
            for t in self.readers.get(k, ()):
                add(t)
        if bg == "s":
            d = self.sdma_rr
            self.sdma_rr = (d + 1) % NSDMA
            s = ("sbg", d)
            if self.semcount[s] > 0:
                add((s, self.semcount[s]))
            self.semcount[s] += 16 * dma
            tok = (s, self.semcount[s])
        elif bg:
            s = ("bg", 0)
            self.semcount[s] += 16 * dma
            tok = (s, self.semcount[s])
        elif dma:
            if eng == "pool":
                d = self.sdma_rr
                self.sdma_rr = (d + 1) % NSDMA
                s = ("sdma", d)
            else:
                d = self.dma_rr
                self.dma_rr = (d + 1) % NDMA
                s = ("dma", d)
            if self.semcount[s] > 0:
                add((s, self.semcount[s]))
            self.semcount[s] += 16 * dma
            tok = (s, self.semcount[s])
        else:
            s = ("eng", eng)
            self.semcount[s] += 1
            tok = (s, self.semcount[s])
        waits = []
        for sk, v in deps.items():
            if sk == ("eng", "pe") and eng == "pe" and not dma:
                continue
            if self.seen[eng].get(sk, 0) >= v:
                continue
            self.seen[eng][sk] = v
            waits.append((sk, v))
        self.q[eng].append((waits, fn, tok[0], dma))
        for k in reads:
            self.readers.setdefault(k, []).append(tok)
        for k in writes:
            self.lastw[k] = tok
            self.readers[k] = []
        return tok

    def barrier(self, final=False, sbg=False):
        for e in ENGS:
            waits = []
            for sk in self.semkeys:
                if sk == ("bg", 0) and not final:
                    continue
                if sk[0] == "sbg" and not (final or sbg):
                    continue
                v = self.semcount[sk]
                if v > 0 and self.seen[e].get(sk, 0) < v:
                    self.seen[e][sk] = v
                    waits.append((sk, v))
            self.q[e].append((waits, None, None, 0))
        self.lastw = {}
        self.readers = {}

    def replay(self, eng, e, sems):
        for waits, fn, s, dma in self.q[eng]:
            for (sk, v) in waits:
                e.wait_ge(sems[sk], v)
            if fn is None:
                continue
            r = fn(e)
            if dma:
                assert len(r) == dma, (len(r), dma)
                for ins in r:
                    ins.then_inc(sems[s], 16)
            else:
                ins = r[-1] if isinstance(r, (list, tuple)) else r
                ins.then_inc(sems[s], 1)


def build(W):
    NSB = W // SB_W
    NT = W // 128
    TT = NT + 1
    NTOK = TT * 128
    LG = [min(g[0], W) for g in GROUPS]
    lg = _lg()
    dec128 = [float(np.exp(lg[h] * 128)) for h in range(4)]
    dec8 = [float(np.exp(lg[h] * 8)) for h in range(4)]

    nc = bass.Bass("TRN2", target_bir_lowering=False)

    def din(name, shape, dt=F32):
        return nc.dram_tensor(name, list(shape), dt, kind="ExternalInput").ap()

    def dout(name, shape):
        return nc.dram_tensor(name, list(shape), F32, kind="ExternalOutput").ap()

    def dscr(name, shape, dt=BF16):
        return nc.dram_tensor(name, list(shape), dt, kind="Internal").ap()

    xw = din("xw", [W, D]); xs = din("xs", [NS * ST, D])
    pw = din("pw", [DEPTH, W, DPLE]); ps_in = din("ps", [DEPTH, NS * ST, DPLE])
    ck = [din("ck%d" % g, [DEPTH, NS, GROUPS[g][0], 512]) for g in range(3)]
    cv = [din("cv%d" % g, [DEPTH, NS, GROUPS[g][0], 512]) for g in range(3)]
    st_in = din("st", [DEPTH, NS, 4, 128, 256])
    w_in = din("w_in", [DEPTH, D, DIN]); w_rbr = din("w_ret_br", [DEPTH, 1024, D]); w_abr = din("w_att_br", [DEPTH, 512, D])
    w_out = din("w_out", [DEPTH, D, D]); w_up = din("w_up", [DEPTH, D, DFF]); w_dn = din("w_down", [DEPTH, DFF, D])
    w_ple = din("w_ple", [DEPTH, DPLE, D]); w_pg = din("w_ple_gate", [DEPTH, D, D])
    nmix = din("nmix", [DEPTH, 128, D]); nffn = din("nffn", [DEPTH, 128, D]); nfin = din("nfin", [128, D])
    c_ident = din("c_ident", [128, 128]); c_rm = din("c_rm", [128, 4, 128]); c_xi = din("c_xi", [128, 4, 128])
    c_z = din("c_z", [128, 8]); c_am = din("c_am", [128, 12, 256]); c_sm = din("c_sm", [128, 24, 32])
    vones_in = din("vones", [128, NSB, 128])

    y = dout("y", [SB_W, D]); ys = dout("ys", [NS * ST, D])
    pk = [dout("pk%d" % g, [DEPTH, LG[g], 512]) for g in range(3)]
    pv = [dout("pv%d" % g, [DEPTH, LG[g], 512]) for g in range(3)]
    pret = dout("pret", [DEPTH, 4, 128, 256])
    sk = [dout("sk%d" % g, [DEPTH, NS, GROUPS[g][0], 512]) for g in range(3)]
    sv = [dout("sv%d" % g, [DEPTH, NS, GROUPS[g][0], 512]) for g in range(3)]
    sret = dout("sret", [DEPTH, NS, 4, 128, 256])

    wb_in = dscr("wb_in", [DEPTH, D, DIN]); wb_rbr = dscr("wb_rbr", [DEPTH, 1024, D]); wb_abr = dscr("wb_abr", [DEPTH, 512, D])
    wb_out = dscr("wb_out", [DEPTH, D, D]); wb_up = dscr("wb_up", [DEPTH, D, DFF]); wb_dn = dscr("wb_dn", [DEPTH, DFF, D])
    wb_ple = dscr("wb_ple", [DEPTH, DPLE, D]); wb_pg = dscr("wb_pg", [DEPTH, D, D])
    xres = dscr("xres", [NTOK, D], F32)
    rqT = dscr("rqT", [512, NTOK]); rkT = dscr("rkT", [512, NTOK])
    rv_d = dscr("rv", [NTOK, 1024]); kz_d = dscr("kz", [NTOK, 512]); srg_d = dscr("srg", [NTOK, 1024])
    aqT = [dscr("aqT%d" % g, [512, NTOK]) for g in range(3)]
    akT = [dscr("akT%d" % g, [512, NTOK]) for g in range(3)]
    av_d = [dscr("av%d" % g, [NTOK, 512]) for g in range(3)]
    sgaT = dscr("sgaT", [1024, NTOK]); sgbT = dscr("sgbT", [1024, NTOK])
    oretT = dscr("oretT", [1024, NTOK]); oattT = dscr("oattT", [512, NTOK])
    uT = dscr("uT", [DFF, NTOK])

    def fm(t):
        return t.rearrange("(k p) n -> p k n", p=128)

    cx = Ctx(nc)
    ARENA = 47 * 1024

    with (
        nc.sbuf_tensor("arena", [128, ARENA], F32) as arena,
        nc.psum_tensor("pab0", [128, 1024], F32) as pab0, nc.psum_tensor("pab1", [128, 1024], F32) as pab1,
        nc.psum_tensor("pa4", [128, 512], F32) as pa4, nc.psum_tensor("pa5", [128, 512], F32) as pa5,
        nc.psum_tensor("pt0", [128, 1024], BF16) as pt0, nc.psum_tensor("pt1", [128, 1024], BF16) as pt1,
    ):
        arena_ap = arena[:, :]
        PAB = [pab0[:, :], pab1[:, :]]
        PA = [PAB[0][:, 0:512], PAB[0][:, 512:1024], PAB[1][:, 0:512], PAB[1][:, 512:1024], pa4[:, :], pa5[:, :]]
        PT = [p_[:, :] for p_ in (pt0, pt1)]
        state = {"off": 0, "base": 0}

        def carve(n_elems, dt=F32, shape=None):
            n32 = (n_elems + 1) // 2 if dt == BF16 else n_elems
            n32 = (n32 + 7) // 8 * 8
            off = state["off"]
            assert off + n32 <= ARENA, ("SBUF arena overflow", off, n32)
            state["off"] = off + n32
            v = arena_ap[:, off:off + n32]
            if dt == BF16:
                v = v.bitcast(BF16)[:, 0:n_elems]
            else:
                v = v[:, 0:n_elems]
            if shape is not None:
                names = " ".join("d%d" % i for i in range(len(shape)))
                kw = {"d%d" % i: s for i, s in enumerate(shape)}
                v = v.rearrange("p (%s) -> p %s" % (names, names), **kw)
            return v

        def phase_begin():
            state["off"] = state["base"]

        class _Stop(Exception):
            pass

        def phase_end():
            cx.barrier()
            state["nph"] = state.get("nph", 0) + 1
            if STOP_AFTER and state["nph"] >= STOP_AFTER and state.get("main"):
                raise _Stop()

        ident = carve(128, BF16)
        ones = carve(128, BF16)
        RM = carve(512, F32, (4, 128))
        XI = carve(512, F32, (4, 128))
        ZC = carve(8, F32)
        AM = carve(12 * 256, F32, (12, 256))
        SM = carve(24 * 32, F32, (24, 32))
        hT = carve(8 * SB_W, BF16, (8, SB_W))
        Rf = carve(4 * 256, F32, (4, 256))
        Rb = carve(4 * 256, BF16, (4, 256))
        epsb = carve(1, F32)
        zt16 = carve(8 * 128, BF16, (8, 128))
        vones = carve(NSB * 128, BF16, (NSB, 128))
        state["base"] = state["off"]

        rr = {"ps": 0, "ev": 0}

        def dma(fn, reads=(), writes=(), n=1, eng="sp", bg=False):
            if eng == "sp" and DMASPREAD:
                rr["dq"] = (rr.get("dq", 0) + 1) % 2
                eng = ("sp", "act")[rr["dq"]]
            return cx.op(eng, fn, reads=reads, writes=writes, dma=n, bg=bg)

        def dma1(out, in_, reads=(), writes=(), eng="sp", bg=False):
            return dma(lambda e: [e.dma_start(out=out, in_=in_)], reads, writes, 1, eng, bg)

        def mm(ps_ap, pairs, reads, ps_key):
            def fn(e):
                r = None
                n = len(pairs)
                for i, (a, b) in enumerate(pairs):
                    r = e.matmul(ps_ap, a, b, start=(i == 0), stop=(i == n - 1))
                return r
            return cx.op("pe", fn, reads=reads, writes=[ps_key])

        def transposes(pt_ap_list, in_list, reads, ps_key):
            def fn(e):
                r = None
                for o, i_ in zip(pt_ap_list, in_list):
                    r = e.transpose(o, i_, ident[0:i_.shape[0], 0:i_.shape[0]])
                return r
            return cx.op("pe", fn, reads=list(reads) + ["ident"], writes=[ps_key])

        def act(out, in_, func, reads, writes, bias=None, scale=None):
            kw = {}
            if bias is not None:
                kw["bias"] = bias
            if scale is not None:
                kw["scale"] = scale
            return cx.op("act", lambda e: e.activation(out=out, in_=in_, func=func, **kw), reads=reads, writes=writes)

        def tt(eng, out, in0, in1, op, reads, writes):
            return cx.op(eng, lambda e: e.tensor_tensor(out=out, in0=in0, in1=in1, op=op), reads=reads, writes=writes)

        def evac_copy(out, in_, reads, writes, force=None):
            rr["ev"] ^= 1
            if force == "dve":
                rr["ev"] = 0
            if rr["ev"]:
                return act(out, in_, AF.Copy, reads, writes)
            return cx.op("dve", lambda e: e.tensor_copy(out=out, in_=in_), reads=reads, writes=writes)


        def dve_ttr(out, in0, in1, accum, reads, writes):
            act(out, in0, AF.Square, reads, [writes[0]])
            return cx.op("dve", lambda e: e.tensor_reduce(out=accum, in_=out, axis=mybir.AxisListType.X, op=ALU.add),
                         reads=[writes[0]], writes=list(writes[1:]))

        def recip(out, in_, reads, writes):
            return cx.op("dve", lambda e: e.reciprocal(out=out, in_=in_), reads=reads, writes=writes)

        def stt(out, in0, scalar, in1, op0, op1, reads, writes):
            return cx.op("dve", lambda e: e.scalar_tensor_tensor(out=out, in0=in0, scalar=scalar, in1=in1, op0=op0, op1=op1),
                         reads=reads, writes=writes)

        def ts(eng, out, in0, s1, s2, op0, op1, reads, writes):
            if s2 is None:
                return cx.op(eng, lambda e: e.tensor_scalar(out=out, in0=in0, scalar1=s1, scalar2=None, op0=op0), reads=reads, writes=writes)
            return cx.op(eng, lambda e: e.tensor_scalar(out=out, in0=in0, scalar1=s1, scalar2=s2, op0=op0, op1=op1), reads=reads, writes=writes)

        def copy(eng, out, in_, reads, writes):
            return cx.op(eng, lambda e: e.tensor_copy(out=out, in_=in_), reads=reads, writes=writes)

        def memset(eng, out, val, writes):
            return cx.op(eng, lambda e: e.memset(out, val), writes=writes)

        def bnstats(out, in_, reads, writes):
            return cx.op("dve", lambda e: e.bn_stats(out=out, in_=in_), reads=reads, writes=writes)

        def bnaggr(out, in_, reads, writes):
            return cx.op("dve", lambda e: e.bn_aggr(out=out, in_=in_), reads=reads, writes=writes)

        def dman(pairs, reads=(), writes=(), eng="sp"):
            pairs = list(pairs)
            return dma(lambda e: [e.dma_start(out=o, in_=i) for (o, i) in pairs], reads, writes, len(pairs), eng)

        phase_begin()
        tmpc = carve(128, F32)
        dma1(tmpc, c_ident[:, :], writes=["tmpc"])
        copy("dve", ident, tmpc, ["tmpc"], ["ident"])
        memset("dve", ones, 1.0, ["ones"])
        memset("dve", epsb, EPS, ["epsb"])
        memset("dve", zt16, 0.0, ["zt16"])
        vtmp = carve(NSB * 128, F32, (NSB, 128))
        dma1(vtmp, vones_in[:, :, :], writes=["vtmp"])
        copy("dve", vones, vtmp, ["vtmp"], ["vones"])
        dma1(RM, c_rm[:, :, :], writes=["RM"])
        dma1(XI, c_xi[:, :, :], writes=["XI"])
        dma1(ZC, c_z[:, :], writes=["ZC"])
        dma1(AM, c_am[:, :, :], writes=["AM"])
        dma1(SM, c_sm[:, :, :], writes=["SM"])
        late_casts = []
        for (src, dst, rows) in ((w_in, wb_in, D), (w_rbr, wb_rbr, 1024), (w_abr, wb_abr, 512), (w_out, wb_out, D),
                                 (w_up, wb_up, D), (w_dn, wb_dn, DFF), (w_ple, wb_ple, DPLE), (w_pg, wb_pg, D)):
            for l in range(DEPTH):
                for r0 in range(0, rows, 128):
                    first = (dst is wb_in and l == 0)
                    if first:
                        dma1(dst[l, r0:r0 + 128, :], src[l, r0:r0 + 128, :], eng="pool")
                    else:
                        late_casts.append((dst[l, r0:r0 + 128, :], src[l, r0:r0 + 128, :]))
        for r0 in range(0, W, 1024):
            dma1(xres[r0:r0 + 1024, :], xw[r0:r0 + 1024, :])
        ztile = carve(D, F32)
        memset("dve", ztile, 0.0, ["ztile"])
        dma1(xres[W:W + 128, :], ztile, reads=["ztile"], writes=["xres_s"])
        dma1(xres[W:W + NS * ST, :], xs[:, :], writes=["xres_s"])
        for l in range(DEPTH):
            for g in range(3):
                L = GROUPS[g][0]
                for (src, dst) in ((ck[g], sk[g]), (cv[g], sv[g])):
                    for q in range(NS):
                        dma1(dst[l, q, 0:L - ST, :], src[l, q, ST:L, :], bg=True)
        phase_end()
        for (o_, i_) in late_casts:
            dma1(o_, i_, eng="pool", bg="s")

        def norm_tile(xt, xkey, gain, col0, bufs, i):
            junk, ss, hb = bufs
            s = i % 2
            dve_ttr(junk[s], xt, xt, ss[s][:, 0:1], [xkey], [("junk", s), ("ss", s)])
            act(ss[s][:, 1:2], ss[s][:, 0:1], AF.Sqrt, [("ss", s), "epsb"], [("ss", s)], bias=epsb[:, 0:1], scale=1.0 / D)
            recip(ss[s][:, 2:3], ss[s][:, 1:2], [("ss", s)], [("ss", s)])
            stt(hb[s], xt, ss[s][:, 2:3], gain, ALU.mult, ALU.mult, [xkey, ("ss", s), "gain"], [("hb", s)])
            ptv = PT[s].rearrange("p (k n) -> p k n", k=8)
            transposes([ptv[:, k, :] for k in range(8)], [hb[s][:, k * 128:(k + 1) * 128] for k in range(8)],
                       [("hb", s)], ("pt", s))
            act(hT[:, :, col0:col0 + 128], ptv, AF.Copy, [("pt", s)], ["hT"])

        def norm_bufs():
            junk = [carve(D, F32) for _ in range(2)]
            ss = [carve(4, F32) for _ in range(2)]
            hb = [carve(D, BF16) for _ in range(2)]
            return junk, ss, hb

        state["main"] = True
        try:
            SBS = [(sb, sb * SB_W, SB_W, False) for sb in range(NSB)] + [(NSB, W, 128, True)]

            for l in range(DEPTH):
                memset("dve", Rf, 0.0, ["Rf"])
                memset("dve", Rb, 0.0, ["Rb"])
                for (sb, c0, Wd, is_s) in SBS:
                    ntile = Wd // 128
                    nblk = max(1, Wd // 512)
                    bw = min(512, Wd)
                    last_prompt = (not is_s) and sb == NSB - 1
                    pruned = PRUNE and (l == DEPTH - 1) and (not is_s) and sb < NSB - 1
                    need_halo_kv = pruned and sb == NSB - 2

                    phase_begin()
                    gain = carve(D, F32)
                    dma1(gain, nmix[l, :, :], writes=["gain"])
                    xts = [carve(D, F32) for _ in range(2)]
                    nb = norm_bufs()
                    for t in range(ntile):
                        s = t % 2
                        dma1(xts[s], xres[c0 + t * 128:c0 + (t + 1) * 128, :], writes=[("xt", s)])
                        norm_tile(xts[s], ("xt", s), gain, t * 128, nb, t)
                    phase_end()

                    phase_begin()
                    wps = [carve(8 * 512, BF16, (8, 512)) for _ in range(2)]
                    stg = [carve(512, BF16) for _ in range(3)]
                    stgf = [carve(512, F32) for _ in range(2)]
                    stgB = [carve(SB_W, BF16) for _ in range(2)]
                    cnt = {"w": 0, "s": 0, "f": 0, "B": 0}

                    def load_w(col0):
                        if KP and cnt["w"] >= KP:
                            raise _Stop()
                        s = cnt["w"] % 2
                        cnt["w"] += 1
                        dma1(wps[s], fm(wb_in[l])[:, :, col0:col0 + 512], writes=[("wp", s)])
                        return wps[s], ("wp", s)

                    def psum_next():
                        rr["ps"] = (rr["ps"] + 1) % 4
                        return PA[rr["ps"]], ("pa", rr["ps"])

                    def fm_piece(wp, wkey, dst, func, dil):
                        dd = 1 if is_s else dil
                        upb = bw // dd
                        for cc in range(4):
                            sB = cnt["B"] % 2
                            cnt["B"] += 1
                            stv = stgB[sB][:, 0:Wd].rearrange("p (r u) -> p r u", r=dd)
                            for b in range(nblk):
                                pap, pkey = psum_next()
                                mm(pap[:, 0:bw], [(wp[:, k, cc * 128:(cc + 1) * 128], hT[:, k, b * bw:(b + 1) * bw]) for k in range(8)],
                                   [wkey, "hT"], pkey)
                                src_ = pap[:, 0:bw].rearrange("p (u r) -> p r u", r=dd)
                                dstv = stv[:, :, b * upb:(b + 1) * upb]
                                if func is None:
                                    evac_copy(dstv, src_, [pkey], [("stgB", sB, b)])
                                else:
                                    act(dstv, src_, func, [pkey], [("stgB", sB, b)])
                            dma1(dst[cc * 128:(cc + 1) * 128, c0:c0 + Wd], stgB[sB][:, 0:Wd], reads=[("stgB", sB, b) for b in range(nblk)])

                    def tm_cols(t, dil):
                        if is_s or dil == 1:
                            return slice(t * 128, (t + 1) * 128)
                        per = 16 // dil
                        r, c = t // per, t % per
                        start = r + dil * 128 * c
                        return slice(start, start + dil * 127 + 1, dil)

                    def tm_piece(wp, wkey, dst, dcol0, func, dil, zscale=False, outs=None):
                        for t in range(ntile):
                            pap, pkey = psum_next()
                            cs = tm_cols(t, dil)
                            mm(pap, [(hT[:, k, cs], wp[:, k, :]) for k in range(8)], [wkey, "hT"], pkey)
                            if dst is not None and not (KSKIP & 1 and cnt["w"] == 9):
                                s = cnt["s"] % 3
                                cnt["s"] += 1
                                if zscale:
                                    zo = 4 if is_s else 0
                                    for h in range(4):
                                        ts("dve", stg[s][:, h * 128:(h + 1) * 128], pap[:, h * 128:(h + 1) * 128], ZC[:, zo + h:zo + h + 1], None,
                                           ALU.mult, None, [pkey, "ZC"], [("stg", s)])
                                elif func is None:
                                    evac_copy(stg[s], pap, [pkey], [("stg", s)], force=("dve" if outs is not None else None))
                                else:
                                    act(stg[s], pap, func, [pkey], [("stg", s)])
                                dma1(dst[c0 + t * 128:c0 + (t + 1) * 128, dcol0:dcol0 + 512], stg[s], reads=[("stg", s)])
                            if outs is not None and not (KSKIP & 2 and cnt["w"] == 9):
                                outs(t, pap, pkey)

                    def out_window(g, dst_p, dst_s):
                        L = LG[g]

                        def f(t, pap, pkey):
                            if is_s:
                                s = cnt["f"] % 2
                                cnt["f"] += 1
                                evac_copy(stgf[s], pap, [pkey], [("stgf", s)], force="dve")
                                Ls = GROUPS[g][0]
                                dman([(dst_s[l, q, Ls - ST:Ls, :], stgf[s][q * ST:(q + 1) * ST, :]) for q in range(NS)], reads=[("stgf", s)])
                            elif last_prompt:
                                tok0 = c0 + t * 128
                                if tok0 >= W - L:
                                    s = cnt["f"] % 2
                                    cnt["f"] += 1
                                    evac_copy(stgf[s], pap, [pkey], [("stgf", s)], force="dve")
                                    r0 = tok0 - (W - L)
                                    dma1(dst_p[l, r0:r0 + 128, :], stgf[s], reads=[("stgf", s)])
                        return f

                    need_out = is_s or last_prompt
                    if not pruned:
                        wp, wk = load_w(C_RQ); fm_piece(wp, wk, rqT, None, 1)
                    wp, wk = load_w(C_RK)
                    if not pruned:
                        fm_piece(wp, wk, rkT, None, 1)
                    tm_piece(wp, wk, kz_d, 0, None, 1, zscale=True)
                    for hlf in range(2):
                        wp, wk = load_w(C_RV + hlf * 512); tm_piece(wp, wk, rv_d, hlf * 512, None, 1)
                    for hlf in range(2):
                        if pruned:
                            break
                        wp, wk = load_w(C_RG + hlf * 512); tm_piece(wp, wk, srg_d, hlf * 512, AF.Silu, 1)
                    for g in range(3):
                        if pruned and not need_halo_kv:
                            break
                        dil = GROUPS[g][1]
                        if not pruned:
                            wp, wk = load_w(C_AQ + g * 512); fm_piece(wp, wk, aqT[g], None, dil)
                        wp, wk = load_w(C_AK + g * 512); fm_piece(wp, wk, akT[g], None, dil)
                        if need_out:
                            tm_piece(wp, wk, None, 0, None, 1, outs=out_window(g, pk[g], sk[g]))
                        wp, wk = load_w(C_AV + g * 512)
                        if dil == 1 or is_s:
                            tm_piece(wp, wk, av_d[g], 0, None, 1, outs=out_window(g, pv[g], sv[g]) if need_out else None)
                        else:
                            tm_piece(wp, wk, av_d[g], 0, None, dil)
                            if need_out:
                                tm_piece(wp, wk, None, 0, None, 1, outs=out_window(g, pv[g], sv[g]))
                    for hlf in range(2):
                        if pruned:
                            break
                        wp, wk = load_w(C_GA + hlf * 512); fm_piece(wp, wk, sgaT[hlf * 512:(hlf + 1) * 512, :], AF.Sigmoid, 1)
                    for hlf in range(2):
                        if pruned:
                            break
                        wp, wk = load_w(C_GB + hlf * 512); fm_piece(wp, wk, sgbT[hlf * 512:(hlf + 1) * 512, :], AF.Sigmoid, 1)
                    phase_end()

                    if l == 0 and sb == 0:
                        cx.barrier(sbg=True)
                    phase_begin()
                    NB2 = 2
                    qTc = [carve(512, BF16, (4, 128)) for _ in range(NB2)]
                    kTc = [carve(512, BF16, (4, 128)) for _ in range(NB2)]
                    vc = [carve(1024, BF16) for _ in range(NB2)]
                    kzc = [carve(512, BF16) for _ in range(NB2)]
                    srgc = [carve(1024, BF16) for _ in range(NB2)]
                    Sm = [carve(128, BF16) for _ in range(2)]
                    qx = [carve(128, BF16) for _ in range(2)]
                    if not is_s and not pruned:
                        Sm4 = [carve(512, BF16) for _ in range(2)]
                        qx4 = [carve(512, BF16) for _ in range(2)]
                        onb4 = carve(1024, F32)
                        rtmp = carve(1024, F32, (4, 256))
                        DEC = carve(1024, F32, (4, 256))
                        st4 = [carve(4 * 6, F32, (4, 6)) for _ in range(2)]
                        mv4 = [carve(4 * 2 + 12, F32) for _ in range(2)]
                        for h_ in range(4):
                            memset("pool", DEC[:, h_, :], dec128[h_], ["DEC"])
                    lnst = [carve(16, F32) for _ in range(2)]
                    onb = [carve(256, F32) for _ in range(2)]
                    og = [carve(1024, BF16) for _ in range(2)]
                    oTs = [carve(8 * 128, BF16, (8, 128)) for _ in range(2)]
                    if is_s:
                        Rfs = carve(4 * 256, F32, (4, 256))
                        Rbs = carve(4 * 256, BF16, (4, 256))
                    nch = NS if is_s else ntile
                    cw = ST if is_s else 128
                    if is_s:
                        dma1(fm(oretT)[:, :, c0:c0 + 128], zt16, reads=["zt16"], writes=["oretT_s"])
                        dma1(fm(oattT)[:, :, c0:c0 + 128], zt16[:, 0:4, :], reads=["zt16"], writes=["oattT_s"])
                    it = 0
                    for n in range(nch):
                        s = n % NB2
                        tc0 = c0 + n * cw
                        dma1(vc[s][0:cw, :], rv_d[tc0:tc0 + cw, :], writes=[("vc", s)])
                        dma1(kzc[s][0:cw, :], kz_d[tc0:tc0 + cw, :], writes=[("kzc", s)])
                        if pruned:
                            for h in range(4):
                                i2 = it % 2
                                it += 1
                                mm(PA[4 + i2][:, 0:256], [(kzc[s][0:cw, h * 128:(h + 1) * 128], vc[s][0:cw, h * 256:(h + 1) * 256])],
                                   [("kzc", s), ("vc", s)], ("pa", 4 + i2))
                                stt(Rf[:, h, :], Rf[:, h, :], dec128[h], PA[4 + i2][:, 0:256], ALU.mult, ALU.add, [("pa", 4 + i2), "Rf"], ["Rf"])
                                if n == nch - 1:
                                    act(Rb[:, h, :], Rf[:, h, :], AF.Copy, ["Rf"], ["Rb"])
                            continue
                        dma1(qTc[s][:, :, 0:cw], fm(rqT)[:, :, tc0:tc0 + cw], writes=[("qTc", s)])
                        dma1(kTc[s][:, :, 0:cw], fm(rkT)[:, :, tc0:tc0 + cw], writes=[("kTc", s)])
                        dma1(srgc[s][0:cw, :], srg_d[tc0:tc0 + cw, :], writes=[("srgc", s)])
                        if is_s:
                            dma1(Rfs, st_in[l, n].rearrange("h p v -> p h v"), writes=["Rfs"])
                            copy("pool", Rbs, Rfs, ["Rfs"], ["Rbs"])
                            RF, RB, rfk, rbk, dec = Rfs, Rbs, "Rfs", "Rbs", dec8
                        else:
                            RF, RB, rfk, rbk, dec = Rf, Rb, "Rf", "Rb", dec128
                        if not is_s:
                            i2 = n % 2
                            SPS = PA[i2]
                            for h in range(4):
                                mm(SPS[:, h * 128:(h + 1) * 128], [(kTc[s][:, h, :], qTc[s][:, h, :])], [("kTc", s), ("qTc", s)], ("pa", i2))
                            tt("dve", Sm4[i2], SPS, RM.rearrange("p h i -> p (h i)"), ALU.mult, [("pa", i2), "RM"], [("Sm4", i2)])
                            tt("pool", qx4[i2], qTc[s].rearrange("p h i -> p (h i)"), XI.rearrange("p h i -> p (h i)"), ALU.mult,
                               [("qTc", s), "XI"], [("qx4", i2)])
                            OPS = PAB[1]
                            for h in range(4):
                                mm(OPS[:, h * 256:(h + 1) * 256], [(Sm4[i2][:, h * 128:(h + 1) * 128], vc[s][:, h * 256:(h + 1) * 256]),
                                                                   (qx4[i2][:, h * 128:(h + 1) * 128], Rb[:, h, :])],
                                   [("Sm4", i2), ("vc", s), ("qx4", i2), "Rb"], ("pab", 1))
                            for h in range(4):
                                ub_ = PA[4 + h // 2]
                                mm(ub_[:, (h % 2) * 256:(h % 2 + 1) * 256], [(kzc[s][:, h * 128:(h + 1) * 128], vc[s][:, h * 256:(h + 1) * 256])],
                                   [("kzc", s), ("vc", s)], ("pa", 4 + h // 2))
                            tt("pool", rtmp.rearrange("p h v -> p (h v)"), Rf.rearrange("p h v -> p (h v)"), DEC.rearrange("p h v -> p (h v)"), ALU.mult,
                               ["Rf", "DEC"], ["rtmp"])
                            for hp in range(2):
                                tt("dve", Rf[:, 2 * hp:2 * hp + 2, :].rearrange("p h v -> p (h v)"), PA[4 + hp],
                                   rtmp[:, 2 * hp:2 * hp + 2, :].rearrange("p h v -> p (h v)"), ALU.add, [("pa", 4 + hp), "rtmp"], ["Rf"])
                            act(Rb.rearrange("p h v -> p (h v)"), Rf.rearrange("p h v -> p (h v)"), AF.Copy, ["Rf"], ["Rb"])
                            stv, mvv = st4[i2], mv4[i2]
                            for h in range(4):
                                bnstats(stv[:, h, :], OPS[:, h * 256:(h + 1) * 256], [("pab", 1)], [("ln4", i2)])
                            for h in range(4):
                                bnaggr(mvv[:, 2 * h:2 * h + 2], stv[:, h, :], [("ln4", i2)], [("ln4", i2)])
                            mvh = mvv[:, 0:8].rearrange("p (h t) -> p h t", t=2)
                            act(mvv[:, 8:12], mvh[:, :, 1], AF.Sqrt, [("ln4", i2), "epsb"], [("ln4", i2)], bias=epsb[:, 0:1], scale=1.0)
                            recip(mvv[:, 12:16], mvv[:, 8:12], [("ln4", i2)], [("ln4", i2)])
                            tt("dve", mvv[:, 16:20], mvh[:, :, 0], mvv[:, 12:16], ALU.mult, [("ln4", i2)], [("ln4", i2)])
                            ts("dve", mvv[:, 16:20], mvv[:, 16:20], -1.0, None, ALU.mult, None, [("ln4", i2)], [("ln4", i2)])
                            for h in range(4):
                                act(onb4[:, h * 256:(h + 1) * 256], OPS[:, h * 256:(h + 1) * 256], AF.Identity, [("pab", 1), ("ln4", i2)], ["onb4"],
                                    bias=mvv[:, 16 + h:17 + h], scale=mvv[:, 12 + h:13 + h])
                            os_ = n % 2
                            tt("pool", og[os_], onb4, srgc[s], ALU.mult, ["onb4", ("srgc", s)], [("og", os_)])
                        for h in (range(4) if is_s else []):
                            i2 = it % 2
                            it += 1
                            mm(PA[i2][0:cw, 0:cw], [(kTc[s][:, h, 0:cw], qTc[s][:, h, 0:cw])], [("kTc", s), ("qTc", s)], ("pa", i2))
                            tt("dve", Sm[i2][0:cw, 0:cw], PA[i2][0:cw, 0:cw], RM[0:cw, h, 0:cw], ALU.mult, [("pa", i2), "RM"], [("Sm", i2)])
                            tt("pool", qx[i2][:, 0:cw], qTc[s][:, h, 0:cw], XI[:, h, 0:cw], ALU.mult, [("qTc", s), "XI"], [("qx", i2)])
                            mm(PA[2 + i2][0:cw, 0:256], [(Sm[i2][0:cw, 0:cw], vc[s][0:cw, h * 256:(h + 1) * 256]),
                                                     (qx[i2][:, 0:cw], RB[:, h, :])],
                               [("Sm", i2), ("vc", s), ("qx", i2), rbk], ("pa", 2 + i2))
                            mm(PA[4 + i2][:, 0:256], [(kzc[s][0:cw, h * 128:(h + 1) * 128], vc[s][0:cw, h * 256:(h + 1) * 256])],
                               [("kzc", s), ("vc", s)], ("pa", 4 + i2))
                            dh = dec[h]
                            stt(RF[:, h, :], RF[:, h, :], dh, PA[4 + i2][:, 0:256], ALU.mult, ALU.add, [("pa", 4 + i2), rfk], [rfk])
                            act(RB[:, h, :], RF[:, h, :], AF.Copy, [rfk], [rbk])
                            ls = lnst[i2]
                            bnstats(ls[0:cw, 0:6], PA[2 + i2][0:cw, 0:256], [("pa", 2 + i2)], [("ln", i2)])
                            bnaggr(ls[0:cw, 6:8], ls[0:cw, 0:6], [("ln", i2)], [("ln", i2)])
                            act(ls[0:cw, 8:9], ls[0:cw, 7:8], AF.Sqrt, [("ln", i2), "epsb"], [("ln", i2)], bias=epsb[0:cw, 0:1], scale=1.0)
                            recip(ls[0:cw, 9:10], ls[0:cw, 8:9], [("ln", i2)], [("ln", i2)])
                            ts("dve", ls[0:cw, 10:11], ls[0:cw, 6:7], ls[0:cw, 9:10], -1.0, ALU.mult, ALU.mult, [("ln", i2)], [("ln", i2)])
                            act(onb[i2][0:cw, :], PA[2 + i2][0:cw, 0:256], AF.Identity, [("pa", 2 + i2), ("ln", i2)], [("onb", i2)],
                                bias=ls[0:cw, 10:11], scale=ls[0:cw, 9:10])
                            os_ = n % 2
                            tt("pool", og[os_][0:cw, h * 256:(h + 1) * 256], onb[i2][0:cw, :], srgc[s][0:cw, h * 256:(h + 1) * 256], ALU.mult,
                               [("onb", i2), ("srgc", s)], [("og", os_)])
                        os_ = n % 2
                        ptv = PT[os_].rearrange("p (k n) -> p k n", k=8)
                        transposes([ptv[:, k, 0:cw] for k in range(8)], [og[os_][0:cw, k * 128:(k + 1) * 128] for k in range(8)],
                                   [("og", os_)], ("pt", os_))
                        act(oTs[os_][:, :, 0:cw], ptv[:, :, 0:cw], AF.Copy, [("pt", os_)], [("oTs", os_)])
                        dma1(fm(oretT)[:, :, tc0:tc0 + cw], oTs[os_][:, :, 0:cw], reads=[("oTs", os_)], writes=(["oretT_s"] if is_s else []))
                        if is_s:
                            dma1(sret[l, n].rearrange("h p v -> p h v"), Rfs, reads=["Rfs"])
                    if last_prompt:
                        dma1(pret[l].rearrange("h p v -> p h v"), Rf, reads=["Rf"])
                    phase_end()

                    phase_begin()
                    sc = 128 ** -0.5
                    if pruned:
                        pass
                    elif not is_s:
                        accU = carve(4 * SB_W, F32, (4, SB_W))
                        accD = carve(4 * SB_W, F32, (4, SB_W))
                        NB3 = 2
                        KTs = [carve(1024, BF16) for _ in range(NB3)]
                        QTs = [carve(512, BF16) for _ in range(NB3)]
                        Vs = [carve(8 * 128, BF16, (8, 128)) for _ in range(NB3)]
                        Eb = [carve(1024, F32) for _ in range(2)]
                        Pb = [carve(1024, BF16) for _ in range(2)]
                        it = 0
                        ld = 0
                        for g in range(3):
                            dil = GROUPS[g][1]
                            U = SB_W // dil
                            for j in range(4):
                                rows = slice(j * 128, (j + 1) * 128)
                                gj = g * 4 + j
                                if U >= 512:
                                    batches = [("c", r, ub) for r in range(dil) for ub in range(0, U, 512)]
                                else:
                                    batches = [("r", r0_, 0) for r0_ in range(0, dil, 4)]
                                for (kind, r, ub) in batches:
                                    s = ld % NB3
                                    ld += 1
                                    i2 = it % 2
                                    it += 1
                                    if kind == "c":
                                        colb = c0 + r * U + ub
                                        hcol = (colb - 128) if ub > 0 else ((c0 - SB_W) + r * U + (U - 128))
                                        halo_k = [not (sb == 0 and ub == 0 and k == 0) for k in range(4)]
                                        if halo_k[0]:
                                            dma1(KTs[s][:, 0:128], akT[g][rows, hcol:hcol + 128], writes=[("KTs", s)])
                                            dma1(Vs[s][:, 0, :], av_d[g][hcol:hcol + 128, rows], writes=[("Vs", s)])
                                        dma1(KTs[s][:, 128:640], akT[g][rows, colb:colb + 512], writes=[("KTs", s)])
                                        dma1(Vs[s][:, 1:5, :], av_d[g][colb:colb + 512, rows].rearrange("(c p) d -> p c d", p=128), writes=[("Vs", s)])
                                        prevK = lambda k: KTs[s][:, k * 128:(k + 1) * 128]
                                        diagK = lambda k: KTs[s][:, (k + 1) * 128:(k + 2) * 128]
                                        prevV = lambda k: Vs[s][:, k, :]
                                        diagV = lambda k: Vs[s][:, k + 1, :]
                                        pv_sb = [(sb - 1 if (k == 0 and ub == 0) else sb) for k in range(4)]
                                        start = r + dil * ub
                                        accsl = lambda a: a[:, j, start:start + dil * 511 + 1:dil]
                                    else:
                                        colb = c0 + r * U
                                        hcol = (c0 - SB_W) + r * U
                                        halo_k = [sb > 0] * 4
                                        if sb > 0:
                                            dma1(KTs[s][:, 0:512], akT[g][rows, hcol:hcol + 512], writes=[("KTs", s)])
                                            dma1(Vs[s][:, 0:4, :], av_d[g][hcol:hcol + 512, rows].rearrange("(c p) d -> p c d", p=128), writes=[("Vs", s)])
                                        dma1(KTs[s][:, 512:1024], akT[g][rows, colb:colb + 512], writes=[("KTs", s)])
                                        dma1(Vs[s][:, 4:8, :], av_d[g][colb:colb + 512, rows].rearrange("(c p) d -> p c d", p=128), writes=[("Vs", s)])
                                        prevK = lambda k: KTs[s][:, k * 128:(k + 1) * 128]
                                        diagK = lambda k: KTs[s][:, 512 + k * 128:512 + (k + 1) * 128]
                                        prevV = lambda k: Vs[s][:, k, :]
                                        diagV = lambda k: Vs[s][:, 4 + k, :]
                                        pv_sb = [sb - 1] * 4
                                        accsl = lambda a: a[:, j, :].rearrange("p (u d) -> p d u", d=dil)[:, r:r + 4, :]
                                    dma1(QTs[s], aqT[g][rows, colb:colb + 512], writes=[("QTs", s)])
                                    SP = PAB[i2]
                                    skey = ("pab", i2)
                                    for k in range(4):
                                        qap = QTs[s][:, k * 128:(k + 1) * 128]
                                        if halo_k[k]:
                                            mm(SP[:, k * 256:k * 256 + 128], [(prevK(k), qap)], [("KTs", s), ("QTs", s)], skey)
                                        mm(SP[:, k * 256 + 128:(k + 1) * 256], [(diagK(k), qap)], [("KTs", s), ("QTs", s)], skey)
                                    if all(halo_k):
                                        act(Eb[i2], SP, AF.Exp, [skey], [("Eb", i2)], scale=sc)
                                    else:
                                        for k in range(4):
                                            lo = k * 256 + (0 if halo_k[k] else 128)
                                            act(Eb[i2][:, lo:(k + 1) * 256], SP[:, lo:(k + 1) * 256], AF.Exp, [skey], [("Eb", i2)], scale=sc)
                                    for k in range(4):
                                        lo = k * 256 + (0 if halo_k[k] else 128)
                                        tt("pool", Pb[i2][:, lo:(k + 1) * 256], Eb[i2][:, lo:(k + 1) * 256], AM[:, gj, lo - k * 256:256], ALU.mult,
                                           [("Eb", i2), "AM"], [("Pb", i2)])
                                    for k in range(4):
                                        pairs_u = []
                                        pairs_d = []
                                        if halo_k[k]:
                                            pairs_u.append((prevV(k), Pb[i2][:, k * 256:k * 256 + 128]))
                                            pairs_d.append((vones[:, pv_sb[k], :], Pb[i2][:, k * 256:k * 256 + 128]))
                                        pairs_u.append((diagV(k), Pb[i2][:, k * 256 + 128:(k + 1) * 256]))
                                        pairs_d.append((vones[:, sb, :], Pb[i2][:, k * 256 + 128:(k + 1) * 256]))
                                        mm(PA[4][:, k * 128:(k + 1) * 128], pairs_u, [("Vs", s), ("Pb", i2)], ("pa", 4))
                                        mm(PA[5][:, k * 128:(k + 1) * 128], pairs_d, ["vones", ("Pb", i2)], ("pa", 5))
                                    if kind == "c":
                                        pu, pd = PA[4], PA[5]
                                    else:
                                        pu = PA[4].rearrange("p (k i) -> p k i", k=4)
                                        pd = PA[5].rearrange("p (k i) -> p k i", k=4)
                                    if g == 0:
                                        copy("dve", accsl(accU), pu, [("pa", 4)], [("accU", j)])
                                        act(accsl(accD), pd, AF.Copy, [("pa", 5)], [("accD", j)])
                                    else:
                                        tt("dve", accsl(accU), pu, accsl(accU), ALU.add, [("pa", 4), ("accU", j)], [("accU", j)])
                                        tt("dve", accsl(accD), pd, accsl(accD), ALU.add, [("pa", 5), ("accD", j)], [("accD", j)])
                        ofin = [carve(512, BF16) for _ in range(2)]
                        k2 = 0
                        for j in range(4):
                            for b in range(SB_W // 512):
                                bs = slice(b * 512, (b + 1) * 512)
                                ts("dve", accD[:, j, bs], accD[:, j, bs], 1e-30, None, ALU.add, None, [("accD", j)], [("accD", j)])
                                recip(accD[:, j, bs], accD[:, j, bs], [("accD", j)], [("accD", j)])
                                s2 = k2 % 2
                                k2 += 1
                                tt("pool", ofin[s2], accU[:, j, bs], accD[:, j, bs], ALU.mult, [("accU", j), ("accD", j)], [("ofin", s2)])
                                dma1(oattT[j * 128:(j + 1) * 128, c0 + b * 512:c0 + (b + 1) * 512], ofin[s2], reads=[("ofin", s2)])
                    else:
                        Kc = [carve(512, BF16) for _ in range(2)]
                        KT = [carve(512, BF16, (4, 128)) for _ in range(2)]
                        Vc = carve(21 * 512, BF16, (21, 512))
                        Knew = carve(3 * 4 * ST, BF16, (3, 4, ST))
                        Vnew = carve(3 * 512, BF16, (3, 512))
                        Qn = carve(3 * 4 * ST, BF16, (3, 4, ST))
                        Pall = carve(24 * 32, BF16, (24, 32))
                        Es = [carve(32, F32) for _ in range(2)]
                        osb = carve(64, F32)
                        ofs = carve(32, BF16)
                        for q in range(NS):
                            tcol = c0 + q * ST
                            for g in range(3):
                                dma1(Knew[:, g, :, :], fm(akT[g])[:, :, tcol:tcol + ST], writes=["Knew"])
                                dma1(Qn[:, g, :, :], fm(aqT[g])[:, :, tcol:tcol + ST], writes=["Qn"])
                                dma1(Vnew[0:ST, g, :], av_d[g][tcol:tcol + ST, :], writes=["Vnew"])
                            vt = 0
                            for g in range(3):
                                L = GROUPS[g][0]
                                dma1(Vc[:, vt:vt + L // 128, :], cv[g][l, q].rearrange("(t p) d -> p t d", p=128), writes=["Vc"], eng="pool")
                                vt += L // 128
                            ti = 0
                            kt_i = 0
                            for g in range(3):
                                L = GROUPS[g][0]
                                for t in range(L // 128 + 1):
                                    e2 = ti % 2
                                    if t < L // 128:
                                        s = kt_i % 2
                                        kt_i += 1
                                        dma1(Kc[s], ck[g][l, q, t * 128:(t + 1) * 128, :], writes=[("Kc", s)], eng="pool")
                                        ptv = PT[s].rearrange("p (k n) -> p k n", k=8)
                                        transposes([ptv[:, jj, :] for jj in range(4)], [Kc[s][:, jj * 128:(jj + 1) * 128] for jj in range(4)],
                                                   [("Kc", s)], ("pt", s))
                                        act(KT[s], ptv[:, 0:4, :], AF.Copy, [("pt", s)], [("KT", s)])
                                        kp = 128
                                        for jj in range(4):
                                            mm(PA[2][:, jj * ST:(jj + 1) * ST], [(KT[s][:, jj, :], Qn[:, g, jj, :])], [("KT", s), "Qn"], ("pa", 2))
                                    else:
                                        kp = ST
                                        for jj in range(4):
                                            mm(PA[2][0:ST, jj * ST:(jj + 1) * ST], [(Knew[:, g, jj, :], Qn[:, g, jj, :])], ["Knew", "Qn"], ("pa", 2))
                                    act(Es[e2][0:kp, :], PA[2][0:kp, 0:32], AF.Exp, [("pa", 2)], [("Es", e2)], scale=sc)
                                    tt("dve", Pall[0:kp, ti, :], Es[e2][0:kp, :], SM[0:kp, ti, :], ALU.mult, [("Es", e2), "SM"], ["Pall"])
                                    ti += 1
                            tiles = []
                            ti = 0
                            vt = 0
                            for g in range(3):
                                L = GROUPS[g][0]
                                for t in range(L // 128):
                                    tiles.append((ti, 128, ("c", vt)))
                                    ti += 1
                                    vt += 1
                                tiles.append((ti, ST, ("n", g)))
                                ti += 1
                            for jj in range(4):
                                pairs = []
                                for (ti_, kp, (kind, idx)) in tiles:
                                    if kind == "c":
                                        lhs = Vc[:, idx, jj * 128:(jj + 1) * 128]
                                    else:
                                        lhs = Vnew[0:ST, idx, jj * 128:(jj + 1) * 128]
                                    pairs.append((lhs, Pall[0:kp, ti_, jj * ST:(jj + 1) * ST]))
                                mm(PA[3][:, jj * ST:(jj + 1) * ST], pairs, ["Vc", "Vnew", "Pall"], ("pa", 3))
                            pairs = [(ones[0:kp, :], Pall[0:kp, ti_, :]) for (ti_, kp, _) in tiles]
                            mm(PA[4][:, 0:32], pairs, ["ones", "Pall"], ("pa", 4))
                            recip(osb[:, 0:32], PA[4][:, 0:32], [("pa", 4)], ["osb"])
                            tt("dve", ofs, PA[3][:, 0:32], osb[:, 0:32], ALU.mult, [("pa", 3), "osb"], ["ofs"])
                            dma1(fm(oattT)[:, :, tcol:tcol + ST], ofs.rearrange("p (j i) -> p j i", j=4), reads=["ofs"])
                    phase_end()
                    if pruned:
                        continue

                    phase_begin()
                    gain = carve(D, F32)
                    dma1(gain, nffn[l, :, :], writes=["gain"])
                    wr = carve(8 * 1024, BF16, (8, 1024))
                    wa = carve(4 * 1024, BF16, (4, 1024))
                    wo = carve(8 * 1024, BF16, (8, 1024))
                    dma1(wr, fm(wb_rbr[l]), writes=["wr"])
                    dma1(wa, fm(wb_abr[l]), writes=["wa"])
                    dma1(wo, fm(wb_out[l]), writes=["wo"])
                    NB4 = 1
                    orT = [carve(8 * 512, BF16, (8, 512)) for _ in range(NB4)]
                    oaT = [carve(4 * 512, BF16, (4, 512)) for _ in range(NB4)]
                    gaT = [carve(8 * 512, BF16, (8, 512)) for _ in range(NB4)]
                    gbT = [carve(8 * 512, BF16, (8, 512)) for _ in range(NB4)]
                    mT = [carve(8 * 512, BF16, (8, 512)) for _ in range(2)]
                    t1 = [carve(512, F32) for _ in range(2)]
                    t2 = [carve(512, F32) for _ in range(2)]
                    xts = [carve(D, F32) for _ in range(2)]
                    nb = norm_bufs()
                    ci = 0
                    ti_g = 0
                    for b in range(nblk):
                        s = b % NB4
                        cb = c0 + b * bw
                        dma1(orT[s][:, :, 0:bw], fm(oretT)[:, :, cb:cb + bw], writes=[("orT", s)])
                        dma1(oaT[s][:, :, 0:bw], fm(oattT)[:, :, cb:cb + bw], writes=[("oaT", s)])
                        dma1(gaT[s][:, :, 0:bw], fm(sgaT)[:, :, cb:cb + bw], writes=[("gaT", s)])
                        dma1(gbT[s][:, :, 0:bw], fm(sgbT)[:, :, cb:cb + bw], writes=[("gbT", s)])
                        ms = b % 2
                        for cc in range(8):
                            c2 = ci % 2
                            ci += 1
                            mm(PA[0][:, 0:bw], [(wr[:, k, cc * 128:(cc + 1) * 128], orT[s][:, k, 0:bw]) for k in range(8)], ["wr", ("orT", s)], ("pa", 0))
                            mm(PA[1][:, 0:bw], [(wa[:, k, cc * 128:(cc + 1) * 128], oaT[s][:, k, 0:bw]) for k in range(4)], ["wa", ("oaT", s)], ("pa", 1))
                            tt("dve", t1[c2][:, 0:bw], PA[0][:, 0:bw], gaT[s][:, cc, 0:bw], ALU.mult, [("pa", 0), ("gaT", s)], [("t1", c2)])
                            tt("dve", t2[c2][:, 0:bw], PA[1][:, 0:bw], gbT[s][:, cc, 0:bw], ALU.mult, [("pa", 1), ("gbT", s)], [("t2", c2)])
                            tt("pool", mT[ms][:, cc, 0:bw], t1[c2][:, 0:bw], t2[c2][:, 0:bw], ALU.add, [("t1", c2), ("t2", c2)], [("mT", ms)])
                        for tl in range(bw // 128):
                            t = b * (bw // 128) + tl
                            xs_ = ti_g % 2
                            ti_g += 1
                            r0 = c0 + t * 128
                            dma1(xts[xs_], xres[r0:r0 + 128, :], writes=[("xt", xs_)])
                            for hf in range(2):
                                pb = 2 + hf
                                mm(PA[pb], [(mT[ms][:, k, tl * 128:(tl + 1) * 128], wo[:, k, hf * 512:(hf + 1) * 512]) for k in range(8)],
                                   [("mT", ms), "wo"], ("pa", pb))
                                tt("dve", xts[xs_][:, hf * 512:(hf + 1) * 512], PA[pb], xts[xs_][:, hf * 512:(hf + 1) * 512], ALU.add,
                                   [("pa", pb), ("xt", xs_)], [("xt", xs_)])
                            dma1(xres[r0:r0 + 128, :], xts[xs_], reads=[("xt", xs_)])
                            norm_tile(xts[xs_], ("xt", xs_), gain, t * 128, nb, ti_g)
                    phase_end()

                    phase_begin()
                    wps = [carve(8 * 512, BF16, (8, 512)) for _ in range(2)]
                    rl = [carve(512, F32) for _ in range(2)]
                    us = [carve(512, BF16) for _ in range(2)]
                    k3 = 0
                    for pc in range(DFF // 512):
                        s = pc % 2
                        dma1(wps[s], fm(wb_up[l])[:, :, pc * 512:(pc + 1) * 512], writes=[("wp", s)])
                        for cc in range(4):
                            for b in range(nblk):
                                rr["ps"] = (rr["ps"] + 1) % 4
                                pb = rr["ps"]
                                mm(PA[pb][:, 0:bw], [(wps[s][:, k, cc * 128:(cc + 1) * 128], hT[:, k, b * bw:(b + 1) * bw]) for k in range(8)],
                                   [("wp", s), "hT"], ("pa", pb))
                                s3 = k3 % 2
                                k3 += 1
                                act(rl[s3][:, 0:bw], PA[pb][:, 0:bw], AF.Relu, [("pa", pb)], [("rl", s3)])
                                tt("pool", us[s3][:, 0:bw], rl[s3][:, 0:bw], rl[s3][:, 0:bw], ALU.mult, [("rl", s3)], [("us", s3)])
                                fr = pc * 512 + cc * 128
                                dma1(uT[fr:fr + 128, c0 + b * bw:c0 + (b + 1) * bw], us[s3][:, 0:bw], reads=[("us", s3)])
                    phase_end()

                    phase_begin()
                    wd = carve(32 * 1024, BF16, (32, 1024))
                    for kq in range(4):
                        dma1(wd[:, kq * 8:(kq + 1) * 8, :], fm(wb_dn[l])[:, kq * 8:(kq + 1) * 8, :], writes=["wd"])
                    wg = carve(8 * 1024, BF16, (8, 1024))
                    dma1(wg, fm(wb_pg[l]), writes=["wg"])
                    wpl = carve(2 * 1024, BF16, (2, 1024))
                    dma1(wpl, fm(wb_ple[l]), writes=["wpl"])
                    final = (l == DEPTH - 1)
                    if final:
                        gain = carve(D, F32)
                        dma1(gain, nfin[:, :], writes=["gain"])
                    hT_flat = hT.rearrange("p k n -> p (k n)")
                    uTs = [hT_flat[:, i_ * 8192:(i_ + 1) * 8192].rearrange("p (k n) -> p k n", k=32) for i_ in range(2)]
                    xts = [carve(D, F32) for _ in range(2)]
                    pts = [carve(DPLE, F32) for _ in range(2)]
                    xb = [carve(D, BF16) for _ in range(1)]
                    pb16 = [carve(DPLE, BF16) for _ in range(2)]
                    xT = [carve(8 * 128, BF16, (8, 128)) for _ in range(1)]
                    pT = [carve(2 * 128, BF16, (2, 128)) for _ in range(2)]
                    sg = [carve(512, F32) for _ in range(1)]
                    pp = [carve(512, F32) for _ in range(1)]
                    yt = [carve(D, F32) for _ in range(1)]
                    ssf = [carve(4, F32) for _ in range(2)]
                    def fd_a(t):
                        s = t % 2
                        r0 = c0 + t * 128
                        us_ = (t // 2) % 2
                        uo = (t % 2) * 128
                        if t % 2 == 0:
                            tw = min(256, Wd - t * 128)
                            dma1(uTs[us_][:, :, 0:tw], fm(uT)[:, :, r0:r0 + tw], writes=[("uTs", us_)])
                        dma1(xts[s], xres[r0:r0 + 128, :], writes=[("xt", s)])
                        if is_s:
                            memset("pool", pts[s], 0.0, [("pts", s)])
                            dma1(pts[s][0:NS * ST, :], ps_in[l, :, :], writes=[("pts", s)])
                        else:
                            dma1(pts[s], pw[l, r0:r0 + 128, :], writes=[("pts", s)])
                        for hf in range(2):
                            pbk = hf
                            mm(PA[pbk], [(uTs[us_][:, k, uo:uo + 128], wd[:, k, hf * 512:(hf + 1) * 512]) for k in range(32)], [("uTs", us_), "wd"], ("pa", pbk))
                            tt("dve", xts[s][:, hf * 512:(hf + 1) * 512], PA[pbk], xts[s][:, hf * 512:(hf + 1) * 512], ALU.add,
                               [("pa", pbk), ("xt", s)], [("xt", s)])
                    def fd_b(t):
                        s = t % 2
                        r0 = c0 + t * 128
                        copy("pool", xb[0], xts[s], [("xt", s)], [("xb", 0)])
                        copy("pool", pb16[s], pts[s], [("pts", s)], [("pb16", s)])
                        ptv = PT[0].rearrange("p (k n) -> p k n", k=8)
                        transposes([ptv[:, k, :] for k in range(8)], [xb[0][:, k * 128:(k + 1) * 128] for k in range(8)], [("xb", 0)], ("pt", 0))
                        act(xT[0], ptv, AF.Copy, [("pt", 0)], [("xT", 0)])
                        ptv1 = PT[1].rearrange("p (k n) -> p k n", k=8)
                        transposes([ptv1[:, k, :] for k in range(2)], [pb16[s][:, k * 128:(k + 1) * 128] for k in range(2)], [("pb16", s)], ("pt", 1))
                        copy("dve", pT[s], ptv1[:, 0:2, :], [("pt", 1)], [("pT", s)])
                        for hf in range(2):
                            s4 = 0
                            mm(PA[2], [(xT[0][:, k, :], wg[:, k, hf * 512:(hf + 1) * 512]) for k in range(8)], [("xT", 0), "wg"], ("pa", 2))
                            mm(PA[3], [(pT[s][:, k, :], wpl[:, k, hf * 512:(hf + 1) * 512]) for k in range(2)], [("pT", s), "wpl"], ("pa", 3))
                            act(sg[s4], PA[2], AF.Sigmoid, [("pa", 2)], [("sg", s4)])
                            tt("dve", pp[s4], PA[3], sg[s4], ALU.mult, [("pa", 3), ("sg", s4)], [("pp", s4)])
                            tt("pool", xts[s][:, hf * 512:(hf + 1) * 512], xts[s][:, hf * 512:(hf + 1) * 512], pp[s4], ALU.add,
                               [("xt", s), ("pp", s4)], [("xt", s)])
                        if not final:
                            dma1(xres[r0:r0 + 128, :], xts[s], reads=[("xt", s)])
                        else:
                            dve_ttr(yt[0], xts[s], xts[s], ssf[s][:, 0:1], [("xt", s)], [("yt", 0), ("ssf", s)])
                            act(ssf[s][:, 1:2], ssf[s][:, 0:1], AF.Sqrt, [("ssf", s), "epsb"], [("ssf", s)], bias=epsb[:, 0:1], scale=1.0 / D)
                            recip(ssf[s][:, 2:3], ssf[s][:, 1:2], [("ssf", s)], [("ssf", s)])
                            stt(yt[0], xts[s], ssf[s][:, 2:3], gain, ALU.mult, ALU.mult, [("xt", s), ("ssf", s), "gain"], [("yt", 0)])
                            if is_s:
                                dma1(ys[:, :], yt[0][0:NS * ST, :], reads=[("yt", 0)])
                            elif sb == NSB - 1:
                                dma1(y[r0 - (W - SB_W):r0 - (W - SB_W) + 128, :], yt[0], reads=[("yt", 0)])

                    fd_a(0)
                    for t in range(ntile):
                        if t + 1 < ntile:
                            fd_a(t + 1)
                        fd_b(t)
                    phase_end()
        except _Stop:
            pass

        cx.barrier(final=True)

        semnames = {}
        import contextlib
        with contextlib.ExitStack() as es:
            sems = {}
            for i, sk_ in enumerate(cx.semkeys):
                sems[sk_] = es.enter_context(nc.semaphore("s%d" % i))
            with nc.Block() as block:
                @block.tensor
                def _(e):
                    cx.replay("pe", e, sems)

                @block.scalar
                def _(e):
                    cx.replay("act", e, sems)

                @block.vector
                def _(e):
                    cx.replay("dve", e, sems)

                @block.gpsimd
                def _(e):
                    cx.replay("pool", e, sems)

                @block.sync
                def _(e):
                    cx.replay("sp", e, sems)
    return nc


_CACHE = {}


def make_in_maps(W, inputs, n_cores=8):
    consts = make_consts()
    f = lambda a: np.ascontiguousarray(np.asarray(a, dtype=np.float32))
    B = inputs["x_prompt"].shape[0]
    cores_per_b = n_cores // B
    shared = {k: f(inputs[k]) for k in ("w_in", "w_ret_br", "w_att_br", "w_out", "w_up", "w_down", "w_ple", "w_ple_gate")}
    shared["nmix"] = f(np.broadcast_to(np.asarray(inputs["norm_mix"])[:, None, :], (DEPTH, 128, D)))
    shared["nffn"] = f(np.broadcast_to(np.asarray(inputs["norm_ffn"])[:, None, :], (DEPTH, 128, D)))
    shared["nfin"] = f(np.broadcast_to(np.asarray(inputs["norm_final"])[None, :], (128, D)))
    shared.update(consts)
    maps = []
    NSB = W // SB_W
    for c in range(n_cores):
        b = c // cores_per_b
        seg = ((c % cores_per_b) * NSB) // cores_per_b
        npad = (NSB - 1 - seg) * SB_W
        m = dict(shared)
        xwin = np.zeros((W, D), np.float32)
        xwin[npad:] = np.asarray(inputs["x_prompt"])[b, :W - npad]
        pwin = np.zeros((DEPTH, W, DPLE), np.float32)
        pwin[:, npad:] = np.asarray(inputs["p_prompt"])[:, b, :W - npad]
        m["xw"] = xwin
        m["pw"] = pwin
        vo = np.zeros((128, NSB, 128), np.float32)
        vo[:, NSB - 1 - seg:, :] = 1.0
        m["vones"] = vo
        sl = slice(c * NS, (c + 1) * NS)
        m["xs"] = f(np.asarray(inputs["x_sample"])[sl].reshape(NS * ST, D))
        m["ps"] = f(np.asarray(inputs["p_sample"])[:, sl].reshape(DEPTH, NS * ST, DPLE))
        caches_k = (inputs["cache_win_k0"], inputs["cache_win_k1"], inputs["cache_win_k2"])
        caches_v = (inputs["cache_win_v0"], inputs["cache_win_v1"], inputs["cache_win_v2"])
        for g in range(3):
            L = GROUPS[g][0]
            m["ck%d" % g] = f(np.asarray(caches_k[g])[:, sl].reshape(DEPTH, NS, L, 512))
            m["cv%d" % g] = f(np.asarray(caches_v[g])[:, sl].reshape(DEPTH, NS, L, 512))
        m["st"] = f(np.asarray(inputs["state_ret"])[:, sl])
        maps.append(m)
    return maps


def assemble(W, res, B, n_cores=8):
    cores_per_b = n_cores // B
    R = res.results
    LG = [min(g[0], W) for g in GROUPS]
    NSB = W // SB_W
    y_prompt = np.zeros((B, W, D), np.float32)
    for c in range(n_cores):
        b = c // cores_per_b
        seg = ((c % cores_per_b) * NSB) // cores_per_b
        y_prompt[b, seg * SB_W:(seg + 1) * SB_W] = R[c]["y"]
    y_sample = np.concatenate([R[c]["ys"].reshape(NS, ST, D) for c in range(n_cores)]).astype(np.float32)
    outs = [y_prompt, y_sample]
    for g in range(3):
        for nm in ("pk", "pv"):
            a = np.stack([R[b * cores_per_b + cores_per_b - 1][nm + str(g)] for b in range(B)], axis=1)
            outs.append(a.reshape(DEPTH, B, LG[g], 4, 128).astype(np.float32))
    outs.append(np.stack([R[b * cores_per_b + cores_per_b - 1]["pret"] for b in range(B)], axis=1).astype(np.float32))
    for g in range(3):
        L = GROUPS[g][0]
        for nm in ("sk", "sv"):
            a = np.concatenate([R[c][nm + str(g)] for c in range(n_cores)], axis=1)
            outs.append(a.reshape(DEPTH, n_cores * NS, L, 4, 128).astype(np.float32))
    outs.append(np.concatenate([R[c]["sret"] for c in range(n_cores)], axis=1).astype(np.float32))
    return tuple(outs)


def kernel(**inputs):
    W = int(np.asarray(inputs["x_prompt"]).shape[1])
    B = int(np.asarray(inputs["x_prompt"]).shape[0])
    if W not in _CACHE:
        _CACHE[W] = build(W)
    nc = _CACHE[W]
    in_maps = make_in_maps(W, inputs)
    res = run_bass_kernel_spmd(nc, in_maps, core_ids=list(range(8)))
    return assemble(W, res, B)
```

```python
import math
import numpy as np
import concourse.bass as bass
import concourse.mybir as mybir
from concourse.bass_utils import run_bass_kernel_spmd

F32 = mybir.dt.float32
BF16 = mybir.dt.bfloat16
AF = mybir.ActivationFunctionType
ALU = mybir.AluOpType

D = 1024
DEPTH = 2
DPLE = 256
DFF = 4096
DIN = 9728
EPS = 1e-6
GROUPS = ((128, 1), (512, 4), (2048, 16))
SB_W = 2048
NS = 4
ST = 8
NDMA = 16
NSDMA = 4
import os
STOP_AFTER = int(os.environ.get("KSTOP", "0"))
KP = int(os.environ.get("KP", "0"))
KSKIP = int(os.environ.get("KSKIP", "0"))
DMASPREAD = int(os.environ.get("DMASPREAD", "0"))
PRUNE = int(os.environ.get("KPRUNE", "1"))
ENGS = ("pe", "act", "dve", "pool", "sp")

C_RQ, C_RK, C_RV, C_RG, C_AQ, C_AK, C_AV, C_GA, C_GB = 0, 512, 1024, 2048, 3072, 4608, 6144, 7680, 8704


def _lg():
    return np.log1p(-np.exp(np.linspace(math.log(1.0 / 32), math.log(1.0 / 512), 4)))


def _slopes():
    return 2.0 ** (-8.0 * (np.arange(12, dtype=np.float64) + 1.0) / 12)


def make_consts():
    lg = _lg()
    sl = _slopes()
    p = np.arange(128)
    c = {}
    c["c_ident"] = np.eye(128, dtype=np.float32)
    diff = p[None, :] - p[:, None]
    rm = np.zeros((128, 4, 128), np.float64)
    for h in range(4):
        rm[:, h, :] = np.where(diff >= 0, np.exp(lg[h] * np.maximum(diff, 0)), 0.0) * (128 ** -0.5)
    c["c_rm"] = rm.astype(np.float32)
    xi = np.zeros((128, 4, 128), np.float64)
    for h in range(4):
        xi[:, h, :] = np.exp(lg[h] * (p + 1.0))[None, :]
    c["c_xi"] = xi.astype(np.float32)
    z = np.zeros((128, 8), np.float64)
    for h in range(4):
        z[:, h] = np.exp(lg[h] * (127.0 - p)) * (128 ** -0.5)
        z[:, 4 + h] = np.exp(lg[h] * (7.0 - (p % 8))) * (128 ** -0.5)
    c["c_z"] = z.astype(np.float32)
    am = np.zeros((128, 12, 256), np.float64)
    i = np.arange(128)
    for g, (Wg, dil) in enumerate(GROUPS):
        for j in range(4):
            s = sl[g * 4 + j] * dil
            dprev = i[None, :] + 128 - p[:, None]
            ddiag = i[None, :] - p[:, None]
            am[:, g * 4 + j, 0:128] = np.where(dprev <= 128, np.exp(-s * dprev), 0.0)
            am[:, g * 4 + j, 128:256] = np.where(ddiag >= 0, np.exp(-s * np.maximum(ddiag, 0)), 0.0)
    c["c_am"] = am.astype(np.float32)
    sm = np.zeros((128, 24, 32), np.float64)
    t = 0
    for g, (Wg, dil) in enumerate(GROUPS):
        L = Wg
        ntile = L // 128
        for tt in range(ntile + 1):
            for j in range(4):
                s = sl[g * 4 + j]
                for qi in range(8):
                    if tt < ntile:
                        kpos = tt * 128 + p
                        ok = np.ones(128, bool)
                    else:
                        kpos = L + p
                        ok = p < 8
                    dist = (L + qi) - kpos
                    valid = ok & (dist >= 0) & (dist % dil == 0) & (dist // dil <= 128)
                    sm[:, t, j * 8 + qi] = np.where(valid, np.exp(-s * np.maximum(dist, 0)), 0.0)
            t += 1
    c["c_sm"] = sm.astype(np.float32)
    return c


class Ctx:
    def __init__(self, nc):
        self.nc = nc
        self.q = {e: [] for e in ENGS}
        self.semcount = {}
        self.seen = {e: {} for e in ENGS}
        self.lastw = {}
        self.readers = {}
        self.dma_rr = 0
        self.sdma_rr = 0
        self.semkeys = [("eng", e) for e in ("pe", "act", "dve", "pool")] + [("dma", i) for i in range(NDMA)] + [("sdma", i) for i in range(NSDMA)] + [("sbg", i) for i in range(NSDMA)] + [("bg", 0)]
        for s in self.semkeys:
            self.semcount[s] = 0

    def op(self, eng, fn, reads=(), writes=(), dma=0, bg=False):
        deps = {}

        def add(tok):
            if tok is None:
                return
            s, v = tok
            if deps.get(s, 0) < v:
                deps[s] = v
        for k in reads:
            add(self.lastw.get(k))
        for k in writes:
            add(self.lastw.get(k))
            for t in self.readers.get(k, ()):
                add(t)
        if bg == "s":
            d = self.sdma_rr
            self.sdma_rr = (d + 1) % NSDMA
            s = ("sbg", d)
            if self.semcount[s] > 0:
                add((s, self.semcount[s]))
            self.semcount[s] += 16 * dma
            tok = (s, self.semcount[s])
        elif bg:
            s = ("bg", 0)
            self.semcount[s] += 16 * dma
            tok = (s, self.semcount[s])
        elif dma:
            if eng == "pool":
                d = self.sdma_rr
                self.sdma_rr = (d + 1) % NSDMA
                s = ("sdma", d)
            else:
                d = self.dma_rr
                self.dma_rr = (d + 1) % NDMA
                s = ("dma", d)
            if self.semcount[s] > 0:
                add((s, self.semcount[s]))
            self.semcount[s] += 16 * dma
            tok = (s, self.semcount[s])
        else:
            s = ("eng", eng)
            self.semcount[s] += 1
            tok = (s, self.semcount[s])
        waits = []
        for sk, v in deps.items():
            if sk == ("eng", "pe") and eng == "pe" and not dma:
                continue
            if self.seen[eng].get(sk, 0) >= v:
                continue
            self.seen[eng][sk] = v
            waits.append((sk, v))
        self.q[eng].append((waits, fn, tok[0], dma))
        for k in reads:
            self.readers.setdefault(k, []).append(tok)
        for k in writes:
            self.lastw[k] = tok
            self.readers[k] = []
        return tok

    def barrier(self, final=False, sbg=False):
        for e in ENGS:
            waits = []
            for sk in self.semkeys:
                if sk == ("bg", 0) and not final:
                    continue
                if sk[0] == "sbg" and not (final or sbg):
                    continue
                v = self.semcount[sk]
                if v > 0 and self.seen[e].get(sk, 0) < v:
                    self.seen[e][sk] = v
                    waits.append((sk, v))
            self.q[e].append((waits, None, None, 0))
        self.lastw = {}
        self.readers = {}

    def replay(self, eng, e, sems):
        for waits, fn, s, dma in self.q[eng]:
            for (sk, v) in waits:
                e.wait_ge(sems[sk], v)
            if fn is None:
                continue
            r = fn(e)
            if dma:
                assert len(r) == dma, (len(r), dma)
                for ins in r:
                    ins.then_inc(sems[s], 16)
            else:
                ins = r[-1] if isinstance(r, (list, tuple)) else r
                ins.then_inc(sems[s], 1)


def build(W):
    NSB = W // SB_W
    NT = W // 128
    TT = NT + 1
    NTOK = TT * 128
    LG = [min(g[0], W) for g in GROUPS]
    lg = _lg()
    dec128 = [float(np.exp(lg[h] * 128)) for h in range(4)]
    dec8 = [float(np.exp(lg[h] * 8)) for h in range(4)]

    nc = bass.Bass("TRN2", target_bir_lowering=False)

    def din(name, shape, dt=F32):
        return nc.dram_tensor(name, list(shape), dt, kind="ExternalInput").ap()

    def dout(name, shape):
        return nc.dram_tensor(name, list(shape), F32, kind="ExternalOutput").ap()

    def dscr(name, shape, dt=BF16):
        return nc.dram_tensor(name, list(shape), dt, kind="Internal").ap()

    xw = din("xw", [W, D]); xs = din("xs", [NS * ST, D])
    pw = din("pw", [DEPTH, W, DPLE]); ps_in = din("ps", [DEPTH, NS * ST, DPLE])
    ck = [din("ck%d" % g, [DEPTH, NS, GROUPS[g][0], 512]) for g in range(3)]
    cv = [din("cv%d" % g, [DEPTH, NS, GROUPS[g][0], 512]) for g in range(3)]
    st_in = din("st", [DEPTH, NS, 4, 128, 256])
    w_in = din("w_in", [DEPTH, D, DIN]); w_rbr = din("w_ret_br", [DEPTH, 1024, D]); w_abr = din("w_att_br", [DEPTH, 512, D])
    w_out = din("w_out", [DEPTH, D, D]); w_up = din("w_up", [DEPTH, D, DFF]); w_dn = din("w_down", [DEPTH, DFF, D])
    w_ple = din("w_ple", [DEPTH, DPLE, D]); w_pg = din("w_ple_gate", [DEPTH, D, D])
    nmix = din("nmix", [DEPTH, 128, D]); nffn = din("nffn", [DEPTH, 128, D]); nfin = din("nfin", [128, D])
    c_ident = din("c_ident", [128, 128]); c_rm = din("c_rm", [128, 4, 128]); c_xi = din("c_xi", [128, 4, 128])
    c_z = din("c_z", [128, 8]); c_am = din("c_am", [128, 12, 256]); c_sm = din("c_sm", [128, 24, 32])
    vones_in = din("vones", [128, NSB, 128])

    y = dout("y", [SB_W, D]); ys = dout("ys", [NS * ST, D])
    pk = [dout("pk%d" % g, [DEPTH, LG[g], 512]) for g in range(3)]
    pv = [dout("pv%d" % g, [DEPTH, LG[g], 512]) for g in range(3)]
    pret = dout("pret", [DEPTH, 4, 128, 256])
    sk = [dout("sk%d" % g, [DEPTH, NS, GROUPS[g][0], 512]) for g in range(3)]
    sv = [dout("sv%d" % g, [DEPTH, NS, GROUPS[g][0], 512]) for g in range(3)]
    sret = dout("sret", [DEPTH, NS, 4, 128, 256])

    wb_in = dscr("wb_in", [DEPTH, D, DIN]); wb_rbr = dscr("wb_rbr", [DEPTH, 1024, D]); wb_abr = dscr("wb_abr", [DEPTH, 512, D])
    wb_out = dscr("wb_out", [DEPTH, D, D]); wb_up = dscr("wb_up", [DEPTH, D, DFF]); wb_dn = dscr("wb_dn", [DEPTH, DFF, D])
    wb_ple = dscr("wb_ple", [DEPTH, DPLE, D]); wb_pg = dscr("wb_pg", [DEPTH, D, D])
    xres = dscr("xres", [NTOK, D], F32)
    rqT = dscr("rqT", [512, NTOK]); rkT = dscr("rkT", [512, NTOK])
    rv_d = dscr("rv", [NTOK, 1024]); kz_d = dscr("kz", [NTOK, 512]); srg_d = dscr("srg", [NTOK, 1024])
    aqT = [dscr("aqT%d" % g, [512, NTOK]) for g in range(3)]
    akT = [dscr("akT%d" % g, [512, NTOK]) for g in range(3)]
    av_d = [dscr("av%d" % g, [NTOK, 512]) for g in range(3)]
    sgaT = dscr("sgaT", [1024, NTOK]); sgbT = dscr("sgbT", [1024, NTOK])
    oretT = dscr("oretT", [1024, NTOK]); oattT = dscr("oattT", [512, NTOK])
    uT = dscr("uT", [DFF, NTOK])

    def fm(t):
        return t.rearrange("(k p) n -> p k n", p=128)

    cx = Ctx(nc)
    ARENA = 47 * 1024

    with (
        nc.sbuf_tensor("arena", [128, ARENA], F32) as arena,
        nc.psum_tensor("pab0", [128, 1024], F32) as pab0, nc.psum_tensor("pab1", [128, 1024], F32) as pab1,
        nc.psum_tensor("pa4", [128, 512], F32) as pa4, nc.psum_tensor("pa5", [128, 512], F32) as pa5,
        nc.psum_tensor("pt0", [128, 1024], BF16) as pt0, nc.psum_tensor("pt1", [128, 1024], BF16) as pt1,
    ):
        arena_ap = arena[:, :]
        PAB = [pab0[:, :], pab1[:, :]]
        PA = [PAB[0][:, 0:512], PAB[0][:, 512:1024], PAB[1][:, 0:512], PAB[1][:, 512:1024], pa4[:, :], pa5[:, :]]
        PT = [p_[:, :] for p_ in (pt0, pt1)]
        state = {"off": 0, "base": 0}

        def carve(n_elems, dt=F32, shape=None):
            n32 = (n_elems + 1) // 2 if dt == BF16 else n_elems
            n32 = (n32 + 7) // 8 * 8
            off = state["off"]
            assert off + n32 <= ARENA, ("SBUF arena overflow", off, n32)
            state["off"] = off + n32
            v = arena_ap[:, off:off + n32]
            if dt == BF16:
                v = v.bitcast(BF16)[:, 0:n_elems]
            else:
                v = v[:, 0:n_elems]
            if shape is not None:
                names = " ".join("d%d" % i for i in range(len(shape)))
                kw = {"d%d" % i: s for i, s in enumerate(shape)}
                v = v.rearrange("p (%s) -> p %s" % (names, names), **kw)
            return v

        def phase_begin():
            state["off"] = state["base"]

        class _Stop(Exception):
            pass

        def phase_end():
            cx.barrier()
            state["nph"] = state.get("nph", 0) + 1
            if STOP_AFTER and state["nph"] >= STOP_AFTER and state.get("main"):
                raise _Stop()

        ident = carve(128, BF16)
        ones = carve(128, BF16)
        RM = carve(512, F32, (4, 128))
        XI = carve(512, F32, (4, 128))
        ZC = carve(8, F32)
        AM = carve(12 * 256, F32, (12, 256))
        SM = carve(24 * 32, F32, (24, 32))
        hT = carve(8 * SB_W, BF16, (8, SB_W))
        Rf = carve(4 * 256, F32, (4, 256))
        Rb = carve(4 * 256, BF16, (4, 256))
        epsb = carve(1, F32)
        zt16 = carve(8 * 128, BF16, (8, 128))
        vones = carve(NSB * 128, BF16, (NSB, 128))
        state["base"] = state["off"]

        rr = {"ps": 0, "ev": 0}

        def dma(fn, reads=(), writes=(), n=1, eng="sp", bg=False):
            if eng == "sp" and DMASPREAD:
                rr["dq"] = (rr.get("dq", 0) + 1) % 2
                eng = ("sp", "act")[rr["dq"]]
            return cx.op(eng, fn, reads=reads, writes=writes, dma=n, bg=bg)

        def dma1(out, in_, reads=(), writes=(), eng="sp", bg=False):
            return dma(lambda e: [e.dma_start(out=out, in_=in_)], reads, writes, 1, eng, bg)

        def mm(ps_ap, pairs, reads, ps_key):
            def fn(e):
                r = None
                n = len(pairs)
                for i, (a, b) in enumerate(pairs):
                    r = e.matmul(ps_ap, a, b, start=(i == 0), stop=(i == n - 1))
                return r
            return cx.op("pe", fn, reads=reads, writes=[ps_key])

        def transposes(pt_ap_list, in_list, reads, ps_key):
            def fn(e):
                r = None
                for o, i_ in zip(pt_ap_list, in_list):
                    r = e.transpose(o, i_, ident[0:i_.shape[0], 0:i_.shape[0]])
                return r
            return cx.op("pe", fn, reads=list(reads) + ["ident"], writes=[ps_key])

        def act(out, in_, func, reads, writes, bias=None, scale=None):
            kw = {}
            if bias is not None:
                kw["bias"] = bias
            if scale is not None:
                kw["scale"] = scale
            return cx.op("act", lambda e: e.activation(out=out, in_=in_, func=func, **kw), reads=reads, writes=writes)

        def tt(eng, out, in0, in1, op, reads, writes):
            return cx.op(eng, lambda e: e.tensor_tensor(out=out, in0=in0, in1=in1, op=op), reads=reads, writes=writes)

        def evac_copy(out, in_, reads, writes, force=None):
            rr["ev"] ^= 1
            if force == "dve":
                rr["ev"] = 0
            if rr["ev"]:
                return act(out, in_, AF.Copy, reads, writes)
            return cx.op("dve", lambda e: e.tensor_copy(out=out, in_=in_), reads=reads, writes=writes)


        def dve_ttr(out, in0, in1, accum, reads, writes):
            act(out, in0, AF.Square, reads, [writes[0]])
            return cx.op("dve", lambda e: e.tensor_reduce(out=accum, in_=out, axis=mybir.AxisListType.X, op=ALU.add),
                         reads=[writes[0]], writes=list(writes[1:]))

        def recip(out, in_, reads, writes):
            return cx.op("dve", lambda e: e.reciprocal(out=out, in_=in_), reads=reads, writes=writes)

        def stt(out, in0, scalar, in1, op0, op1, reads, writes):
            return cx.op("dve", lambda e: e.scalar_tensor_tensor(out=out, in0=in0, scalar=scalar, in1=in1, op0=op0, op1=op1),
                         reads=reads, writes=writes)

        def ts(eng, out, in0, s1, s2, op0, op1, reads, writes):
            if s2 is None:
                return cx.op(eng, lambda e: e.tensor_scalar(out=out, in0=in0, scalar1=s1, scalar2=None, op0=op0), reads=reads, writes=writes)
            return cx.op(eng, lambda e: e.tensor_scalar(out=out, in0=in0, scalar1=s1, scalar2=s2, op0=op0, op1=op1), reads=reads, writes=writes)

        def copy(eng, out, in_, reads, writes):
            return cx.op(eng, lambda e: e.tensor_copy(out=out, in_=in_), reads=reads, writes=writes)

        def memset(eng, out, val, writes):
            return cx.op(eng, lambda e: e.memset(out, val), writes=writes)

        def bnstats(out, in_, reads, writes):
            return cx.op("dve", lambda e: e.bn_stats(out=out, in_=in_), reads=reads, writes=writes)

        def bnaggr(out, in_, reads, writes):
            return cx.op("dve", lambda e: e.bn_aggr(out=out, in_=in_), reads=reads, writes=writes)

        def dman(pairs, reads=(), writes=(), eng="sp"):
            pairs = list(pairs)
            return dma(lambda e: [e.dma_start(out=o, in_=i) for (o, i) in pairs], reads, writes, len(pairs), eng)

        phase_begin()
        tmpc = carve(128, F32)
        dma1(tmpc, c_ident[:, :], writes=["tmpc"])
        copy("dve", ident, tmpc, ["tmpc"], ["ident"])
        memset("dve", ones, 1.0, ["ones"])
        memset("dve", epsb, EPS, ["epsb"])
        memset("dve", zt16, 0.0, ["zt16"])
        vtmp = carve(NSB * 128, F32, (NSB, 128))
        dma1(vtmp, vones_in[:, :, :], writes=["vtmp"])
        copy("dve", vones, vtmp, ["vtmp"], ["vones"])
        dma1(RM, c_rm[:, :, :], writes=["RM"])
        dma1(XI, c_xi[:, :, :], writes=["XI"])
        dma1(ZC, c_z[:, :], writes=["ZC"])
        dma1(AM, c_am[:, :, :], writes=["AM"])
        dma1(SM, c_sm[:, :, :], writes=["SM"])
        late_casts = []
        for (src, dst, rows) in ((w_in, wb_in, D), (w_rbr, wb_rbr, 1024), (w_abr, wb_abr, 512), (w_out, wb_out, D),
                                 (w_up, wb_up, D), (w_dn, wb_dn, DFF), (w_ple, wb_ple, DPLE), (w_pg, wb_pg, D)):
            for l in range(DEPTH):
                for r0 in range(0, rows, 128):
                    first = (dst is wb_in and l == 0)
                    if first:
                        dma1(dst[l, r0:r0 + 128, :], src[l, r0:r0 + 128, :], eng="pool")
                    else:
                        late_casts.append((dst[l, r0:r0 + 128, :], src[l, r0:r0 + 128, :]))
        for r0 in range(0, W, 1024):
            dma1(xres[r0:r0 + 1024, :], xw[r0:r0 + 1024, :])
        ztile = carve(D, F32)
        memset("dve", ztile, 0.0, ["ztile"])
        dma1(xres[W:W + 128, :], ztile, reads=["ztile"], writes=["xres_s"])
        dma1(xres[W:W + NS * ST, :], xs[:, :], writes=["xres_s"])
        for l in range(DEPTH):
            for g in range(3):
                L = GROUPS[g][0]
                for (src, dst) in ((ck[g], sk[g]), (cv[g], sv[g])):
                    for q in range(NS):
                        dma1(dst[l, q, 0:L - ST, :], src[l, q, ST:L, :], bg=True)
        phase_end()
        for (o_, i_) in late_casts:
            dma1(o_, i_, eng="pool", bg="s")

        def norm_tile(xt, xkey, gain, col0, bufs, i):
            junk, ss, hb = bufs
            s = i % 2
            dve_ttr(junk[s], xt, xt, ss[s][:, 0:1], [xkey], [("junk", s), ("ss", s)])
            act(ss[s][:, 1:2], ss[s][:, 0:1], AF.Sqrt, [("ss", s), "epsb"], [("ss", s)], bias=epsb[:, 0:1], scale=1.0 / D)
            recip(ss[s][:, 2:3], ss[s][:, 1:2], [("ss", s)], [("ss", s)])
            stt(hb[s], xt, ss[s][:, 2:3], gain, ALU.mult, ALU.mult, [xkey, ("ss", s), "gain"], [("hb", s)])
            ptv = PT[s].rearrange("p (k n) -> p k n", k=8)
            transposes([ptv[:, k, :] for k in range(8)], [hb[s][:, k * 128:(k + 1) * 128] for k in range(8)],
                       [("hb", s)], ("pt", s))
            act(hT[:, :, col0:col0 + 128], ptv, AF.Copy, [("pt", s)], ["hT"])

        def norm_bufs():
            junk = [carve(D, F32) for _ in range(2)]
            ss = [carve(4, F32) for _ in range(2)]
            hb = [carve(D, BF16) for _ in range(2)]
            return junk, ss, hb

        state["main"] = True
        try:
            SBS = [(sb, sb * SB_W, SB_W, False) for sb in range(NSB)] + [(NSB, W, 128, True)]

            for l in range(DEPTH):
                memset("dve", Rf, 0.0, ["Rf"])
                memset("dve", Rb, 0.0, ["Rb"])
                for (sb, c0, Wd, is_s) in SBS:
                    ntile = Wd // 128
                    nblk = max(1, Wd // 512)
                    bw = min(512, Wd)
                    last_prompt = (not is_s) and sb == NSB - 1
                    pruned = PRUNE and (l == DEPTH - 1) and (not is_s) and sb < NSB - 1
                    need_halo_kv = pruned and sb == NSB - 2

                    phase_begin()
                    gain = carve(D, F32)
                    dma1(gain, nmix[l, :, :], writes=["gain"])
                    xts = [carve(D, F32) for _ in range(2)]
                    nb = norm_bufs()
                    for t in range(ntile):
                        s = t % 2
                        dma1(xts[s], xres[c0 + t * 128:c0 + (t + 1) * 128, :], writes=[("xt", s)])
                        norm_tile(xts[s], ("xt", s), gain, t * 128, nb, t)
                    phase_end()

                    phase_begin()
                    wps = [carve(8 * 512, BF16, (8, 512)) for _ in range(2)]
                    stg = [carve(512, BF16) for _ in range(3)]
                    stgf = [carve(512, F32) for _ in range(2)]
                    stgB = [carve(SB_W, BF16) for _ in range(2)]
                    cnt = {"w": 0, "s": 0, "f": 0, "B": 0}

                    def load_w(col0):
                        if KP and cnt["w"] >= KP:
                            raise _Stop()
                        s = cnt["w"] % 2
                        cnt["w"] += 1
                        dma1(wps[s], fm(wb_in[l])[:, :, col0:col0 + 512], writes=[("wp", s)])
                        return wps[s], ("wp", s)

                    def psum_next():
                        rr["ps"] = (rr["ps"] + 1) % 4
                        return PA[rr["ps"]], ("pa", rr["ps"])

                    def fm_piece(wp, wkey, dst, func, dil):
                        dd = 1 if is_s else dil
                        upb = bw // dd
                        for cc in range(4):
                            sB = cnt["B"] % 2
                            cnt["B"] += 1
                            stv = stgB[sB][:, 0:Wd].rearrange("p (r u) -> p r u", r=dd)
                            for b in range(nblk):
                                pap, pkey = psum_next()
                                mm(pap[:, 0:bw], [(wp[:, k, cc * 128:(cc + 1) * 128], hT[:, k, b * bw:(b + 1) * bw]) for k in range(8)],
                                   [wkey, "hT"], pkey)
                                src_ = pap[:, 0:bw].rearrange("p (u r) -> p r u", r=dd)
                                dstv = stv[:, :, b * upb:(b + 1) * upb]
                                if func is None:
                                    evac_copy(dstv, src_, [pkey], [("stgB", sB, b)])
                                else:
                                    act(dstv, src_, func, [pkey], [("stgB", sB, b)])
                            dma1(dst[cc * 128:(cc + 1) * 128, c0:c0 + Wd], stgB[sB][:, 0:Wd], reads=[("stgB", sB, b) for b in range(nblk)])

                    def tm_cols(t, dil):
                        if is_s or dil == 1:
                            return slice(t * 128, (t + 1) * 128)
                        per = 16 // dil
                        r, c = t // per, t % per
                        start = r + dil * 128 * c
                        return slice(start, start + dil * 127 + 1, dil)

                    def tm_piece(wp, wkey, dst, dcol0, func, dil, zscale=False, outs=None):
                        for t in range(ntile):
                            pap, pkey = psum_next()
                            cs = tm_cols(t, dil)
                            mm(pap, [(hT[:, k, cs], wp[:, k, :]) for k in range(8)], [wkey, "hT"], pkey)
                            if dst is not None and not (KSKIP & 1 and cnt["w"] == 9):
                                s = cnt["s"] % 3
                                cnt["s"] += 1
                                if zscale:
                                    zo = 4 if is_s else 0
                                    for h in range(4):
                                        ts("dve", stg[s][:, h * 128:(h + 1) * 128], pap[:, h * 128:(h + 1) * 128], ZC[:, zo + h:zo + h + 1], None,
                                           ALU.mult, None, [pkey, "ZC"], [("stg", s)])
                                elif func is None:
                                    evac_copy(stg[s], pap, [pkey], [("stg", s)], force=("dve" if outs is not None else None))
                                else:
                                    act(stg[s], pap, func, [pkey], [("stg", s)])
                                dma1(dst[c0 + t * 128:c0 + (t + 1) * 128, dcol0:dcol0 + 512], stg[s], reads=[("stg", s)])
                            if outs is not None and not (KSKIP & 2 and cnt["w"] == 9):
                                outs(t, pap, pkey)

                    def out_window(g, dst_p, dst_s):
                        L = LG[g]

                        def f(t, pap, pkey):
                            if is_s:
                                s = cnt["f"] % 2
                                cnt["f"] += 1
                                evac_copy(stgf[s], pap, [pkey], [("stgf", s)], force="dve")
                                Ls = GROUPS[g][0]
                                dman([(dst_s[l, q, Ls - ST:Ls, :], stgf[s][q * ST:(q + 1) * ST, :]) for q in range(NS)], reads=[("stgf", s)])
                            elif last_prompt:
                                tok0 = c0 + t * 128
                                if tok0 >= W - L:
                                    s = cnt["f"] % 2
                                    cnt["f"] += 1
                                    evac_copy(stgf[s], pap, [pkey], [("stgf", s)], force="dve")
                                    r0 = tok0 - (W - L)
                                    dma1(dst_p[l, r0:r0 + 128, :], stgf[s], reads=[("stgf", s)])
                        return f

                    need_out = is_s or last_prompt
                    if not pruned:
                        wp, wk = load_w(C_RQ); fm_piece(wp, wk, rqT, None, 1)
                    wp, wk = load_w(C_RK)
                    if not pruned:
                        fm_piece(wp, wk, rkT, None, 1)
                    tm_piece(wp, wk, kz_d, 0, None, 1, zscale=True)
                    for hlf in range(2):
                        wp, wk = load_w(C_RV + hlf * 512); tm_piece(wp, wk, rv_d, hlf * 512, None, 1)
                    for hlf in range(2):
                        if pruned:
                            break
                        wp, wk = load_w(C_RG + hlf * 512); tm_piece(wp, wk, srg_d, hlf * 512, AF.Silu, 1)
                    for g in range(3):
                        if pruned and not need_halo_kv:
                            break
                        dil = GROUPS[g][1]
                        if not pruned:
                            wp, wk = load_w(C_AQ + g * 512); fm_piece(wp, wk, aqT[g], None, dil)
                        wp, wk = load_w(C_AK + g * 512); fm_piece(wp, wk, akT[g], None, dil)
                        if need_out:
                            tm_piece(wp, wk, None, 0, None, 1, outs=out_window(g, pk[g], sk[g]))
                        wp, wk = load_w(C_AV + g * 512)
                        if dil == 1 or is_s:
                            tm_piece(wp, wk, av_d[g], 0, None, 1, outs=out_window(g, pv[g], sv[g]) if need_out else None)
                        else:
                            tm_piece(wp, wk, av_d[g], 0, None, dil)
                            if need_out:
                                tm_piece(wp, wk, None, 0, None, 1, outs=out_window(g, pv[g], sv[g]))
                    for hlf in range(2):
                        if pruned:
                            break
                        wp, wk = load_w(C_GA + hlf * 512); fm_piece(wp, wk, sgaT[hlf * 512:(hlf + 1) * 512, :], AF.Sigmoid, 1)
                    for hlf in range(2):
                        if pruned:
                            break
                        wp, wk = load_w(C_GB + hlf * 512); fm_piece(wp, wk, sgbT[hlf * 512:(hlf + 1) * 512, :], AF.Sigmoid, 1)
                    phase_end()

                    if l == 0 and sb == 0:
                        cx.barrier(sbg=True)
                    phase_begin()
                    NB2 = 3
                    qTc = [carve(512, BF16, (4, 128)) for _ in range(NB2)]
                    kTc = [carve(512, BF16, (4, 128)) for _ in range(NB2)]
                    vc = [carve(1024, BF16) for _ in range(NB2)]
                    kzc = [carve(512, BF16) for _ in range(NB2)]
                    srgc = [carve(1024, BF16) for _ in range(NB2)]
                    Sm = [carve(128, BF16) for _ in range(2)]
                    qx = [carve(128, BF16) for _ in range(2)]
                    if not is_s and not pruned:
                        Sm4 = [carve(512, BF16) for _ in range(2)]
                        qx4 = [carve(512, BF16) for _ in range(2)]
                        onb4 = carve(1024, F32)
                        rtmp = carve(1024, F32, (4, 256))
                        DEC = carve(1024, F32, (4, 256))
                        st4 = [carve(4 * 6, F32, (4, 6)) for _ in range(2)]
                        mv4 = [carve(4 * 2 + 12, F32) for _ in range(2)]
                        for h_ in range(4):
                            memset("pool", DEC[:, h_, :], dec128[h_], ["DEC"])
                    lnst = [carve(16, F32) for _ in range(2)]
                    onb = [carve(256, F32) for _ in range(2)]
                    og = [carve(1024, BF16) for _ in range(2)]
                    oTs = [carve(8 * 128, BF16, (8, 128)) for _ in range(2)]
                    if is_s:
                        Rfs = carve(4 * 256, F32, (4, 256))
                        Rbs = carve(4 * 256, BF16, (4, 256))
                    nch = NS if is_s else ntile
                    cw = ST if is_s else 128
                    if is_s:
                        dma1(fm(oretT)[:, :, c0:c0 + 128], zt16, reads=["zt16"], writes=["oretT_s"])
                        dma1(fm(oattT)[:, :, c0:c0 + 128], zt16[:, 0:4, :], reads=["zt16"], writes=["oattT_s"])
                    it = 0
                    for n in range(nch):
                        s = n % NB2
                        tc0 = c0 + n * cw
                        dma1(vc[s][0:cw, :], rv_d[tc0:tc0 + cw, :], writes=[("vc", s)])
                        dma1(kzc[s][0:cw, :], kz_d[tc0:tc0 + cw, :], writes=[("kzc", s)])
                        if pruned:
                            for h in range(4):
                                i2 = it % 2
                                it += 1
                                mm(PA[4 + i2][:, 0:256], [(kzc[s][0:cw, h * 128:(h + 1) * 128], vc[s][0:cw, h * 256:(h + 1) * 256])],
                                   [("kzc", s), ("vc", s)], ("pa", 4 + i2))
                                stt(Rf[:, h, :], Rf[:, h, :], dec128[h], PA[4 + i2][:, 0:256], ALU.mult, ALU.add, [("pa", 4 + i2), "Rf"], ["Rf"])
                                if n == nch - 1:
                                    act(Rb[:, h, :], Rf[:, h, :], AF.Copy, ["Rf"], ["Rb"])
                            continue
                        dma1(qTc[s][:, :, 0:cw], fm(rqT)[:, :, tc0:tc0 + cw], writes=[("qTc", s)])
                        dma1(kTc[s][:, :, 0:cw], fm(rkT)[:, :, tc0:tc0 + cw], writes=[("kTc", s)])
                        dma1(srgc[s][0:cw, :], srg_d[tc0:tc0 + cw, :], writes=[("srgc", s)])
                        if is_s:
                            dma1(Rfs, st_in[l, n].rearrange("h p v -> p h v"), writes=["Rfs"])
                            copy("pool", Rbs, Rfs, ["Rfs"], ["Rbs"])
                            RF, RB, rfk, rbk, dec = Rfs, Rbs, "Rfs", "Rbs", dec8
                        else:
                            RF, RB, rfk, rbk, dec = Rf, Rb, "Rf", "Rb", dec128
                        if not is_s:
                            i2 = n % 2
                            SPS = PA[i2]
                            for h in range(4):
                                mm(SPS[:, h * 128:(h + 1) * 128], [(kTc[s][:, h, :], qTc[s][:, h, :])], [("kTc", s), ("qTc", s)], ("pa", i2))
                            tt("dve", Sm4[i2], SPS, RM.rearrange("p h i -> p (h i)"), ALU.mult, [("pa", i2), "RM"], [("Sm4", i2)])
                            tt("pool", qx4[i2], qTc[s].rearrange("p h i -> p (h i)"), XI.rearrange("p h i -> p (h i)"), ALU.mult,
                               [("qTc", s), "XI"], [("qx4", i2)])
                            OPS = PAB[1]
                            for h in range(4):
                                mm(OPS[:, h * 256:(h + 1) * 256], [(Sm4[i2][:, h * 128:(h + 1) * 128], vc[s][:, h * 256:(h + 1) * 256]),
                                                                   (qx4[i2][:, h * 128:(h + 1) * 128], Rb[:, h, :])],
                                   [("Sm4", i2), ("vc", s), ("qx4", i2), "Rb"], ("pab", 1))
                            for h in range(4):
                                ub_ = PA[4 + h // 2]
                                mm(ub_[:, (h % 2) * 256:(h % 2 + 1) * 256], [(kzc[s][:, h * 128:(h + 1) * 128], vc[s][:, h * 256:(h + 1) * 256])],
                                   [("kzc", s), ("vc", s)], ("pa", 4 + h // 2))
                            tt("pool", rtmp.rearrange("p h v -> p (h v)"), Rf.rearrange("p h v -> p (h v)"), DEC.rearrange("p h v -> p (h v)"), ALU.mult,
                               ["Rf", "DEC"], ["rtmp"])
                            for hp in range(2):
                                tt("dve", Rf[:, 2 * hp:2 * hp + 2, :].rearrange("p h v -> p (h v)"), PA[4 + hp],
                                   rtmp[:, 2 * hp:2 * hp + 2, :].rearrange("p h v -> p (h v)"), ALU.add, [("pa", 4 + hp), "rtmp"], ["Rf"])
                            act(Rb.rearrange("p h v -> p (h v)"), Rf.rearrange("p h v -> p (h v)"), AF.Copy, ["Rf"], ["Rb"])
                            stv, mvv = st4[i2], mv4[i2]
                            for h in range(4):
                                bnstats(stv[:, h, :], OPS[:, h * 256:(h + 1) * 256], [("pab", 1)], [("ln4", i2)])
                            for h in range(4):
                                bnaggr(mvv[:, 2 * h:2 * h + 2], stv[:, h, :], [("ln4", i2)], [("ln4", i2)])
                            mvh = mvv[:, 0:8].rearrange("p (h t) -> p h t", t=2)
                            act(mvv[:, 8:12], mvh[:, :, 1], AF.Sqrt, [("ln4", i2), "epsb"], [("ln4", i2)], bias=epsb[:, 0:1], scale=1.0)
                            recip(mvv[:, 12:16], mvv[:, 8:12], [("ln4", i2)], [("ln4", i2)])
                            tt("dve", mvv[:, 16:20], mvh[:, :, 0], mvv[:, 12:16], ALU.mult, [("ln4", i2)], [("ln4", i2)])
                            ts("dve", mvv[:, 16:20], mvv[:, 16:20], -1.0, None, ALU.mult, None, [("ln4", i2)], [("ln4", i2)])
                            for h in range(4):
                                act(onb4[:, h * 256:(h + 1) * 256], OPS[:, h * 256:(h + 1) * 256], AF.Identity, [("pab", 1), ("ln4", i2)], ["onb4"],
                                    bias=mvv[:, 16 + h:17 + h], scale=mvv[:, 12 + h:13 + h])
                            os_ = n % 2
                            tt("pool", og[os_], onb4, srgc[s], ALU.mult, ["onb4", ("srgc", s)], [("og", os_)])
                        for h in (range(4) if is_s else []):
                            i2 = it % 2
                            it += 1
                            mm(PA[i2][0:cw, 0:cw], [(kTc[s][:, h, 0:cw], qTc[s][:, h, 0:cw])], [("kTc", s), ("qTc", s)], ("pa", i2))
                            tt("dve", Sm[i2][0:cw, 0:cw], PA[i2][0:cw, 0:cw], RM[0:cw, h, 0:cw], ALU.mult, [("pa", i2), "RM"], [("Sm", i2)])
                            tt("pool", qx[i2][:, 0:cw], qTc[s][:, h, 0:cw], XI[:, h, 0:cw], ALU.mult, [("qTc", s), "XI"], [("qx", i2)])
                            mm(PA[2 + i2][0:cw, 0:256], [(Sm[i2][0:cw, 0:cw], vc[s][0:cw, h * 256:(h + 1) * 256]),
                                                     (qx[i2][:, 0:cw], RB[:, h, :])],
                               [("Sm", i2), ("vc", s), ("qx", i2), rbk], ("pa", 2 + i2))
                            mm(PA[4 + i2][:, 0:256], [(kzc[s][0:cw, h * 128:(h + 1) * 128], vc[s][0:cw, h * 256:(h + 1) * 256])],
                               [("kzc", s), ("vc", s)], ("pa", 4 + i2))
                            dh = dec[h]
                            stt(RF[:, h, :], RF[:, h, :], dh, PA[4 + i2][:, 0:256], ALU.mult, ALU.add, [("pa", 4 + i2), rfk], [rfk])
                            act(RB[:, h, :], RF[:, h, :], AF.Copy, [rfk], [rbk])
                            ls = lnst[i2]
                            bnstats(ls[0:cw, 0:6], PA[2 + i2][0:cw, 0:256], [("pa", 2 + i2)], [("ln", i2)])
                            bnaggr(ls[0:cw, 6:8], ls[0:cw, 0:6], [("ln", i2)], [("ln", i2)])
                            act(ls[0:cw, 8:9], ls[0:cw, 7:8], AF.Sqrt, [("ln", i2), "epsb"], [("ln", i2)], bias=epsb[0:cw, 0:1], scale=1.0)
                            recip(ls[0:cw, 9:10], ls[0:cw, 8:9], [("ln", i2)], [("ln", i2)])
                            ts("dve", ls[0:cw, 10:11], ls[0:cw, 6:7], ls[0:cw, 9:10], -1.0, ALU.mult, ALU.mult, [("ln", i2)], [("ln", i2)])
                            act(onb[i2][0:cw, :], PA[2 + i2][0:cw, 0:256], AF.Identity, [("pa", 2 + i2), ("ln", i2)], [("onb", i2)],
                                bias=ls[0:cw, 10:11], scale=ls[0:cw, 9:10])
                            os_ = n % 2
                            tt("pool", og[os_][0:cw, h * 256:(h + 1) * 256], onb[i2][0:cw, :], srgc[s][0:cw, h * 256:(h + 1) * 256], ALU.mult,
                               [("onb", i2), ("srgc", s)], [("og", os_)])
                        os_ = n % 2
                        ptv = PT[os_].rearrange("p (k n) -> p k n", k=8)
                        transposes([ptv[:, k, 0:cw] for k in range(8)], [og[os_][0:cw, k * 128:(k + 1) * 128] for k in range(8)],
                                   [("og", os_)], ("pt", os_))
                        act(oTs[os_][:, :, 0:cw], ptv[:, :, 0:cw], AF.Copy, [("pt", os_)], [("oTs", os_)])
                        dma1(fm(oretT)[:, :, tc0:tc0 + cw], oTs[os_][:, :, 0:cw], reads=[("oTs", os_)], writes=(["oretT_s"] if is_s else []))
                        if is_s:
                            dma1(sret[l, n].rearrange("h p v -> p h v"), Rfs, reads=["Rfs"])
                    if last_prompt:
                        dma1(pret[l].rearrange("h p v -> p h v"), Rf, reads=["Rf"])
                    phase_end()

                    phase_begin()
                    sc = 128 ** -0.5
                    if pruned:
                        pass
                    elif not is_s:
                        accU = carve(4 * SB_W, F32, (4, SB_W))
                        accD = carve(4 * SB_W, F32, (4, SB_W))
                        NB3 = 3
                        UB = [PA[4], PT[0].bitcast(F32)]
                        DB = [PA[5], PT[1].bitcast(F32)]
                        KTs = [carve(1024, BF16) for _ in range(NB3)]
                        QTs = [carve(512, BF16) for _ in range(NB3)]
                        Vs = [carve(8 * 128, BF16, (8, 128)) for _ in range(NB3)]
                        Eb = [carve(1024, F32) for _ in range(2)]
                        Pb = [carve(1024, BF16) for _ in range(2)]
                        it = 0
                        ld = 0
                        pend = []

                        def at_stage2(B_):
                            (g, j, gj, s, i2, kind, r, dil, halo_k, prevV, diagV, pv_sb, accsl) = B_
                            for k in range(4):
                                pairs_u = []
                                pairs_d = []
                                if halo_k[k]:
                                    pairs_u.append((prevV(k), Pb[i2][:, k * 256:k * 256 + 128]))
                                    pairs_d.append((vones[:, pv_sb[k], :], Pb[i2][:, k * 256:k * 256 + 128]))
                                pairs_u.append((diagV(k), Pb[i2][:, k * 256 + 128:(k + 1) * 256]))
                                pairs_d.append((vones[:, sb, :], Pb[i2][:, k * 256 + 128:(k + 1) * 256]))
                                mm(UB[i2][:, k * 128:(k + 1) * 128], pairs_u, [("Vs", s), ("Pb", i2)], ("ub", i2))
                                mm(DB[i2][:, k * 128:(k + 1) * 128], pairs_d, ["vones", ("Pb", i2)], ("db", i2))
                            if kind == "c":
                                pu, pd = UB[i2], DB[i2]
                            else:
                                pu = UB[i2].rearrange("p (k i) -> p k i", k=4)
                                pd = DB[i2].rearrange("p (k i) -> p k i", k=4)
                            if g == 0:
                                copy("dve", accsl(accU), pu, [("ub", i2)], [("accU", j)])
                                act(accsl(accD), pd, AF.Copy, [("db", i2)], [("accD", j)])
                            else:
                                tt("dve", accsl(accU), pu, accsl(accU), ALU.add, [("ub", i2), ("accU", j)], [("accU", j)])
                                tt("dve", accsl(accD), pd, accsl(accD), ALU.add, [("db", i2), ("accD", j)], [("accD", j)])

                        for g in range(3):
                            dil = GROUPS[g][1]
                            U = SB_W // dil
                            for j in range(4):
                                rows = slice(j * 128, (j + 1) * 128)
                                gj = g * 4 + j
                                if U >= 512:
                                    batches = [("c", r, ub) for r in range(dil) for ub in range(0, U, 512)]
                                else:
                                    batches = [("r", r0_, 0) for r0_ in range(0, dil, 4)]
                                for (kind, r, ub) in batches:
                                    s = ld % NB3
                                    ld += 1
                                    i2 = it % 2
                                    it += 1
                                    if kind == "c":
                                        colb = c0 + r * U + ub
                                        hcol = (colb - 128) if ub > 0 else ((c0 - SB_W) + r * U + (U - 128))
                                        halo_k = [not (sb == 0 and ub == 0 and k == 0) for k in range(4)]
                                        if halo_k[0]:
                                            dma1(KTs[s][:, 0:128], akT[g][rows, hcol:hcol + 128], writes=[("KTs", s)])
                                            dma1(Vs[s][:, 0, :], av_d[g][hcol:hcol + 128, rows], writes=[("Vs", s)])
                                        dma1(KTs[s][:, 128:640], akT[g][rows, colb:colb + 512], writes=[("KTs", s)])
                                        dma1(Vs[s][:, 1:5, :], av_d[g][colb:colb + 512, rows].rearrange("(c p) d -> p c d", p=128), writes=[("Vs", s)])
                                        prevK = lambda k: KTs[s][:, k * 128:(k + 1) * 128]
                                        diagK = lambda k: KTs[s][:, (k + 1) * 128:(k + 2) * 128]
                                        prevV = lambda k, s=s: Vs[s][:, k, :]
                                        diagV = lambda k, s=s: Vs[s][:, k + 1, :]
                                        pv_sb = [(sb - 1 if (k == 0 and ub == 0) else sb) for k in range(4)]
                                        start = r + dil * ub
                                        accsl = lambda a, j=j, start=start, dil=dil: a[:, j, start:start + dil * 511 + 1:dil]
                                    else:
                                        colb = c0 + r * U
                                        hcol = (c0 - SB_W) + r * U
                                        halo_k = [sb > 0] * 4
                                        if sb > 0:
                                            dma1(KTs[s][:, 0:512], akT[g][rows, hcol:hcol + 512], writes=[("KTs", s)])
                                            dma1(Vs[s][:, 0:4, :], av_d[g][hcol:hcol + 512, rows].rearrange("(c p) d -> p c d", p=128), writes=[("Vs", s)])
                                        dma1(KTs[s][:, 512:1024], akT[g][rows, colb:colb + 512], writes=[("KTs", s)])
                                        dma1(Vs[s][:, 4:8, :], av_d[g][colb:colb + 512, rows].rearrange("(c p) d -> p c d", p=128), writes=[("Vs", s)])
                                        prevK = lambda k: KTs[s][:, k * 128:(k + 1) * 128]
                                        diagK = lambda k: KTs[s][:, 512 + k * 128:512 + (k + 1) * 128]
                                        prevV = lambda k, s=s: Vs[s][:, k, :]
                                        diagV = lambda k, s=s: Vs[s][:, 4 + k, :]
                                        pv_sb = [sb - 1] * 4
                                        accsl = lambda a, j=j, r=r, dil=dil: a[:, j, :].rearrange("p (u d) -> p d u", d=dil)[:, r:r + 4, :]
                                    dma1(QTs[s], aqT[g][rows, colb:colb + 512], writes=[("QTs", s)])
                                    SP = PAB[i2]
                                    skey = ("pab", i2)
                                    for k in range(4):
                                        qap = QTs[s][:, k * 128:(k + 1) * 128]
                                        if halo_k[k]:
                                            mm(SP[:, k * 256:k * 256 + 128], [(prevK(k), qap)], [("KTs", s), ("QTs", s)], skey)
                                        mm(SP[:, k * 256 + 128:(k + 1) * 256], [(diagK(k), qap)], [("KTs", s), ("QTs", s)], skey)
                                    if all(halo_k):
                                        act(Eb[i2], SP, AF.Exp, [skey], [("Eb", i2)], scale=sc)
                                    else:
                                        for k in range(4):
                                            lo = k * 256 + (0 if halo_k[k] else 128)
                                            act(Eb[i2][:, lo:(k + 1) * 256], SP[:, lo:(k + 1) * 256], AF.Exp, [skey], [("Eb", i2)], scale=sc)
                                    for k in range(4):
                                        lo = k * 256 + (0 if halo_k[k] else 128)
                                        tt("pool", Pb[i2][:, lo:(k + 1) * 256], Eb[i2][:, lo:(k + 1) * 256], AM[:, gj, lo - k * 256:256], ALU.mult,
                                           [("Eb", i2), "AM"], [("Pb", i2)])
                                    pend.append((g, j, gj, s, i2, kind, r, dil, halo_k, prevV, diagV, pv_sb, accsl))
                                    if len(pend) > 1:
                                        at_stage2(pend.pop(0))
                        while pend:
                            at_stage2(pend.pop(0))
                        ofin = [carve(512, BF16) for _ in range(2)]
                        k2 = 0
                        for j in range(4):
                            for b in range(SB_W // 512):
                                bs = slice(b * 512, (b + 1) * 512)
                                ts("dve", accD[:, j, bs], accD[:, j, bs], 1e-30, None, ALU.add, None, [("accD", j)], [("accD", j)])
                                recip(accD[:, j, bs], accD[:, j, bs], [("accD", j)], [("accD", j)])
                                s2 = k2 % 2
                                k2 += 1
                                tt("pool", ofin[s2], accU[:, j, bs], accD[:, j, bs], ALU.mult, [("accU", j), ("accD", j)], [("ofin", s2)])
                                dma1(oattT[j * 128:(j + 1) * 128, c0 + b * 512:c0 + (b + 1) * 512], ofin[s2], reads=[("ofin", s2)])
                    else:
                        Kc = [carve(512, BF16) for _ in range(2)]
                        KT = [carve(512, BF16, (4, 128)) for _ in range(2)]
                        Vc = carve(21 * 512, BF16, (21, 512))
                        Knew = carve(3 * 4 * ST, BF16, (3, 4, ST))
                        Vnew = carve(3 * 512, BF16, (3, 512))
                        Qn = carve(3 * 4 * ST, BF16, (3, 4, ST))
                        Pall = carve(24 * 32, BF16, (24, 32))
                        Es = [carve(32, F32) for _ in range(2)]
                        osb = carve(64, F32)
                        ofs = carve(32, BF16)
                        for q in range(NS):
                            tcol = c0 + q * ST
                            for g in range(3):
                                dma1(Knew[:, g, :, :], fm(akT[g])[:, :, tcol:tcol + ST], writes=["Knew"])
                                dma1(Qn[:, g, :, :], fm(aqT[g])[:, :, tcol:tcol + ST], writes=["Qn"])
                                dma1(Vnew[0:ST, g, :], av_d[g][tcol:tcol + ST, :], writes=["Vnew"])
                            vt = 0
                            for g in range(3):
                                L = GROUPS[g][0]
                                dma1(Vc[:, vt:vt + L // 128, :], cv[g][l, q].rearrange("(t p) d -> p t d", p=128), writes=["Vc"], eng="pool")
                                vt += L // 128
                            ti = 0
                            kt_i = 0
                            for g in range(3):
                                L = GROUPS[g][0]
                                for t in range(L // 128 + 1):
                                    e2 = ti % 2
                                    if t < L // 128:
                                        s = kt_i % 2
                                        kt_i += 1
                                        dma1(Kc[s], ck[g][l, q, t * 128:(t + 1) * 128, :], writes=[("Kc", s)], eng="pool")
                                        ptv = PT[s].rearrange("p (k n) -> p k n", k=8)
                                        transposes([ptv[:, jj, :] for jj in range(4)], [Kc[s][:, jj * 128:(jj + 1) * 128] for jj in range(4)],
                                                   [("Kc", s)], ("pt", s))
                                        act(KT[s], ptv[:, 0:4, :], AF.Copy, [("pt", s)], [("KT", s)])
                                        kp = 128
                                        for jj in range(4):
                                            mm(PA[2][:, jj * ST:(jj + 1) * ST], [(KT[s][:, jj, :], Qn[:, g, jj, :])], [("KT", s), "Qn"], ("pa", 2))
                                    else:
                                        kp = ST
                                        for jj in range(4):
                                            mm(PA[2][0:ST, jj * ST:(jj + 1) * ST], [(Knew[:, g, jj, :], Qn[:, g, jj, :])], ["Knew", "Qn"], ("pa", 2))
                                    act(Es[e2][0:kp, :], PA[2][0:kp, 0:32], AF.Exp, [("pa", 2)], [("Es", e2)], scale=sc)
                                    tt("dve", Pall[0:kp, ti, :], Es[e2][0:kp, :], SM[0:kp, ti, :], ALU.mult, [("Es", e2), "SM"], ["Pall"])
                                    ti += 1
                            tiles = []
                            ti = 0
                            vt = 0
                            for g in range(3):
                                L = GROUPS[g][0]
                                for t in range(L // 128):
                                    tiles.append((ti, 128, ("c", vt)))
                                    ti += 1
                                    vt += 1
                                tiles.append((ti, ST, ("n", g)))
                                ti += 1
                            for jj in range(4):
                                pairs = []
                                for (ti_, kp, (kind, idx)) in tiles:
                                    if kind == "c":
                                        lhs = Vc[:, idx, jj * 128:(jj + 1) * 128]
                                    else:
                                        lhs = Vnew[0:ST, idx, jj * 128:(jj + 1) * 128]
                                    pairs.append((lhs, Pall[0:kp, ti_, jj * ST:(jj + 1) * ST]))
                                mm(PA[3][:, jj * ST:(jj + 1) * ST], pairs, ["Vc", "Vnew", "Pall"], ("pa", 3))
                            pairs = [(ones[0:kp, :], Pall[0:kp, ti_, :]) for (ti_, kp, _) in tiles]
                            mm(PA[4][:, 0:32], pairs, ["ones", "Pall"], ("pa", 4))
                            recip(osb[:, 0:32], PA[4][:, 0:32], [("pa", 4)], ["osb"])
                            tt("dve", ofs, PA[3][:, 0:32], osb[:, 0:32], ALU.mult, [("pa", 3), "osb"], ["ofs"])
                            dma1(fm(oattT)[:, :, tcol:tcol + ST], ofs.rearrange("p (j i) -> p j i", j=4), reads=["ofs"])
                    phase_end()
                    if pruned:
                        continue

                    phase_begin()
                    gain = carve(D, F32)
                    dma1(gain, nffn[l, :, :], writes=["gain"])
                    wr = carve(8 * 1024, BF16, (8, 1024))
                    wa = carve(4 * 1024, BF16, (4, 1024))
                    wo = carve(8 * 1024, BF16, (8, 1024))
                    dma1(wr, fm(wb_rbr[l]), writes=["wr"])
                    dma1(wa, fm(wb_abr[l]), writes=["wa"])
                    dma1(wo, fm(wb_out[l]), writes=["wo"])
                    NB4 = 1
                    orT = [carve(8 * 512, BF16, (8, 512)) for _ in range(NB4)]
                    oaT = [carve(4 * 512, BF16, (4, 512)) for _ in range(NB4)]
                    gaT = [carve(8 * 512, BF16, (8, 512)) for _ in range(NB4)]
                    gbT = [carve(8 * 512, BF16, (8, 512)) for _ in range(NB4)]
                    mT = [carve(8 * 512, BF16, (8, 512)) for _ in range(2)]
                    t1 = [carve(512, F32) for _ in range(2)]
                    t2 = [carve(512, F32) for _ in range(2)]
                    xts = [carve(D, F32) for _ in range(2)]
                    nb = norm_bufs()
                    ci = 0
                    ti_g = 0
                    for b in range(nblk):
                        s = b % NB4
                        cb = c0 + b * bw
                        dma1(orT[s][:, :, 0:bw], fm(oretT)[:, :, cb:cb + bw], writes=[("orT", s)])
                        dma1(oaT[s][:, :, 0:bw], fm(oattT)[:, :, cb:cb + bw], writes=[("oaT", s)])
                        dma1(gaT[s][:, :, 0:bw], fm(sgaT)[:, :, cb:cb + bw], writes=[("gaT", s)])
                        dma1(gbT[s][:, :, 0:bw], fm(sgbT)[:, :, cb:cb + bw], writes=[("gbT", s)])
                        ms = b % 2
                        for cc in range(8):
                            c2 = ci % 2
                            ci += 1
                            mm(PA[0][:, 0:bw], [(wr[:, k, cc * 128:(cc + 1) * 128], orT[s][:, k, 0:bw]) for k in range(8)], ["wr", ("orT", s)], ("pa", 0))
                            mm(PA[1][:, 0:bw], [(wa[:, k, cc * 128:(cc + 1) * 128], oaT[s][:, k, 0:bw]) for k in range(4)], ["wa", ("oaT", s)], ("pa", 1))
                            tt("dve", t1[c2][:, 0:bw], PA[0][:, 0:bw], gaT[s][:, cc, 0:bw], ALU.mult, [("pa", 0), ("gaT", s)], [("t1", c2)])
                            tt("dve", t2[c2][:, 0:bw], PA[1][:, 0:bw], gbT[s][:, cc, 0:bw], ALU.mult, [("pa", 1), ("gbT", s)], [("t2", c2)])
                            tt("pool", mT[ms][:, cc, 0:bw], t1[c2][:, 0:bw], t2[c2][:, 0:bw], ALU.add, [("t1", c2), ("t2", c2)], [("mT", ms)])
                        for tl in range(bw // 128):
                            t = b * (bw // 128) + tl
                            xs_ = ti_g % 2
                            ti_g += 1
                            r0 = c0 + t * 128
                            dma1(xts[xs_], xres[r0:r0 + 128, :], writes=[("xt", xs_)])
                            for hf in range(2):
                                pb = 2 + hf
                                mm(PA[pb], [(mT[ms][:, k, tl * 128:(tl + 1) * 128], wo[:, k, hf * 512:(hf + 1) * 512]) for k in range(8)],
                                   [("mT", ms), "wo"], ("pa", pb))
                                tt("dve", xts[xs_][:, hf * 512:(hf + 1) * 512], PA[pb], xts[xs_][:, hf * 512:(hf + 1) * 512], ALU.add,
                                   [("pa", pb), ("xt", xs_)], [("xt", xs_)])
                            dma1(xres[r0:r0 + 128, :], xts[xs_], reads=[("xt", xs_)])
                            norm_tile(xts[xs_], ("xt", xs_), gain, t * 128, nb, ti_g)
                    phase_end()

                    phase_begin()
                    wps = [carve(8 * 512, BF16, (8, 512)) for _ in range(2)]
                    rl = [carve(512, F32) for _ in range(2)]
                    us = [carve(512, BF16) for _ in range(2)]
                    k3 = 0
                    for pc in range(DFF // 512):
                        s = pc % 2
                        dma1(wps[s], fm(wb_up[l])[:, :, pc * 512:(pc + 1) * 512], writes=[("wp", s)])
                        for cc in range(4):
                            for b in range(nblk):
                                rr["ps"] = (rr["ps"] + 1) % 4
                                pb = rr["ps"]
                                mm(PA[pb][:, 0:bw], [(wps[s][:, k, cc * 128:(cc + 1) * 128], hT[:, k, b * bw:(b + 1) * bw]) for k in range(8)],
                                   [("wp", s), "hT"], ("pa", pb))
                                s3 = k3 % 2
                                k3 += 1
                                act(rl[s3][:, 0:bw], PA[pb][:, 0:bw], AF.Relu, [("pa", pb)], [("rl", s3)])
                                tt("pool", us[s3][:, 0:bw], rl[s3][:, 0:bw], rl[s3][:, 0:bw], ALU.mult, [("rl", s3)], [("us", s3)])
                                fr = pc * 512 + cc * 128
                                dma1(uT[fr:fr + 128, c0 + b * bw:c0 + (b + 1) * bw], us[s3][:, 0:bw], reads=[("us", s3)])
                    phase_end()

                    phase_begin()
                    wd = carve(32 * 1024, BF16, (32, 1024))
                    for kq in range(4):
                        dma1(wd[:, kq * 8:(kq + 1) * 8, :], fm(wb_dn[l])[:, kq * 8:(kq + 1) * 8, :], writes=["wd"])
                    wg = carve(8 * 1024, BF16, (8, 1024))
                    dma1(wg, fm(wb_pg[l]), writes=["wg"])
                    wpl = carve(2 * 1024, BF16, (2, 1024))
                    dma1(wpl, fm(wb_ple[l]), writes=["wpl"])
                    final = (l == DEPTH - 1)
                    if final:
                        gain = carve(D, F32)
                        dma1(gain, nfin[:, :], writes=["gain"])
                    hT_flat = hT.rearrange("p k n -> p (k n)")
                    uTs = [hT_flat[:, i_ * 8192:(i_ + 1) * 8192].rearrange("p (k n) -> p k n", k=32) for i_ in range(2)]
                    xts = [carve(D, F32) for _ in range(2)]
                    pts = [carve(DPLE, F32) for _ in range(2)]
                    xb = [carve(D, BF16) for _ in range(1)]
                    pb16 = [carve(DPLE, BF16) for _ in range(2)]
                    xT = [carve(8 * 128, BF16, (8, 128)) for _ in range(1)]
                    pT = [carve(2 * 128, BF16, (2, 128)) for _ in range(2)]
                    sg = [carve(512, F32) for _ in range(1)]
                    pp = [carve(512, F32) for _ in range(1)]
                    yt = [carve(D, F32) for _ in range(1)]
                    ssf = [carve(4, F32) for _ in range(2)]
                    def fd_a(t):
                        s = t % 2
                        r0 = c0 + t * 128
                        us_ = (t // 2) % 2
                        uo = (t % 2) * 128
                        if t % 2 == 0:
                            tw = min(256, Wd - t * 128)
                            dma1(uTs[us_][:, :, 0:tw], fm(uT)[:, :, r0:r0 + tw], writes=[("uTs", us_)])
                        dma1(xts[s], xres[r0:r0 + 128, :], writes=[("xt", s)])
                        if is_s:
                            memset("pool", pts[s], 0.0, [("pts", s)])
                            dma1(pts[s][0:NS * ST, :], ps_in[l, :, :], writes=[("pts", s)])
                        else:
                            dma1(pts[s], pw[l, r0:r0 + 128, :], writes=[("pts", s)])
                        for hf in range(2):
                            pbk = hf
                            mm(PA[pbk], [(uTs[us_][:, k, uo:uo + 128], wd[:, k, hf * 512:(hf + 1) * 512]) for k in range(32)], [("uTs", us_), "wd"], ("pa", pbk))
                            tt("dve", xts[s][:, hf * 512:(hf + 1) * 512], PA[pbk], xts[s][:, hf * 512:(hf + 1) * 512], ALU.add,
                               [("pa", pbk), ("xt", s)], [("xt", s)])
                    def fd_b(t):
                        s = t % 2
                        r0 = c0 + t * 128
                        copy("pool", xb[0], xts[s], [("xt", s)], [("xb", 0)])
                        copy("pool", pb16[s], pts[s], [("pts", s)], [("pb16", s)])
                        ptv = PT[0].rearrange("p (k n) -> p k n", k=8)
                        transposes([ptv[:, k, :] for k in range(8)], [xb[0][:, k * 128:(k + 1) * 128] for k in range(8)], [("xb", 0)], ("pt", 0))
                        act(xT[0], ptv, AF.Copy, [("pt", 0)], [("xT", 0)])
                        ptv1 = PT[1].rearrange("p (k n) -> p k n", k=8)
                        transposes([ptv1[:, k, :] for k in range(2)], [pb16[s][:, k * 128:(k + 1) * 128] for k in range(2)], [("pb16", s)], ("pt", 1))
                        copy("dve", pT[s], ptv1[:, 0:2, :], [("pt", 1)], [("pT", s)])
                        for hf in range(2):
                            s4 = 0
                            mm(PA[2], [(xT[0][:, k, :], wg[:, k, hf * 512:(hf + 1) * 512]) for k in range(8)], [("xT", 0), "wg"], ("pa", 2))
                            mm(PA[3], [(pT[s][:, k, :], wpl[:, k, hf * 512:(hf + 1) * 512]) for k in range(2)], [("pT", s), "wpl"], ("pa", 3))
                            act(sg[s4], PA[2], AF.Sigmoid, [("pa", 2)], [("sg", s4)])
                            tt("dve", pp[s4], PA[3], sg[s4], ALU.mult, [("pa", 3), ("sg", s4)], [("pp", s4)])
                            tt("pool", xts[s][:, hf * 512:(hf + 1) * 512], xts[s][:, hf * 512:(hf + 1) * 512], pp[s4], ALU.add,
                               [("xt", s), ("pp", s4)], [("xt", s)])
                        if not final:
                            dma1(xres[r0:r0 + 128, :], xts[s], reads=[("xt", s)])
                        else:
                            dve_ttr(yt[0], xts[s], xts[s], ssf[s][:, 0:1], [("xt", s)], [("yt", 0), ("ssf", s)])
                            act(ssf[s][:, 1:2], ssf[s][:, 0:1], AF.Sqrt, [("ssf", s), "epsb"], [("ssf", s)], bias=epsb[:, 0:1], scale=1.0 / D)
                            recip(ssf[s][:, 2:3], ssf[s][:, 1:2], [("ssf", s)], [("ssf", s)])
                            stt(yt[0], xts[s], ssf[s][:, 2:3], gain, ALU.mult, ALU.mult, [("xt", s), ("ssf", s), "gain"], [("yt", 0)])
                            if is_s:
                                dma1(ys[:, :], yt[0][0:NS * ST, :], reads=[("yt", 0)])
                            elif sb == NSB - 1:
                                dma1(y[r0 - (W - SB_W):r0 - (W - SB_W) + 128, :], yt[0], reads=[("yt", 0)])

                    fd_a(0)
                    for t in range(ntile):
                        if t + 1 < ntile:
                            fd_a(t + 1)
                        fd_b(t)
                    phase_end()
        except _Stop:
            pass

        cx.barrier(final=True)

        semnames = {}
        import contextlib
        with contextlib.ExitStack() as es:
            sems = {}
            for i, sk_ in enumerate(cx.semkeys):
                sems[sk_] = es.enter_context(nc.semaphore("s%d" % i))
            with nc.Block() as block:
                @block.tensor
                def _(e):
                    cx.replay("pe", e, sems)

                @block.scalar
                def _(e):
                    cx.replay("act", e, sems)

                @block.vector
                def _(e):
                    cx.replay("dve", e, sems)

                @block.gpsimd
                def _(e):
                    cx.replay("pool", e, sems)

                @block.sync
                def _(e):
                    cx.replay("sp", e, sems)
    return nc


_CACHE = {}


def make_in_maps(W, inputs, n_cores=8):
    consts = make_consts()
    f = lambda a: np.ascontiguousarray(np.asarray(a, dtype=np.float32))
    B = inputs["x_prompt"].shape[0]
    cores_per_b = n_cores // B
    shared = {k: f(inputs[k]) for k in ("w_in", "w_ret_br", "w_att_br", "w_out", "w_up", "w_down", "w_ple", "w_ple_gate")}
    shared["nmix"] = f(np.broadcast_to(np.asarray(inputs["norm_mix"])[:, None, :], (DEPTH, 128, D)))
    shared["nffn"] = f(np.broadcast_to(np.asarray(inputs["norm_ffn"])[:, None, :], (DEPTH, 128, D)))
    shared["nfin"] = f(np.broadcast_to(np.asarray(inputs["norm_final"])[None, :], (128, D)))
    shared.update(consts)
    maps = []
    NSB = W // SB_W
    for c in range(n_cores):
        b = c // cores_per_b
        seg = ((c % cores_per_b) * NSB) // cores_per_b
        npad = (NSB - 1 - seg) * SB_W
        m = dict(shared)
        xwin = np.zeros((W, D), np.float32)
        xwin[npad:] = np.asarray(inputs["x_prompt"])[b, :W - npad]
        pwin = np.zeros((DEPTH, W, DPLE), np.float32)
        pwin[:, npad:] = np.asarray(inputs["p_prompt"])[:, b, :W - npad]
        m["xw"] = xwin
        m["pw"] = pwin
        vo = np.zeros((128, NSB, 128), np.float32)
        vo[:, NSB - 1 - seg:, :] = 1.0
        m["vones"] = vo
        sl = slice(c * NS, (c + 1) * NS)
        m["xs"] = f(np.asarray(inputs["x_sample"])[sl].reshape(NS * ST, D))
        m["ps"] = f(np.asarray(inputs["p_sample"])[:, sl].reshape(DEPTH, NS * ST, DPLE))
        caches_k = (inputs["cache_win_k0"], inputs["cache_win_k1"], inputs["cache_win_k2"])
        caches_v = (inputs["cache_win_v0"], inputs["cache_win_v1"], inputs["cache_win_v2"])
        for g in range(3):
            L = GROUPS[g][0]
            m["ck%d" % g] = f(np.asarray(caches_k[g])[:, sl].reshape(DEPTH, NS, L, 512))
            m["cv%d" % g] = f(np.asarray(caches_v[g])[:, sl].reshape(DEPTH, NS, L, 512))
        m["st"] = f(np.asarray(inputs["state_ret"])[:, sl])
        maps.append(m)
    return maps


def assemble(W, res, B, n_cores=8):
    cores_per_b = n_cores // B
    R = res.results
    LG = [min(g[0], W) for g in GROUPS]
    NSB = W // SB_W
    y_prompt = np.zeros((B, W, D), np.float32)
    for c in range(n_cores):
        b = c // cores_per_b
        seg = ((c % cores_per_b) * NSB) // cores_per_b
        y_prompt[b, seg * SB_W:(seg + 1) * SB_W] = R[c]["y"]
    y_sample = np.concatenate([R[c]["ys"].reshape(NS, ST, D) for c in range(n_cores)]).astype(np.float32)
    outs = [y_prompt, y_sample]
    for g in range(3):
        for nm in ("pk", "pv"):
            a = np.stack([R[b * cores_per_b + cores_per_b - 1][nm + str(g)] for b in range(B)], axis=1)
            outs.append(a.reshape(DEPTH, B, LG[g], 4, 128).astype(np.float32))
    outs.append(np.stack([R[b * cores_per_b + cores_per_b - 1]["pret"] for b in range(B)], axis=1).astype(np.float32))
    for g in range(3):
        L = GROUPS[g][0]
        for nm in ("sk", "sv"):
            a = np.concatenate([R[c][nm + str(g)] for c in range(n_cores)], axis=1)
            outs.append(a.reshape(DEPTH, n_cores * NS, L, 4, 128).astype(np.float32))
    outs.append(np.concatenate([R[c]["sret"] for c in range(n_cores)], axis=1).astype(np.float32))
    return tuple(outs)


def kernel(**inputs):
    W = int(np.asarray(inputs["x_prompt"]).shape[1])
    B = int(np.asarray(inputs["x_prompt"]).shape[0])
    if W not in _CACHE:
        _CACHE[W] = build(W)
    nc = _CACHE[W]
    in_maps = make_in_maps(W, inputs)
    res = run_bass_kernel_spmd(nc, in_maps, core_ids=list(range(8)))
    return assemble(W, res, B)
```
